# Optimizing a Trainium2 kernel written in Bass

```python
import math
import jax, jax.numpy as jnp
from jax import lax
import numpy as np

D_MODEL = 2048
BATCH = 4
SEQ = 2048
DEPTH = 2
DEC_BATCH = 128
DEC_SEQ = 8
PAST_LEN = 16384
PAGE_SIZE = 128

MIX_WIDTH = D_MODEL
M_WIDTH = MIX_WIDTH // 2
M_HEADS = 4
M_DK = M_WIDTH // M_HEADS
M_DV = M_WIDTH // M_HEADS
S_WIDTH = MIX_WIDTH - M_WIDTH
S_HEADDIM = 64
S_HEADS = S_WIDTH // S_HEADDIM
S_GROUPS = 2
S_STATE = 128
S_CONV = 4
S_CONV_DIM = S_WIDTH + 2 * S_GROUPS * S_STATE
D_FF = ((8 * D_MODEL) // 3 + 127) // 128 * 128
FFN_CONV = 3
CHUNK = 64
ALPHA = (2 * DEPTH) ** 0.25
BETA = (8 * DEPTH) ** -0.25
LN_EPS = 1e-5
GN_EPS = 1e-6
SPLIT_SIZES = (M_WIDTH, M_WIDTH, M_WIDTH, M_WIDTH, M_HEADS, M_HEADS, S_WIDTH, S_CONV_DIM, S_HEADS)
IN_DIM = M_WIDTH * 4 + M_HEADS * 2 + S_WIDTH + S_CONV_DIM + S_HEADS

kernel_name = 'hymba_mlstm_ssd_convffn_deepnorm_step'


def split_columns(u, sizes):
    idx = []
    acc = 0
    for s in sizes[:-1]:
        acc += s
        idx.append(acc)
    return jnp.split(u, idx, axis=-1)


def layer_norm(x, g, b):
    xf = x.astype(jnp.float32)
    mu = jnp.mean(xf, axis=-1, keepdims=True)
    var = jnp.mean(jnp.square(xf - mu), axis=-1, keepdims=True)
    return ((xf - mu) * lax.rsqrt(var + LN_EPS) * g + b).astype(x.dtype)


def causal_dwconv(xp, w, bias):
    K = w.shape[0]
    T = xp.shape[1] - K + 1
    out = xp[:, 0:T] * w[0]
    for j in range(1, K):
        out = out + xp[:, j:j + T] * w[j]
    return out + bias


def chunk_len(T):
    return CHUNK if T % CHUNK == 0 else T


def to_chunks(a, L):
    B, T = a.shape[0], a.shape[1]
    return jnp.moveaxis(a.reshape((B, T // L, L) + a.shape[2:]), 1, 0)


def from_chunks(a):
    nc, B, L = a.shape[0], a.shape[1], a.shape[2]
    return jnp.moveaxis(a, 0, 1).reshape((B, nc * L) + a.shape[3:])


def mlstm_chunk(carry, inp):
    C, n, m = carry
    q, k, v, ig, lf = inp
    L = q.shape[1]
    causal = jnp.tril(jnp.ones((L, L), dtype=bool))[None, :, :, None]
    b = jnp.cumsum(lf, axis=1)
    log_w = jnp.where(causal, b[:, :, None, :] - b[:, None, :, :] + ig[:, None, :, :], -jnp.inf)
    log_inter = b + m[:, None, :]
    m_t = jnp.maximum(log_inter, jnp.max(log_w, axis=2))
    w_intra = jnp.exp(log_w - m_t[:, :, None, :])
    w_inter = jnp.exp(log_inter - m_t)
    s = jnp.einsum('bthd,bshd->btsh', q, k) * w_intra
    num = jnp.einsum('btsh,bshe->bthe', s, v) + w_inter[..., None] * jnp.einsum('bthd,bhde->bthe', q, C)
    den = jnp.sum(s, axis=2) + w_inter * jnp.einsum('bthd,bhd->bth', q, n)
    h = num / jnp.maximum(jnp.abs(den), jnp.exp(-m_t))[..., None]
    m_end = m_t[:, -1]
    w_end = jnp.exp(b[:, -1:, :] - b + ig - m_end[:, None, :])
    decay = jnp.exp(b[:, -1] + m - m_end)
    C_new = decay[..., None, None] * C + jnp.einsum('bsh,bshd,bshe->bhde', w_end, k, v)
    n_new = decay[..., None] * n + jnp.einsum('bsh,bshd->bhd', w_end, k)
    return (C_new, n_new, m_end), h


def mlstm_mixer(q, k, v, o_pre, i_pre, f_pre, b_i, b_f, norm_w, C0, n0, m0):
    f32 = jnp.float32
    B, T = q.shape[0], q.shape[1]
    q = q.reshape(B, T, M_HEADS, M_DK).astype(f32)
    k = k.reshape(B, T, M_HEADS, M_DK).astype(f32) * (M_DK ** -0.5)
    v = v.reshape(B, T, M_HEADS, M_DV).astype(f32)
    ig = (i_pre + b_i).astype(f32)
    lf = jax.nn.log_sigmoid((f_pre + b_f).astype(f32))
    L = chunk_len(T)
    xs = (to_chunks(q, L), to_chunks(k, L), to_chunks(v, L), to_chunks(ig, L), to_chunks(lf, L))
    (C, n, m), h = lax.scan(mlstm_chunk, (C0.astype(f32), n0.astype(f32), m0.astype(f32)), xs)
    h = from_chunks(h)
    mu = jnp.mean(h, axis=-1, keepdims=True)
    var = jnp.mean(jnp.square(h - mu), axis=-1, keepdims=True)
    hn = ((h - mu) * lax.rsqrt(var + GN_EPS)).reshape(B, T, M_WIDTH) * norm_w.astype(f32)
    out = hn * jax.nn.sigmoid(o_pre.astype(f32))
    return out, C, n, m


def ssd_chunk(S, inp):
    xs, dt, a, Bm, Cm = inp
    L = xs.shape[1]
    causal = jnp.tril(jnp.ones((L, L), dtype=bool))[None, :, :, None, None]
    b = jnp.cumsum(a, axis=1)
    decay = jnp.exp(jnp.where(causal, b[:, :, None] - b[:, None], -jnp.inf))
    cb = jnp.einsum('btgn,bsgn->btsg', Cm, Bm)
    mw = cb[..., None] * decay * dt[:, None]
    y = jnp.einsum('btsgh,bsghp->btghp', mw, xs) + jnp.exp(b)[..., None] * jnp.einsum('btgn,bghpn->btghp', Cm, S)
    w_end = jnp.exp(b[:, -1:] - b) * dt
    S_new = jnp.exp(b[:, -1])[..., None, None] * S + jnp.einsum('bsgh,bsgn,bsghp->bghpn', w_end, Bm, xs)
    return S_new, y


def ssd_mixer(z, xbc, dt_pre, conv_w, conv_b, dt_bias, A_log, D_skip, norm_w, S0, conv0):
    f32 = jnp.float32
    B, T = z.shape[0], z.shape[1]
    HG = S_HEADS // S_GROUPS
    xp = jnp.concatenate([conv0.astype(xbc.dtype), xbc], axis=1)
    conv_new = xp[:, -(S_CONV - 1):]
    xbc = jax.nn.silu(causal_dwconv(xp, conv_w, conv_b)).astype(f32)
    xs, Bm, Cm = jnp.split(xbc, [S_WIDTH, S_WIDTH + S_GROUPS * S_STATE], axis=-1)
    xs = xs.reshape(B, T, S_GROUPS, HG, S_HEADDIM)
    Bm = Bm.reshape(B, T, S_GROUPS, S_STATE)
    Cm = Cm.reshape(B, T, S_GROUPS, S_STATE)
    dt = jax.nn.softplus(dt_pre.astype(f32) + dt_bias.astype(f32)).reshape(B, T, S_GROUPS, HG)
    A = -jnp.exp(A_log.astype(f32)).reshape(S_GROUPS, HG)
    a = dt * A
    L = chunk_len(T)
    S0 = S0.astype(f32).reshape(B, S_GROUPS, HG, S_HEADDIM, S_STATE)
    inp = (to_chunks(xs, L), to_chunks(dt, L), to_chunks(a, L), to_chunks(Bm, L), to_chunks(Cm, L))
    S, y = lax.scan(ssd_chunk, S0, inp)
    y = from_chunks(y) + D_skip.astype(f32).reshape(S_GROUPS, HG)[..., None] * xs
    g = (y.reshape(B, T, S_WIDTH) * jax.nn.silu(z.astype(f32))).reshape(B, T, S_GROUPS, S_WIDTH // S_GROUPS)
    g = g * lax.rsqrt(jnp.mean(jnp.square(g), axis=-1, keepdims=True) + GN_EPS)
    out = g.reshape(B, T, S_WIDTH) * norm_w.astype(f32)
    return out, S.reshape(B, S_HEADS, S_HEADDIM, S_STATE), conv_new


def conv_ffn(x, w_up, conv_w, conv_b, w_down, buf):
    up = x @ w_up
    xp = jnp.concatenate([buf.astype(up.dtype), up], axis=1)
    buf_new = xp[:, -(FFN_CONV - 1):]
    c = causal_dwconv(xp, conv_w, conv_b)
    g, val = jnp.split(c, 2, axis=-1)
    return (jax.nn.silu(g) * val) @ w_down, buf_new


def layer(x, st, p):
    C0, n0, m0, S0, sconv0, fconv0 = st
    (w_in, b_i, b_f, m_norm_w, s_conv_w, s_conv_b, dt_bias, A_log, D_skip, s_norm_w, w_out,
     ln1_g, ln1_b, w_up, f_conv_w, f_conv_b, w_down, ln2_g, ln2_b) = p
    u = x @ w_in
    q, k, v, o_pre, i_pre, f_pre, z, xbc, dt_pre = split_columns(u, SPLIT_SIZES)
    h_m, C, n, m = mlstm_mixer(q, k, v, o_pre, i_pre, f_pre, b_i, b_f, m_norm_w, C0, n0, m0)
    h_s, S, sconv = ssd_mixer(z, xbc, dt_pre, s_conv_w, s_conv_b, dt_bias, A_log, D_skip, s_norm_w, S0, sconv0)
    mix = jnp.concatenate([h_m, h_s], axis=-1).astype(x.dtype) @ w_out
    x = layer_norm(ALPHA * x + mix, ln1_g, ln1_b)
    f, fconv = conv_ffn(x, w_up, f_conv_w, f_conv_b, w_down, fconv0)
    x = layer_norm(ALPHA * x + f, ln2_g, ln2_b)
    return x, (C, n, m, S, sconv, fconv)


def trunk(x, states, params):
    new = [[] for _ in states]
    for l in range(DEPTH):
        st_l = tuple(s[l] for s in states)
        p_l = tuple(w[l] for w in params)
        x, ns = layer(x, st_l, p_l)
        for j in range(len(ns)):
            new[j].append(ns[j])
    return x, tuple(jnp.stack(a, axis=0) for a in new)


def setup_inputs(seed: int = 0) -> dict:
    key = jax.random.key(seed)
    ks = iter(jax.random.split(key, 40))
    f32 = jnp.float32

    def nrm(shape, scale):
        return scale * jax.random.normal(next(ks), shape, f32)

    x_prompt = nrm((BATCH, SEQ, D_MODEL), 1.0)
    x_sample = nrm((DEC_BATCH, DEC_SEQ, D_MODEL), 1.0)
    state_mlstm_C = nrm((DEPTH, DEC_BATCH, M_HEADS, M_DK, M_DV), 0.5)
    state_mlstm_n = nrm((DEPTH, DEC_BATCH, M_HEADS, M_DK), 0.5)
    state_mlstm_m = nrm((DEPTH, DEC_BATCH, M_HEADS), 1.0)
    state_ssm = nrm((DEPTH, DEC_BATCH, S_HEADS, S_HEADDIM, S_STATE), 0.5)
    state_ssm_conv = nrm((DEPTH, DEC_BATCH, S_CONV - 1, S_CONV_DIM), 1.0)
    state_ffn_conv = nrm((DEPTH, DEC_BATCH, FFN_CONV - 1, 2 * D_FF), 1.0)
    w_in = nrm((DEPTH, D_MODEL, IN_DIM), D_MODEL ** -0.5)
    mlstm_b_i = nrm((DEPTH, M_HEADS), 0.1)
    mlstm_b_f = jnp.linspace(3.0, 6.0, M_HEADS, dtype=f32)[None, :] + nrm((DEPTH, M_HEADS), 0.1)
    mlstm_norm_w = 1.0 + nrm((DEPTH, M_WIDTH), 0.02)
    ssm_conv_w = nrm((DEPTH, S_CONV, S_CONV_DIM), S_CONV ** -0.5)
    ssm_conv_b = nrm((DEPTH, S_CONV_DIM), 0.02)
    dt0 = jnp.exp(jax.random.uniform(next(ks), (DEPTH, S_HEADS), f32, math.log(1e-3), math.log(1e-1)))
    ssm_dt_bias = dt0 + jnp.log(-jnp.expm1(-dt0))
    ssm_A_log = jnp.log(jax.random.uniform(next(ks), (DEPTH, S_HEADS), f32, 1.0, 16.0))
    ssm_D = 1.0 + nrm((DEPTH, S_HEADS), 0.1)
    ssm_norm_w = 1.0 + nrm((DEPTH, S_WIDTH), 0.02)
    w_out = nrm((DEPTH, MIX_WIDTH, D_MODEL), BETA * MIX_WIDTH ** -0.5)
    ln1_g = 1.0 + nrm((DEPTH, D_MODEL), 0.02)
    ln1_b = nrm((DEPTH, D_MODEL), 0.02)
    ffn_w_up = nrm((DEPTH, D_MODEL, 2 * D_FF), D_MODEL ** -0.5)
    ffn_conv_w = nrm((DEPTH, FFN_CONV, 2 * D_FF), FFN_CONV ** -0.5)
    ffn_conv_b = nrm((DEPTH, 2 * D_FF), 0.02)
    ffn_w_down = nrm((DEPTH, D_FF, D_MODEL), BETA * D_FF ** -0.5)
    ln2_g = 1.0 + nrm((DEPTH, D_MODEL), 0.02)
    ln2_b = nrm((DEPTH, D_MODEL), 0.02)
    return {'x_prompt': x_prompt, 'x_sample': x_sample,
            'state_mlstm_C': state_mlstm_C, 'state_mlstm_n': state_mlstm_n, 'state_mlstm_m': state_mlstm_m,
            'state_ssm': state_ssm, 'state_ssm_conv': state_ssm_conv, 'state_ffn_conv': state_ffn_conv,
            'w_in': w_in, 'mlstm_b_i': mlstm_b_i, 'mlstm_b_f': mlstm_b_f, 'mlstm_norm_w': mlstm_norm_w,
            'ssm_conv_w': ssm_conv_w, 'ssm_conv_b': ssm_conv_b, 'ssm_dt_bias': ssm_dt_bias,
            'ssm_A_log': ssm_A_log, 'ssm_D': ssm_D, 'ssm_norm_w': ssm_norm_w, 'w_out': w_out,
            'ln1_g': ln1_g, 'ln1_b': ln1_b, 'ffn_w_up': ffn_w_up, 'ffn_conv_w': ffn_conv_w,
            'ffn_conv_b': ffn_conv_b, 'ffn_w_down': ffn_w_down, 'ln2_g': ln2_g, 'ln2_b': ln2_b}


def reference(x_prompt, x_sample, state_mlstm_C, state_mlstm_n, state_mlstm_m, state_ssm, state_ssm_conv,
              state_ffn_conv, w_in, mlstm_b_i, mlstm_b_f, mlstm_norm_w, ssm_conv_w, ssm_conv_b, ssm_dt_bias,
              ssm_A_log, ssm_D, ssm_norm_w, w_out, ln1_g, ln1_b, ffn_w_up, ffn_conv_w, ffn_conv_b,
              ffn_w_down, ln2_g, ln2_b):
    params = (w_in, mlstm_b_i, mlstm_b_f, mlstm_norm_w, ssm_conv_w, ssm_conv_b, ssm_dt_bias, ssm_A_log,
              ssm_D, ssm_norm_w, w_out, ln1_g, ln1_b, ffn_w_up, ffn_conv_w, ffn_conv_b, ffn_w_down,
              ln2_g, ln2_b)
    f32 = jnp.float32
    B = x_prompt.shape[0]
    zero_states = (jnp.zeros((DEPTH, B, M_HEADS, M_DK, M_DV), f32),
                   jnp.zeros((DEPTH, B, M_HEADS, M_DK), f32),
                   jnp.zeros((DEPTH, B, M_HEADS), f32),
                   jnp.zeros((DEPTH, B, S_HEADS, S_HEADDIM, S_STATE), f32),
                   jnp.zeros((DEPTH, B, S_CONV - 1, S_CONV_DIM), x_prompt.dtype),
                   jnp.zeros((DEPTH, B, FFN_CONV - 1, 2 * D_FF), x_prompt.dtype))
    y_prompt, (p_C, p_n, p_m, p_ssm, p_sconv, p_fconv) = trunk(x_prompt, zero_states, params)
    past = (state_mlstm_C, state_mlstm_n, state_mlstm_m, state_ssm, state_ssm_conv, state_ffn_conv)
    y_sample, (s_C, s_n, s_m, s_ssm, s_sconv, s_fconv) = trunk(x_sample, past, params)
    return (y_prompt, y_sample, p_C, p_n, p_m, p_ssm, p_sconv, p_fconv,
            s_C, s_n, s_m, s_ssm, s_sconv, s_fconv)
```

```python
import numpy as np
from contextlib import ExitStack
import concourse.bass as bass
import concourse.mybir as mybir
from concourse.bass_utils import run_bass_kernel_spmd

F32 = mybir.dt.float32
BF16 = mybir.dt.bfloat16
AF = mybir.ActivationFunctionType
ALU = mybir.AluOpType
AX = mybir.AxisListType

NCORES = 8
D = 2048
TP = 2048
NS = 16
LS = 8
NT = TP + NS * LS
NTB = NT // 128
DFF = 5504
NCH = DFF // 128
IN_DIM = 6680
ALPHA = 4 ** 0.25
LN_EPS = 1e-5
GN_EPS = 1e-6
TGS = [(0, 512), (512, 512), (1024, 512), (1536, 512), (2048, 128)]


class Buf:
    __slots__ = ("w", "r", "name", "track")

    def __init__(self, name="", track=True):
        self.w = None
        self.r = {}
        self.name = name
        self.track = track


class V:
    __slots__ = ("ap", "buf")

    def __init__(self, ap, buf=None):
        self.ap = ap
        self.buf = buf if buf is not None else Buf()

    def __getitem__(self, k):
        return V(self.ap[k], self.buf)

    def re(self, pat, **kw):
        return V(self.ap.rearrange(pat, **kw), self.buf)

    def bc(self, axis, shape):
        return V(self.ap.unsqueeze(axis).to_broadcast(list(shape)), self.buf)

    def bitcast(self, dt):
        return V(self.ap.bitcast(dt), self.buf)

    def sub(self, ap):
        return V(ap, self.buf)

    def newbuf(self):
        return V(self.ap, Buf())


class Ctx:
    ENG = ("pe", "act", "dve", "pool", "sp")

    def __init__(self, nc, n_dma_sems=48):
        self.nc = nc
        self.eng = {"pe": nc.tensor, "act": nc.scalar, "dve": nc.vector, "pool": nc.gpsimd, "sp": nc.sync}
        self.sem = {e: nc.alloc_semaphore("s_" + e) for e in self.ENG}
        self.cnt = {e: 0 for e in self.ENG}
        self.seen = {e: {} for e in self.ENG}
        self.dsem = [nc.alloc_semaphore("d%d" % i) for i in range(n_dma_sems)]
        self.dval = [0] * n_dma_sems
        self.dtok = [None] * n_dma_sems
        self.dnext = 0
        self.dnext_sw = 0
        self.n_inst = 0
        self.n_wait = 0

    def _wait(self, e, tok):
        if tok is None:
            return
        key, sem, val, snap = tok
        if key == e and e == "pe":
            return
        seen = self.seen[e]
        if seen.get(key, 0) >= val:
            return
        self.eng[e].wait_ge(sem, val)
        self.n_wait += 1
        seen[key] = val
        for k, v in snap.items():
            if seen.get(k, 0) < v:
                seen[k] = v

    def _deps(self, e, reads, writes):
        for b in reads:
            if b.track:
                self._wait(e, b.w)
        for b in writes:
            if b.track:
                self._wait(e, b.w)
                for tok in b.r.values():
                    self._wait(e, tok)

    def op(self, e, fn, reads=(), writes=(), inc=True):
        self._deps(e, reads, writes)
        ins = fn(self.eng[e])
        self.n_inst += 1
        val = self.cnt[e] + 1
        if inc:
            ins.then_inc(self.sem[e], 1)
            self.cnt[e] = val
        tok = (e, self.sem[e], val, dict(self.seen[e]))
        for b in reads:
            if b.track:
                b.r[e] = tok
        for b in writes:
            if b.track:
                b.w = tok
                b.r = {}
        return tok

    def dma(self, q, out, in_, **kw):
        reads, writes = [in_.buf], [out.buf]
        self._deps(q, reads, writes)
        nsw = 12
        if q == "pool":
            j = self.dnext_sw
            self.dnext_sw = (j + 1) % nsw
        else:
            j = nsw + self.dnext
            self.dnext = (self.dnext + 1) % (len(self.dsem) - nsw)
        self._wait(q, self.dtok[j])
        ins = self.eng[q].dma_start(out=out.ap, in_=in_.ap, **kw)
        self.n_inst += 1
        self.dval[j] += 16
        ins.then_inc(self.dsem[j], 16)
        tok = ("d%d" % j, self.dsem[j], self.dval[j], dict(self.seen[q]))
        self.dtok[j] = tok
        for b in reads:
            if b.track:
                b.r["dma%d" % j] = tok
        for b in writes:
            if b.track:
                b.w = tok
                b.r = {}
        return tok

    def barrier(self):
        for e in self.ENG:
            for e2 in self.ENG:
                if e2 != e and self.cnt[e2] > 0:
                    self._wait(e, (e2, self.sem[e2], self.cnt[e2], {}))
            for tok in self.dtok:
                self._wait(e, tok)

    def finish(self):
        for tok in self.dtok:
            self._wait("sp", tok)
        for e2 in self.ENG:
            if e2 != "sp" and self.cnt[e2] > 0:
                self._wait("sp", (e2, self.sem[e2], self.cnt[e2], {}))


def run_gens(gens):
    gens = list(gens)
    while gens:
        for g in list(gens):
            try:
                next(g)
            except StopIteration:
                gens.remove(g)


def rr_gen(gens):
    gens = list(gens)
    while gens:
        for g in list(gens):
            try:
                next(g)
            except StopIteration:
                gens.remove(g)
        yield


class Prog:
    def __init__(self, debug=False, upto="Z"):
        self.debug = debug
        self.upto = upto
        nc = self.nc = bass.Bass("TRN2", target_bir_lowering=False)
        self.c = Ctx(nc)
        self.es = None
        self.PS = [V(nc.alloc_psum_tensor("ps%d" % i, [128, 512], F32).ap()) for i in range(8)]
        self.rr = 0
        self.nsb = 0

    def din(self, name, shape, dt=F32):
        return V(self.nc.dram_tensor(name, list(shape), dt, kind="ExternalInput").ap(), Buf(name, track=False))

    def dout(self, name, shape, dt=F32):
        return V(self.nc.dram_tensor(name, list(shape), dt, kind="ExternalOutput").ap(), Buf(name, track=False))

    def dscr(self, name, shape, dt=F32):
        kind = "ExternalOutput" if self.debug else "Internal"
        return V(self.nc.dram_tensor(name, list(shape), dt, kind=kind).ap(), Buf(name, track=False))

    def sb(self, name, shape, dt=F32):
        self.nsb += 1
        t = self.es.enter_context(self.nc.sbuf_tensor("%s_%d" % (name, self.nsb), list(shape), dt))
        return V(t.ap())

    def TT(self, e, out, a, b, op):
        self.c.op(e, lambda g: g.tensor_tensor(out=out.ap, in0=a.ap, in1=b.ap, op=op), [a.buf, b.buf], [out.buf])

    def TS(self, e, out, a, s1, op0, s2=None, op1=None):
        rd = [a.buf]
        s1a, s2a = s1, s2
        if isinstance(s1, V):
            rd.append(s1.buf); s1a = s1.ap
        if isinstance(s2, V):
            rd.append(s2.buf); s2a = s2.ap
        if op1 is None:
            self.c.op(e, lambda g: g.tensor_scalar(out=out.ap, in0=a.ap, scalar1=s1a, scalar2=None, op0=op0), rd, [out.buf])
        else:
            self.c.op(e, lambda g: g.tensor_scalar(out=out.ap, in0=a.ap, scalar1=s1a, scalar2=s2a, op0=op0, op1=op1), rd, [out.buf])

    def STT(self, e, out, a, s, b, op0, op1):
        rd = [a.buf, b.buf]
        sa = s
        if isinstance(s, V):
            rd.append(s.buf); sa = s.ap
        self.c.op(e, lambda g: g.scalar_tensor_tensor(out=out.ap, in0=a.ap, scalar=sa, in1=b.ap, op0=op0, op1=op1), rd, [out.buf])

    def ACT(self, out, a, func, bias=None, scale=None, accum=None):
        rd = [a.buf]
        wr = [out.buf]
        kw = {}
        if bias is not None:
            if isinstance(bias, V):
                rd.append(bias.buf); kw["bias"] = bias.ap
            else:
                kw["bias"] = float(bias)
        if scale is not None:
            if isinstance(scale, V):
                rd.append(scale.buf); kw["scale"] = scale.ap
            else:
                kw["scale"] = float(scale)
        if accum is not None:
            wr.append(accum.buf); kw["accum_out"] = accum.ap
        self.c.op("act", lambda g: g.activation(out=out.ap, in_=a.ap, func=func, **kw), rd, wr)

    def COPY(self, e, out, a):
        if e == "act":
            self.c.op("act", lambda g: g.copy(out=out.ap, in_=a.ap), [a.buf], [out.buf])
        else:
            self.c.op(e, lambda g: g.tensor_copy(out=out.ap, in_=a.ap), [a.buf], [out.buf])

    def MEMSET(self, e, out, val):
        self.c.op(e, lambda g: g.memset(out.ap, val), [], [out.buf])

    def MM(self, out, lhsT, rhs, start=True, stop=True, inc=True):
        self.c.op("pe", lambda g: g.matmul(out.ap, lhsT.ap, rhs.ap, start=start, stop=stop), [lhsT.buf, rhs.buf], [out.buf], inc=inc)

    def TR(self, out, a, ident, inc=True):
        self.c.op("pe", lambda g: g.transpose(out.ap, a.ap, ident.ap), [a.buf, ident.buf], [out.buf], inc=inc)

    def DMA(self, q, out, a, **kw):
        self.c.dma(q, out, a, **kw)

    def RECIP(self, out, a):
        self.c.op("dve", lambda g: g.reciprocal(out=out.ap, in_=a.ap), [a.buf], [out.buf])

    def REDMAX(self, out, a):
        self.c.op("dve", lambda g: g.tensor_reduce(out=out.ap, in_=a.ap, axis=AX.X, op=ALU.max), [a.buf], [out.buf])

    def evac(self, out, a, scale=None):
        self.rr += 1
        if self.rr % 2 == 0:
            if scale is None:
                self.COPY("act", out, a)
            else:
                self.c.op("act", lambda g: g.mul(out=out.ap, in_=a.ap, mul=float(scale)), [a.buf], [out.buf])
        else:
            if scale is None:
                self.COPY("dve", out, a)
            else:
                self.TS("dve", out, a, float(scale), ALU.mult)

    def declare(self):
        L = 2
        self.x_in = self.din("x_in", [NT, D])
        self.w_in = self.din("w_in", [L, D, IN_DIM])
        self.w_out = self.din("w_out", [L, D, D])
        self.w_up = self.din("w_up", [L, D, 2 * DFF])
        self.w_down = self.din("w_down", [L, DFF, D])
        self.b_i = self.din("b_i", [L, 4]); self.b_f = self.din("b_f", [L, 4])
        self.mnw = self.din("mnw", [L, 1024]); self.snw = self.din("snw", [L, 1024])
        self.dtb = self.din("dtb", [L, 16]); self.alog = self.din("alog", [L, 16]); self.dsk = self.din("dsk", [L, 16])
        self.ln1g = self.din("ln1g", [L, D]); self.ln1b = self.din("ln1b", [L, D])
        self.ln2g = self.din("ln2g", [L, D]); self.ln2b = self.din("ln2b", [L, D])
        self.scw = self.din("scw", [L, 128, 12, 4]); self.scb = self.din("scb", [L, 128, 12])
        self.scb_row = self.din("scb_row", [L, 1536])
        self.fcw = self.din("fcw", [L, 128, 86, 3]); self.fcb = self.din("fcb", [L, 128, 86])
        self.sC = self.din("sC", [L, NS, 4, 256, 256])
        self.snT = self.din("snT", [L, 4, 128, NS, 2])
        self.smrep = self.din("smrep", [L, 128, 4])
        self.sS = self.din("sS", [L, 2, 128, NS, 512])
        self.ssc = self.din("ssc", [L, 128, 12, NS, 3])
        self.sfc = self.din("sfc", [L, 128, 86, NS, 2])
        self.cst = self.din("cst", [12, 128, 128])
        self.smt = self.din("smt", [NS, 128])
        self.seqm = self.din("seqm", [2, 128, NS])
        self.y = self.dout("y", [NT, D])
        self.o_pC = self.dout("o_pC", [L, 4, 256, 256]); self.o_pnT = self.dout("o_pnT", [L, 128, 4, 2])
        self.o_pm = self.dout("o_pm", [L, 4]); self.o_pS = self.dout("o_pS", [L, 128, 1024])
        self.o_psc = self.dout("o_psc", [L, 128, 12, 3]); self.o_pfc = self.dout("o_pfc", [L, 128, 86, 2])
        self.o_sC = self.dout("o_sC", [L, NS, 4, 256, 256]); self.o_snT = self.dout("o_snT", [L, 4, 128, NS, 2])
        self.o_sm = self.dout("o_sm", [L, NS, 4]); self.o_sS = self.dout("o_sS", [L, 2, 128, NS, 512])
        self.o_ssc = self.dout("o_ssc", [L, 128, 12, NS, 3]); self.o_sfc = self.dout("o_sfc", [L, 128, 86, NS, 2])
        self.qT_s = [self.dscr("qT_s%d" % l, [1024, NT], BF16) for l in range(L)]
        self.kT_s = [self.dscr("kT_s%d" % l, [1024, NT], BF16) for l in range(L)]
        self.kv_s = [self.dscr("kv_s%d" % l, [NT, 2048], BF16) for l in range(L)]
        self.oz_s = [self.dscr("oz_s%d" % l, [NT, 2048]) for l in range(L)]
        self.g_s = [self.dscr("g_s%d" % l, [NT, 24]) for l in range(L)]
        self.xbcT_s = [self.dscr("xbcT_s%d" % l, [1536, NT]) for l in range(L)]
        self.mix_s = [self.dscr("mix_s%d" % l, [NT, 2048], BF16) for l in range(L)]
        self.x1_s = [self.dscr("x1_s%d" % l, [NT, D]) for l in range(L)]
        self.hT_s = [self.dscr("hT_s%d" % l, [NTB, 128, NCH, 128], BF16) for l in range(L)]
        self.pre_s = [self.dscr("pre_s%d" % l, [NT, D]) for l in range(L)]
        self.x2_s = self.dscr("x2_s", [NT, D])

    def load_consts(self, kinds=True):
        K = {}
        names = ["identf", "ones", "tri_p", "maskn_p", "masktp_p", "mask01t_p", "elast_p",
                 "tri_s", "maskn_s", "masktp_s", "mask01t_s", "elast_s"]
        for i, n in enumerate(names):
            t = self.sb("k_" + n, [128, 128])
            self.DMA("sp", t, self.cst[i])
            K[n] = t
        idb = self.sb("k_identb", [128, 128], BF16)
        self.DMA("pool", idb, self.cst[0])
        K["identb"] = idb
        return K

    def build_xT(self, src, xT, identb):
        xb = [self.sb("xb%d" % i, [128, D], BF16) for i in range(2)]
        for tb in range(NTB):
            b = xb[tb % 2]
            self.DMA("pool", b, src[tb * 128:(tb + 1) * 128, :])
            for half in range(2):
                ps = self.PS[4 + (tb * 2 + half) % 4]
                psb = ps.bitcast(BF16)
                for k in range(8):
                    kk = half * 8 + k
                    self.TR(psb[:, k * 128:(k + 1) * 128], b[:, kk * 128:(kk + 1) * 128], identb, inc=(k == 7))
                self.evac(xT[:, half * 8:(half + 1) * 8, tb * 128:(tb + 1) * 128], psb.re("p (k t) -> p k t", k=8))

    def phaseA(self, l, src):
        c = self.c
        with ExitStack() as es:
            self.es = es
            identb = self.sb("identb", [128, 128], BF16)
            self.DMA("pool", identb, self.cst[0])
            xT = self.sb("xT", [128, 16, NT], BF16)
            self.build_xT(src, xT, identb)
            wb = [self.sb("wA%d" % i, [128, 16, 512], BF16) for i in range(2)]
            st32 = [self.sb("st32_%d" % i, [128, 512]) for i in range(4)]
            st16 = [self.sb("st16_%d" % i, [128, 512], BF16) for i in range(4)]
            fm32 = [self.sb("fm32_%d" % i, [128, NT]) for i in range(2)]
            fm16 = [self.sb("fm16_%d" % i, [128, NT], BF16) for i in range(2)]
            wsrc = self.w_in[l].re("(k p) n -> p k n", p=128)
            jobs = [(0, 512, "q"), (512, 512, "q"), (1024, 512, "k"), (1536, 512, "k"),
                    (2048, 512, "v"), (2560, 512, "v"), (3072, 512, "o"), (3584, 512, "o"),
                    (4104, 512, "z"), (4616, 512, "z"), (5128, 512, "x"), (5640, 512, "x"), (6152, 512, "x"),
                    (-1, 24, "g")]

            def loadw(j):
                c0, ncol, mode = jobs[j]
                w = wb[j % 2]
                if mode == "g":
                    self.DMA("pool", w[:, :, 0:8], wsrc[:, :, 4096:4104])
                    self.DMA("pool", w[:, :, 8:24], wsrc[:, :, 6664:6680])
                else:
                    self.DMA("pool", w, wsrc[:, :, c0:c0 + ncol])

            cnt = {"q": 0, "k": 0, "v": 0, "o": 0, "z": 0, "x": 0}
            nps = 0
            nst = 0
            nfm = 0
            loadw(0)
            for j in range(len(jobs)):
                if j + 1 < len(jobs):
                    loadw(j + 1)
                c0, ncol, mode = jobs[j]
                w = wb[j % 2]
                if mode in ("q", "x"):
                    for cc in range(4):
                        if mode == "x":
                            stg = fm32[nfm % 2]
                        else:
                            stg = fm16[nfm % 2]
                        nfm += 1
                        for (t0, tn) in TGS:
                            ps = self.PS[nps % 4]; nps += 1
                            for k in range(16):
                                self.MM(ps[:, 0:tn], w[:, k, cc * 128:(cc + 1) * 128], xT[:, k, t0:t0 + tn],
                                        start=(k == 0), stop=(k == 15), inc=(k == 15))
                            self.evac(stg[:, t0:t0 + tn], ps[:, 0:tn])
                        ch = cnt[mode] * 4 + cc
                        dst = {"q": self.qT_s, "x": self.xbcT_s}[mode][l]
                        self.DMA("sp", dst[ch * 128:(ch + 1) * 128, :], stg)
                if mode in ("k", "v", "o", "z", "g"):
                    for tb in range(NTB):
                        ps = self.PS[nps % 4]; nps += 1
                        for k in range(16):
                            self.MM(ps[:, 0:ncol], xT[:, k, tb * 128:(tb + 1) * 128], w[:, k, 0:ncol],
                                    start=(k == 0), stop=(k == 15), inc=(k == 15))
                        rows = slice(tb * 128, (tb + 1) * 128)
                        if mode in ("k", "v"):
                            stg = st16[nst % 4]; nst += 1
                            self.evac(stg, ps, scale=(0.0625 if mode == "k" else None))
                            cb = (0 if mode == "k" else 1024) + cnt[mode] * 512
                            self.DMA("sp", self.kv_s[l][rows, cb:cb + 512], stg)
                        elif mode in ("o", "z"):
                            stg = st32[nst % 4]; nst += 1
                            self.evac(stg, ps)
                            cb = (0 if mode == "o" else 1024) + cnt[mode] * 512
                            self.DMA("sp", self.oz_s[l][rows, cb:cb + 512], stg)
                        else:
                            stg = st32[nst % 4]; nst += 1
                            self.evac(stg[:, 0:24], ps[:, 0:24])
                            self.DMA("sp", self.g_s[l][rows, :], stg[:, 0:24])
                if mode in cnt:
                    cnt[mode] += 1
            c.barrier()
        self.es = None

    def phaseB(self, l):
        c = self.c
        PS = self.PS
        with ExitStack() as es:
            self.es = es
            K = self.load_consts()
            identf, ones, identb = K["identf"], K["ones"], K["identb"]
            bi = self.sb("bi", [128, 4]); bf = self.sb("bf", [128, 4])
            self.DMA("sp", bi, V(self.b_i.ap[l:l + 1, :].partition_broadcast(128), self.b_i.buf))
            self.DMA("sp", bf, V(self.b_f.ap[l:l + 1, :].partition_broadcast(128), self.b_f.buf))
            mnw = self.sb("mnw", [128, 1024])
            self.DMA("sp", mnw, V(self.mnw.ap[l:l + 1, :].partition_broadcast(128), self.mnw.buf))
            smt = self.sb("smt", [128, NS, 128], BF16)
            self.DMA("pool", smt, V(self.smt.ap.unsqueeze(0).to_broadcast([128, NS, 128]), self.smt.buf))
            seqm = self.sb("seqm", [128, NS]); seql = self.sb("seql", [128, NS])
            self.DMA("sp", seqm, self.seqm[0]); self.DMA("sp", seql, self.seqm[1])
            C32 = self.sb("C32", [128, 4, 2, 257]); Cbf = self.sb("Cbf", [128, 4, 2, 257], BF16)
            self.MEMSET("dve", C32, 0.0); self.MEMSET("pool", Cbf, 0.0)
            mprev = self.sb("mprev", [128, 4]); self.MEMSET("dve", mprev, 0.0)
            vaug = self.sb("vaug", [128, 4, 257], BF16); self.MEMSET("pool", vaug, 1.0)
            qT = [self.sb("qT%d" % i, [128, 8, 128], BF16) for i in range(2)]
            kT = [self.sb("kT%d" % i, [128, 8, 128], BF16) for i in range(2)]
            kv = [self.sb("kv%d" % i, [128, 2048], BF16) for i in range(2)]
            oo = [self.sb("oo%d" % i, [128, 1024]) for i in range(2)]
            gt = [self.sb("gt%d" % i, [128, 8]) for i in range(2)]
            gnames = ["ig", "fz", "e1", "sp", "b", "a", "cm", "g", "t1", "wint", "enm", "mend", "t2", "wend", "dec", "t3"]
            g4 = [{n: self.sb("g4_%s%d" % (n, i), [128, 4]) for n in gnames} for i in range(2)]
            gb8 = [self.sb("gb8_%d" % i, [128, 8]) for i in range(2)]
            glb = [self.sb("glb_%d" % i, [128, 8]) for i in range(2)]
            DT = [self.sb("DT%d" % i, [128, 4, 128]) for i in range(2)]
            R = self.sb("R", [128, 4, 128]); tmpA = self.sb("tmpA", [128, 4, 128])
            PT = self.sb("PT", [128, 4, 128], BF16)
            wv = self.sb("wv", [128, 4, 257], BF16)
            tmpI = [self.sb("tmpI%d" % i, [128, 257]) for i in range(2)]
            comb = [self.sb("comb%d" % i, [128, 257]) for i in range(2)]
            dd = [self.sb("dd%d" % i, [128, 1]) for i in range(2)]
            rr_ = [self.sb("rr%d" % i, [128, 1]) for i in range(2)]
            bn = [self.sb("bn%d" % i, [128, 6]) for i in range(2)]
            mv = [self.sb("mv%d" % i, [128, 2]) for i in range(2)]
            rstd = [self.sb("rstd%d" % i, [128, 1]) for i in range(2)]
            hh = [self.sb("hh%d" % i, [128, 256]) for i in range(2)]
            hn = self.sb("hn", [128, 1024]); sig = self.sb("sig", [128, 1024])
            mixm = self.sb("mixm", [128, 1024], BF16)
            pn_t = self.sb("pn_t", [128, 4, 2])
            Cs32s = [self.sb("Cs32_%d" % i, [128, NS, 2, 257]) for i in range(2)]
            Csbf = self.sb("Csbf", [128, NS, 2, 257], BF16)
            ns_ts = [self.sb("ns_t%d" % i, [128, NS, 2]) for i in range(2)]
            ns_o = self.sb("ns_o", [128, NS, 2])

            def load_cs(h):
                for cc in range(2):
                    self.DMA("sp", Cs32s[h % 2][:, :, cc, 0:256], self.sC[l, :, h, cc * 128:(cc + 1) * 128, :].re("i p e -> p i e"))
                self.DMA("sp", ns_ts[h % 2], self.snT[l, h])
            qTm = self.sb("qTm", [128, 2, NS, 128], BF16); wvm = self.sb("wvm", [128, NS, 257], BF16)
            R3 = self.sb("R3", [128, 4, NS]); decrep = self.sb("decrep", [128, 4, NS])

            def load(tb):
                i = tb % 2
                cols = slice(tb * 128, (tb + 1) * 128)
                self.DMA("sp", gt[i], self.g_s[l][cols, 0:8])
                self.DMA("sp", qT[i], self.qT_s[l][:, cols].re("(j p) t -> p j t", p=128))
                self.DMA("sp", kv[i], self.kv_s[l][cols, :])
                self.DMA("sp", oo[i], self.oz_s[l][cols, 0:1024])

            def gates(tb):
                i = tb % 2
                G = g4[i]
                smp = (tb == NTB - 1)
                sfx = "_s" if smp else "_p"
                TRI, MASKN, MASKTP, ELAST = K["tri" + sfx], K["maskn" + sfx], K["masktp" + sfx], K["elast" + sfx]
                g_ = gt[i]
                if smp:
                    self.DMA("sp", mprev, self.smrep[l])
                self.TT("dve", G["ig"], g_[:, 0:4], bi, ALU.add)
                self.TT("dve", G["fz"], g_[:, 4:8], bf, ALU.add)
                yield
                self.ACT(G["e1"], G["fz"], AF.Exp, scale=-1.0)
                yield
                self.ACT(G["sp"], G["e1"], AF.Ln, bias=1.0)
                yield
                self.MM(PS[0][:, 0:4], TRI, G["sp"])
                yield
                self.TS("dve", G["b"], PS[0][:, 0:4], -1.0, ALU.mult)
                yield
                self.TT("dve", G["a"], G["ig"], G["b"], ALU.subtract)
                yield
                self.TT("dve", R, identf.bc(1, [128, 4, 128]), G["a"].bc(2, [128, 4, 128]), ALU.mult)
                yield
                self.MM(PS[1], ones, R.re("p h s -> p (h s)"))
                yield
                self.TT("dve", tmpA, PS[1].re("p (h s) -> p h s", h=4), MASKN.bc(1, [128, 4, 128]), ALU.add)
                yield
                self.REDMAX(G["cm"], tmpA)
                yield
                self.TT("dve", G["g"], G["cm"], mprev, ALU.max)
                yield
                self.TT("dve", R, identf.bc(1, [128, 4, 128]), G["g"].bc(2, [128, 4, 128]), ALU.mult)
                self.TT("dve", G["t1"], mprev, G["g"], ALU.subtract)
                self.TT("dve", G["t2"], G["b"], G["g"], ALU.add)
                self.COPY("dve", gb8[i][:, 0:4], G["g"]); self.COPY("dve", gb8[i][:, 4:8], G["b"])
                yield
                self.MM(PS[1], ones, R.re("p h s -> p (h s)"))
                self.MM(PS[0][:, 8:16], ELAST, gb8[i])
                self.ACT(G["wint"], G["t1"], AF.Exp)
                self.ACT(G["enm"], G["t2"], AF.Exp, scale=-1.0)
                yield
                self.TT("dve", tmpA, PS[1].re("p (h s) -> p h s", h=4), MASKTP.bc(1, [128, 4, 128]), ALU.add)
                self.COPY("dve", glb[i], PS[0][:, 8:16])
                yield
                for h in range(4):
                    self.ACT(DT[i][:, h, :], tmpA[:, h, :], AF.Exp, bias=G["a"][:, h:h + 1], scale=-1.0)
                self.TT("dve", G["mend"], glb[i][:, 0:4], glb[i][:, 4:8], ALU.add)
                self.TT("dve", G["t3"], G["a"], glb[i][:, 0:4], ALU.subtract)
                yield
                self.ACT(G["wend"], G["t3"], AF.Exp)
                yield
                self.TT("dve", G["t3"], mprev, glb[i][:, 0:4], ALU.subtract)
                yield
                self.ACT(G["dec"], G["t3"], AF.Exp)
                yield
                if smp:
                    self.TT("dve", R3, seql.bc(1, [128, 4, NS]), G["dec"].bc(2, [128, 4, NS]), ALU.mult)
                    self.MM(PS[0][:, 64:128], ones, R3.re("p h i -> p (h i)"))
                    self.COPY("dve", decrep, PS[0][:, 64:128].re("p (h i) -> p h i", h=4))
                    self.DMA("sp", self.o_sm[l], G["mend"][7:128:8, :])
                else:
                    self.COPY("dve", mprev, G["mend"])
                    if tb == NTB - 2:
                        self.DMA("sp", self.o_pm[l:l + 1, :], G["mend"][0:1, :])
                yield

            def head(tb, h):
                i = tb % 2
                G = g4[i]
                ti = h % 2
                q_T, kv_ = qT[i], kv[i]
                pi, pn, pu = PS[3 + ti], PS[5 + ti], PS[7]
                self.MM(pi[:, 0:257], PT[:, h, :], vaug[:, h, :])
                for cc in range(2):
                    self.MM(pn[:, 0:257], q_T[:, h * 2 + cc, :], Cbf[:, h, cc, :], start=(cc == 0), stop=(cc == 1), inc=(cc == 1))
                yield
                self.ACT(tmpI[ti], pn[:, 0:257], AF.Copy, scale=G["wint"][:, h:h + 1])
                yield
                self.TT("dve", comb[ti], tmpI[ti], pi[:, 0:257], ALU.add)
                yield
                self.ACT(dd[ti], comb[ti][:, 256:257], AF.Abs)
                for cc in range(2):
                    pu2 = pu if cc == 0 else PS[2]
                    self.MM(pu2[:, 0:257], kv_[:, h * 256 + cc * 128: h * 256 + (cc + 1) * 128], wv[:, h, :])
                    self.STT("dve", C32[:, h, cc, :], C32[:, h, cc, :], G["dec"][:, h:h + 1], pu2[:, 0:257], ALU.mult, ALU.add)
                    self.COPY("act", Cbf[:, h, cc, :], C32[:, h, cc, :])
                yield
                self.TS("dve", dd[ti], dd[ti], G["enm"][:, h:h + 1], ALU.max)
                yield
                self.RECIP(rr_[ti], dd[ti])
                yield
                self.TS("dve", hh[ti], comb[ti][:, 0:256], rr_[ti], ALU.mult)
                yield
                self.c.op("dve", lambda g: g.bn_stats(out=bn[ti].ap, in_=hh[ti].ap), [hh[ti].buf], [bn[ti].buf])
                yield
                self.c.op("dve", lambda g: g.bn_aggr(out=mv[ti].ap, in_=bn[ti].ap), [bn[ti].buf], [mv[ti].buf])
                yield
                self.ACT(rstd[ti], mv[ti][:, 1:2], AF.Ln, bias=GN_EPS)
                yield
                self.ACT(rstd[ti], rstd[ti], AF.Exp, scale=-0.5)
                yield
                self.TS("dve", hn[:, h * 256:(h + 1) * 256], hh[ti], mv[ti][:, 0:1], ALU.subtract, rstd[ti], ALU.mult)
                yield

            def head_smp(tb, h):
                i = tb % 2
                G = g4[i]
                ti = 0
                q_T, kv_ = qT[i], kv[i]
                pi, pn = PS[3], PS[5]
                self.MM(pi[:, 0:257], PT[:, h, :], vaug[:, h, :])
                Cs32 = Cs32s[h % 2]
                ns_t = ns_ts[h % 2]
                if h == 0:
                    load_cs(0)
                if h + 1 < 4:
                    load_cs(h + 1)
                self.COPY("dve", Cs32[:, :, :, 256], ns_t)
                self.COPY("act", Csbf, Cs32)
                for cc in range(2):
                    self.TT("dve", qTm[:, cc], q_T[:, h * 2 + cc, :].bc(1, [128, NS, 128]), smt, ALU.mult)
                for si in range(NS):
                    for cc in range(2):
                        self.MM(pn[:, 0:257], qTm[:, cc, si, :], Csbf[:, si, cc, :],
                                start=(si == 0 and cc == 0), stop=(si == NS - 1 and cc == 1), inc=(si == NS - 1 and cc == 1))
                yield
                self.ACT(tmpI[ti], pn[:, 0:257], AF.Copy, scale=G["wint"][:, h:h + 1])
                self.TT("dve", comb[ti], tmpI[ti], pi[:, 0:257], ALU.add)
                self.ACT(dd[ti], comb[ti][:, 256:257], AF.Abs)
                self.TS("dve", dd[ti], dd[ti], G["enm"][:, h:h + 1], ALU.max)
                self.RECIP(rr_[ti], dd[ti])
                self.TS("dve", hh[ti], comb[ti][:, 0:256], rr_[ti], ALU.mult)
                self.c.op("dve", lambda g: g.bn_stats(out=bn[ti].ap, in_=hh[ti].ap), [hh[ti].buf], [bn[ti].buf])
                self.c.op("dve", lambda g: g.bn_aggr(out=mv[ti].ap, in_=bn[ti].ap), [bn[ti].buf], [mv[ti].buf])
                self.ACT(rstd[ti], mv[ti][:, 1:2], AF.Ln, bias=GN_EPS)
                self.ACT(rstd[ti], rstd[ti], AF.Exp, scale=-0.5)
                self.TS("dve", hn[:, h * 256:(h + 1) * 256], hh[ti], mv[ti][:, 0:1], ALU.subtract, rstd[ti], ALU.mult)
                yield
                self.TT("dve", wvm, wv[:, h, :].bc(1, [128, NS, 257]), seqm.bc(2, [128, NS, 257]), ALU.mult)
                n = 0
                for si in range(NS):
                    for cc in range(2):
                        pu2 = (PS[7], PS[2], PS[4], PS[6])[n % 4]
                        n += 1
                        self.MM(pu2[:, 0:257], kv_[:, h * 256 + cc * 128: h * 256 + (cc + 1) * 128], wvm[:, si, :])
                        self.STT("dve", Cs32[:, si, cc, :], Cs32[:, si, cc, :], decrep[:, h, si:si + 1], pu2[:, 0:257], ALU.mult, ALU.add)
                for cc in range(2):
                    self.DMA("sp", self.o_sC[l, :, h, cc * 128:(cc + 1) * 128, :].re("i p e -> p i e"), Cs32[:, :, cc, 0:256])
                self.COPY("dve", ns_o, Cs32[:, :, :, 256])
                self.DMA("sp", self.o_snT[l, h], ns_o)
                yield

            def heavy(tb):
                i = tb % 2
                G = g4[i]
                smp = (tb == NTB - 1)
                q_T, k_T, kv_, o_ = qT[i], kT[i], kv[i], oo[i]
                self.COPY("act", vaug[:, :, 0:256], kv_[:, 1024:2048].re("p (h e) -> p h e", h=4))
                psb = PS[2].bitcast(BF16)
                for j in range(8):
                    self.TR(psb[:, j * 128:(j + 1) * 128], kv_[:, j * 128:(j + 1) * 128], identb, inc=(j == 7))
                self.COPY("act", k_T, psb.re("p (j t) -> p j t", j=8))
                yield
                for h in range(4):
                    for cc in range(2):
                        self.MM(PS[2][:, h * 128:(h + 1) * 128], k_T[:, h * 2 + cc, :], q_T[:, h * 2 + cc, :],
                                start=(cc == 0), stop=(cc == 1), inc=(cc == 1))
                self.ACT(sig, o_, AF.Exp, scale=-1.0)
                self.ACT(sig, sig, AF.Ln, bias=1.0)
                self.ACT(sig, sig, AF.Exp, scale=-1.0)
                yield
                self.TT("dve", wv, vaug, G["wend"].bc(2, [128, 4, 257]), ALU.mult)
                self.TT("dve", PT, PS[2].re("p (h t) -> p h t", h=4), DT[i], ALU.mult)
                yield
                if not smp:
                    for hp in range(2):
                        yield from rr_gen([head(tb, 2 * hp), head(tb, 2 * hp + 1)])
                else:
                    for h in range(4):
                        yield from head_smp(tb, h)
                self.TT("dve", hn, hn, mnw, ALU.mult)
                yield
                self.TT("dve", mixm, hn, sig, ALU.mult)
                self.DMA("sp", self.mix_s[l][tb * 128:(tb + 1) * 128, 0:1024], mixm)
                if tb == NTB - 2:
                    for cc in range(2):
                        self.DMA("sp", self.o_pC[l, :, cc * 128:(cc + 1) * 128, :].re("h p e -> p h e"), C32[:, :, cc, 0:256])
                    self.COPY("dve", pn_t, C32[:, :, :, 256])
                    self.DMA("sp", self.o_pnT[l], pn_t)
                yield

            load(0)
            run_gens([gates(0)])
            for tb in range(NTB):
                if tb + 1 < NTB:
                    load(tb + 1)
                    run_gens([heavy(tb), gates(tb + 1)])
                else:
                    run_gens([heavy(tb)])
            c.barrier()
        self.es = None

    def phaseC(self, l):
        c = self.c
        PS = self.PS
        with ExitStack() as es:
            self.es = es
            K = self.load_consts()
            identf, ones = K["identf"], K["ones"]
            dtb = self.sb("dtb", [128, 16]); aneg = self.sb("aneg", [128, 16]); dsk = self.sb("dsk", [128, 16])
            self.DMA("sp", dtb, V(self.dtb.ap[l:l + 1, :].partition_broadcast(128), self.dtb.buf))
            self.DMA("sp", aneg, V(self.alog.ap[l:l + 1, :].partition_broadcast(128), self.alog.buf))
            self.DMA("sp", dsk, V(self.dsk.ap[l:l + 1, :].partition_broadcast(128), self.dsk.buf))
            self.ACT(aneg, aneg, AF.Exp)
            self.TS("dve", aneg, aneg, -1.0, ALU.mult)
            snw = self.sb("snw", [128, 1024])
            self.DMA("sp", snw, V(self.snw.ap[l:l + 1, :].partition_broadcast(128), self.snw.buf))
            cw = self.sb("cw", [128, 12, 4]); cb = self.sb("cb", [128, 12])
            self.DMA("sp", cw, self.scw[l]); self.DMA("sp", cb, self.scb[l])
            smtb = self.sb("smtb", [128, NS, 128], BF16)
            self.DMA("pool", smtb, V(self.smt.ap.unsqueeze(0).to_broadcast([128, NS, 128]), self.smt.buf))
            seqm = self.sb("seqm", [128, NS]); seql = self.sb("seql", [128, NS])
            self.DMA("sp", seqm, self.seqm[0]); self.DMA("sp", seql, self.seqm[1])
            nident = self.sb("nident", [128, 128])
            self.TS("dve", nident, identf, -1.0, ALU.mult)
            ca = [self.sb("ca%d" % i, [128, 12, 128]) for i in range(4)]
            ST32 = self.sb("ST32", [128, 1024]); STb = self.sb("STb", [128, 1024], BF16)
            self.MEMSET("dve", ST32, 0.0); self.MEMSET("pool", STb, 0.0)
            XP = [self.sb("XP%d" % i, [128, 12, 131]) for i in range(3)]
            self.MEMSET("dve", XP[0], 0.0)
            zz = [self.sb("zz%d" % i, [128, 1024]) for i in range(2)]
            gt = [self.sb("gtc%d" % i, [128, 16]) for i in range(3)]
            gn = ["fz", "e1", "dt", "a", "b", "eb", "bl", "t1", "wend", "ebl", "lnd", "nb"]
            g16 = [{n: self.sb("g16_%s%d" % (n, i), [128, 16]) for n in gn} for i in range(2)]
            xbca = self.sb("xbca", [128, 12, 128]); u_t = self.sb("u_t", [128, 12, 128])
            xs32 = [self.sb("xs32_%d" % i, [128, 1024]) for i in range(3)]
            xsbs = [self.sb("xsb%d" % i, [128, 1024], BF16) for i in range(2)]
            Btok = [self.sb("Btok%d" % i, [128, 2, 128], BF16) for i in range(3)]
            CTb = [self.sb("CTb%d" % i, [128, 2, 128], BF16) for i in range(3)]
            BTbs = [self.sb("BTb%d" % i, [128, 2, 128], BF16) for i in range(2)]
            cbm = self.sb("cbm", [128, 2, 128])
            R = [self.sb("Rc%d" % i, [128, 4, 128]) for i in range(2)]
            decT = self.sb("decT", [128, 16, 128])
            mwT = self.sb("mwT", [128, 16, 128], BF16)
            yin = [self.sb("yin%d" % i, [128, 1024]) for i in range(2)]
            wxs = [self.sb("wxs%d" % i, [128, 1024], BF16) for i in range(2)]
            y1 = self.sb("y1", [128, 1024]); y2 = self.sb("y2", [128, 1024])
            sq = self.sb("sq", [128, 512]); ss = self.sb("ss", [128, 2]); rinv = self.sb("rinv", [128, 2])
            mixs = self.sb("mixs", [128, 1024], BF16)
            sc_t = self.sb("sc_t", [128, 12, 3])
            XS = self.sb("XS", [128, 12, NS, 11])
            sc_in = self.sb("sc_in", [128, 12, NS, 3])
            Ss32 = self.sb("Ss32", [128, 8, 512]); Ssb = self.sb("Ssb", [128, 8, 512], BF16)
            CTm = self.sb("CTm", [128, NS, 128], BF16); wxsm = self.sb("wxsm", [128, 4, 512], BF16)
            R3 = self.sb("R3c", [128, NS, 16]); decS = self.sb("decS", [128, NS, 16])

            def load(tb):
                i = tb % 2
                cols = slice(tb * 128, (tb + 1) * 128)
                if tb < NTB - 1:
                    self.DMA("sp", XP[tb % 3][:, :, 3:131], self.xbcT_s[l][:, cols].re("(j p) t -> p j t", p=128))
                else:
                    for j in range(12):
                        self.DMA("sp", XS[:, j, :, 3:11], self.xbcT_s[l][j * 128:(j + 1) * 128, cols].re("p (i t) -> p i t", i=NS))
                    self.DMA("sp", sc_in, self.ssc[l])
                self.DMA("sp", gt[tb % 3], self.g_s[l][cols, 8:24])

            def load_z(tb):
                self.DMA("sp", zz[tb % 2], self.oz_s[l][tb * 128:(tb + 1) * 128, 1024:2048])

            done1b = {}

            def stage1b(tb):
                i = tb % 2
                smp = (tb == NTB - 1)
                sfx = "_s" if smp else "_p"
                TRI, MASKTP, ELAST = K["tri" + sfx], K["masktp" + sfx], K["elast" + sfx]
                G = g16[i]
                self.TT("dve", G["fz"], gt[tb % 3], dtb, ALU.add)
                yield
                self.ACT(G["e1"], G["fz"], AF.Exp)
                self.ACT(G["dt"], G["e1"], AF.Ln, bias=1.0)
                self.ACT(G["lnd"], G["dt"], AF.Ln)
                yield
                self.TT("dve", G["a"], G["dt"], aneg, ALU.mult)
                yield
                self.MM(PS[3][:, 0:16], TRI, G["a"])
                yield
                self.COPY("dve", G["b"], PS[3][:, 0:16])
                yield
                self.MM(PS[3][:, 16:32], ELAST, G["b"])
                self.ACT(G["eb"], G["b"], AF.Exp)
                self.TT("dve", G["nb"], G["lnd"], G["b"], ALU.subtract)
                yield
                self.COPY("dve", G["bl"], PS[3][:, 16:32])
                yield
                self.ACT(G["ebl"], G["bl"], AF.Exp)
                self.TT("dve", G["t1"], G["bl"], G["b"], ALU.subtract)
                yield
                self.ACT(G["wend"], G["t1"], AF.Exp)
                yield
                self.TT("dve", G["wend"], G["wend"], G["dt"], ALU.mult)
                for qd in range(4):
                    Rq = R[qd % 2]
                    ps = PS[4]
                    self.TT("pool", Rq, identf.bc(1, [128, 4, 128]), G["b"][:, qd * 4:(qd + 1) * 4].bc(2, [128, 4, 128]), ALU.mult)
                    yield
                    self.MM(ps, ones, Rq.re("p h t -> p (h t)"), start=True, stop=False, inc=False)
                    self.MM(ps.re("p (h t) -> p h t", h=4), nident, MASKTP.bc(1, [128, 4, 128]), start=False, stop=True)
                    yield
                    for hh_ in range(4):
                        h = qd * 4 + hh_
                        self.ACT(decT[:, h, :], ps[:, hh_ * 128:(hh_ + 1) * 128], AF.Exp, bias=G["nb"][:, h:h + 1])
                    yield
                done1b[tb] = True

            def stage0(tb):
                i = tb % 2
                k3 = tb % 3
                smp = (tb == NTB - 1)
                xp = XP[k3]
                xsb = xsbs[i]
                BTb = BTbs[i]
                if smp:
                    self.COPY("dve", XS[:, :, :, 0:3], sc_in)
                    yield
                def tapv(t):
                    if not smp:
                        return xp[:, :, t:t + 128], V(cw.ap[:, :, t:t + 1].to_broadcast([128, 12, 128]), cw.buf), (lambda a: a)
                    return (XS[:, :, :, t:t + 8], V(cw.ap[:, :, t:t + 1].unsqueeze(3).to_broadcast([128, 12, NS, 8]), cw.buf),
                            (lambda a: a.re("p j (i t) -> p j i t", i=NS)))
                for t in range(4):
                    xin, wbc, view = tapv(t)
                    self.TT("dve" if t % 2 == 0 else "pool", view(ca[t]), xin, wbc, ALU.mult)
                yield
                self.TT("dve", ca[0], ca[0], ca[2], ALU.add)
                self.TT("pool", ca[1], ca[1], ca[3], ALU.add)
                yield
                self.TT("dve", ca[0], ca[0], ca[1], ALU.add)
                yield
                self.TT("dve", u_t, ca[0], cb.bc(2, [128, 12, 128]), ALU.add)
                yield
                if not smp:
                    if tb + 1 < NTB - 1:
                        self.COPY("pool", XP[(tb + 1) % 3][:, :, 0:3], xp[:, :, 128:131])
                    if tb == NTB - 2:
                        self.COPY("dve", sc_t, xp[:, :, 128:131])
                        self.DMA("sp", self.o_psc[l], sc_t)
                else:
                    self.COPY("dve", sc_in, XS[:, :, :, 8:11])
                    self.DMA("sp", self.o_ssc[l], sc_in)
                self.ACT(xbca, u_t, AF.Exp, scale=-1.0)
                yield
                self.ACT(xbca, xbca, AF.Ln, bias=1.0)
                yield
                self.ACT(xbca, xbca, AF.Exp, scale=-1.0)
                yield
                self.TT("dve", xbca, xbca, u_t, ALU.mult)
                yield
                for half in range(2):
                    for j in range(4):
                        self.TR(PS[0][:, j * 128:(j + 1) * 128], xbca[:, half * 4 + j, :], identf, inc=(j == 3))
                    self.COPY("act", xs32[k3][:, half * 512:(half + 1) * 512], PS[0])
                    yield
                for g in range(2):
                    self.TR(PS[2][:, g * 128:(g + 1) * 128], xbca[:, 8 + g, :], identf, inc=(g == 1))
                self.COPY("pool", BTb, xbca[:, 8:10, :])
                self.COPY("pool", CTb[k3], xbca[:, 10:12, :])
                self.COPY("act", xsb, xs32[k3])
                yield
                self.COPY("dve", Btok[k3], PS[2][:, 0:256].re("p (g n) -> p g n", g=2))
                yield

            def stage1j(tb):
                i = tb % 2
                k3 = tb % 3
                smp = (tb == NTB - 1)
                sfx = "_s" if smp else "_p"
                MASK01T = K["mask01t" + sfx]
                G = g16[i]
                xsb = xsbs[i]
                BTb = BTbs[i]
                for g in range(2):
                    self.MM(PS[2][:, 256 + g * 128:256 + (g + 1) * 128], BTb[:, g, :], CTb[k3][:, g, :])
                yield
                self.TT("dve", cbm, PS[2][:, 256:512].re("p (g t) -> p g t", g=2), MASK01T.bc(1, [128, 2, 128]), ALU.mult)
                yield
                while not done1b.get(tb):
                    yield
                for g in range(2):
                    self.TT("pool", mwT[:, g * 8:(g + 1) * 8, :], decT[:, g * 8:(g + 1) * 8, :], cbm[:, g, :].bc(1, [128, 8, 128]), ALU.mult)
                    yield
                self.TT("pool", wxs[i].re("p (h q) -> p h q", h=16), xs32[k3].re("p (h q) -> p h q", h=16),
                        G["wend"].bc(2, [128, 16, 64]), ALU.mult)
                for h in range(16):
                    ps = PS[6 + h // 8]
                    hh_ = h % 8
                    self.MM(ps[:, hh_ * 64:(hh_ + 1) * 64], mwT[:, h, :], xsb[:, h * 64:(h + 1) * 64], inc=(hh_ == 7))
                for g in range(2):
                    self.COPY("act", yin[i][:, g * 512:(g + 1) * 512], PS[6 + g])
                yield

            def stage2(tb):
                i = tb % 2
                k3 = tb % 3
                smp = (tb == NTB - 1)
                G = g16[i]
                z_ = zz[i]
                cols = slice(tb * 128, (tb + 1) * 128)
                pbank = (PS[1], PS[5])
                self.TT("pool", y2.re("p (h q) -> p h q", h=16), xs32[k3].re("p (h q) -> p h q", h=16), dsk.bc(2, [128, 16, 64]), ALU.mult)
                if smp:
                    self.TT("dve", R3, seql.bc(2, [128, NS, 16]), G["ebl"].bc(1, [128, NS, 16]), ALU.mult)
                    self.MM(PS[3][:, 256:512], ones, R3.re("p i h -> p (i h)"))
                    self.COPY("dve", decS, PS[3][:, 256:512].re("p (i h) -> p i h", i=NS))
                for g in range(2):
                    ps = pbank[g]
                    gs = slice(g * 512, (g + 1) * 512)
                    if not smp:
                        self.MM(ps, CTb[k3][:, g, :], STb[:, gs])
                    else:
                        self.TT("dve", CTm, CTb[k3][:, g, :].bc(1, [128, NS, 128]), smtb, ALU.mult)
                        for hf in range(2):
                            self.DMA("sp", Ss32, self.sS[l, g][:, hf * 8:(hf + 1) * 8, :])
                            self.COPY("act", Ssb, Ss32)
                            for s8 in range(8):
                                si = hf * 8 + s8
                                self.MM(ps, CTm[:, si, :], Ssb[:, s8, :], start=(si == 0), stop=(si == NS - 1), inc=(s8 == 7))
                            for qq in range(2):
                                self.TT("dve", wxsm, wxs[i][:, gs].bc(1, [128, 4, 512]),
                                        seqm[:, hf * 8 + qq * 4:hf * 8 + (qq + 1) * 4].bc(2, [128, 4, 512]), ALU.mult)
                                for s4 in range(4):
                                    s8 = qq * 4 + s4
                                    si = hf * 8 + s8
                                    pu = PS[0] if si % 2 == 0 else PS[6]
                                    self.MM(pu, Btok[k3][:, g, :], wxsm[:, s4, :])
                                    sv = Ss32[:, s8, :].re("p (h q) -> p h q", h=8)
                                    self.TT("dve", sv, sv, decS[:, si, g * 8:(g + 1) * 8].bc(2, [128, 8, 64]), ALU.mult)
                                    self.TT("dve", Ss32[:, s8, :], Ss32[:, s8, :], pu, ALU.add)
                            self.DMA("sp", self.o_sS[l, g][:, hf * 8:(hf + 1) * 8, :], Ss32)
                    yield
                    self.TT("dve", y1[:, gs].re("p (h q) -> p h q", h=8), ps.re("p (h q) -> p h q", h=8),
                            G["eb"][:, g * 8:(g + 1) * 8].bc(2, [128, 8, 64]), ALU.mult)
                    yield
                    self.TT("dve", y1[:, gs], y1[:, gs], yin[i][:, gs], ALU.add)
                    yield
                self.TT("dve", y1, y1, y2, ALU.add)
                yield
                self.ACT(y2, z_, AF.Exp, scale=-1.0)
                yield
                self.ACT(y2, y2, AF.Ln, bias=1.0)
                self.TT("dve", y1, y1, z_, ALU.mult)
                yield
                self.ACT(y2, y2, AF.Exp, scale=-1.0)
                yield
                self.TT("dve", y1, y1, y2, ALU.mult)
                yield
                for g in range(2):
                    self.ACT(sq, y1[:, g * 512:(g + 1) * 512], AF.Square, accum=ss[:, g:g + 1])
                yield
                self.ACT(rinv, ss, AF.Ln, scale=1.0 / 512.0, bias=GN_EPS)
                yield
                self.ACT(rinv, rinv, AF.Exp, scale=-0.5)
                yield
                for g in range(2):
                    gs = slice(g * 512, (g + 1) * 512)
                    self.STT("dve", mixs[:, gs], y1[:, gs], rinv[:, g:g + 1], snw[:, gs], ALU.mult, ALU.mult)
                self.DMA("sp", self.mix_s[l][cols, 1024:2048], mixs)
                yield
                if not smp:
                    for g in range(2):
                        gs = slice(g * 512, (g + 1) * 512)
                        ps = pbank[g]
                        self.MM(ps, Btok[k3][:, g, :], wxs[i][:, gs])
                        sv = ST32[:, gs].re("p (h q) -> p h q", h=8)
                        self.TT("dve", sv, sv, G["ebl"][:, g * 8:(g + 1) * 8].bc(2, [128, 8, 64]), ALU.mult)
                        yield
                        self.TT("dve", ST32[:, gs], ST32[:, gs], ps, ALU.add)
                        self.COPY("act", STb[:, gs], ST32[:, gs])
                        yield
                    if tb == NTB - 2:
                        self.DMA("sp", self.o_pS[l], ST32)
                yield

            load(0); load_z(0)
            load(1); load_z(1)
            load(2)
            run_gens([stage1b(0), stage0(0)])
            run_gens([stage1j(0), stage0(1)])
            for tb in range(NTB):
                if tb + 3 < NTB:
                    load(tb + 3)
                gens = [stage2(tb)]
                if tb + 1 < NTB:
                    gens += [stage1b(tb + 1), stage1j(tb + 1)]
                if tb + 2 < NTB:
                    gens += [stage0(tb + 2)]
                run_gens(gens)
                if tb + 2 < NTB:
                    load_z(tb + 2)
            c.barrier()
        self.es = None

    def layernorm(self, pre, out, gam, bet, tmp, bn, mv, rstd, nmr):
        for q in range(4):
            self.c.op("dve", lambda g: g.bn_stats(out=bn.ap[:, q * 6:(q + 1) * 6], in_=pre.ap[:, q * 512:(q + 1) * 512]), [pre.buf], [bn.buf])
        self.c.op("dve", lambda g: g.bn_aggr(out=mv.ap, in_=bn.ap), [bn.buf], [mv.buf])
        self.ACT(rstd, mv[:, 1:2], AF.Sqrt, bias=LN_EPS)
        self.RECIP(rstd, rstd)
        self.STT("dve", nmr, mv[:, 0:1], -1.0, rstd, ALU.mult, ALU.mult)
        self.ACT(tmp, pre, AF.Identity, bias=nmr, scale=rstd)
        self.TT("pool", tmp, tmp, gam, ALU.mult)
        self.TT("dve", out, tmp, bet, ALU.add)

    def phaseD(self, l, src):
        c = self.c
        PS = self.PS
        with ExitStack() as es:
            self.es = es
            identb = self.sb("identb", [128, 128], BF16)
            self.DMA("pool", identb, self.cst[0])
            wo = [self.sb("wo%d" % q, [128, 16, 512], BF16) for q in range(4)]
            wsrc = self.w_out[l].re("(k p) n -> p k n", p=128)
            for q in range(4):
                self.DMA("pool", wo[q], wsrc[:, :, q * 512:(q + 1) * 512])
            gam = self.sb("gam", [128, D]); bet = self.sb("bet", [128, D])
            self.DMA("sp", gam, V(self.ln1g.ap[l:l + 1, :].partition_broadcast(128), self.ln1g.buf))
            self.DMA("sp", bet, V(self.ln1b.ap[l:l + 1, :].partition_broadcast(128), self.ln1b.buf))
            mx = [self.sb("mx%d" % i, [128, 2048], BF16) for i in range(2)]
            xr = [self.sb("xr%d" % i, [128, D]) for i in range(2)]
            mT = [self.sb("mT%d" % i, [128, 16, 128], BF16) for i in range(2)]
            pre = [self.sb("pre%d" % i, [128, D]) for i in range(2)]
            x1 = [self.sb("x1_%d" % i, [128, D]) for i in range(2)]
            tmp = [self.sb("tmp%d" % i, [128, D]) for i in range(2)]
            bn = [self.sb("bn%d" % i, [128, 24]) for i in range(2)]; mv = [self.sb("mv%d" % i, [128, 2]) for i in range(2)]
            rstd = [self.sb("rstd%d" % i, [128, 1]) for i in range(2)]; nmr = [self.sb("nmr%d" % i, [128, 1]) for i in range(2)]

            def load(tb):
                i = tb % 2
                rows = slice(tb * 128, (tb + 1) * 128)
                self.DMA("sp", mx[i], self.mix_s[l][rows, :])
                self.DMA("sp", xr[i], src[rows, :])

            def tre(tb):
                i = tb % 2
                for half in range(2):
                    psb = PS[4 + half].bitcast(BF16)
                    for k in range(8):
                        kk = half * 8 + k
                        self.TR(psb[:, k * 128:(k + 1) * 128], mx[i][:, kk * 128:(kk + 1) * 128], identb, inc=(k == 7))
                    self.evac(mT[i][:, half * 8:(half + 1) * 8, :], psb.re("p (k t) -> p k t", k=8))

            load(0)
            tre(0)
            for tb in range(NTB):
                if tb + 1 < NTB:
                    load(tb + 1)
                i = tb % 2
                rows = slice(tb * 128, (tb + 1) * 128)
                for q in range(4):
                    ps = PS[q]
                    for k in range(16):
                        self.MM(ps, mT[i][:, k, :], wo[q][:, k, :], start=(k == 0), stop=(k == 15), inc=(k == 15))
                    qs = slice(q * 512, (q + 1) * 512)
                    self.STT("dve", pre[i][:, qs], xr[i][:, qs], ALPHA, ps, ALU.mult, ALU.add)
                if tb + 1 < NTB:
                    tre(tb + 1)
                self.layernorm(pre[i], x1[i], gam, bet, tmp[i], bn[i], mv[i], rstd[i], nmr[i])
                self.DMA("sp", self.x1_s[l][rows, :], x1[i])
            c.barrier()
        self.es = None

    def phaseE(self, l):
        c = self.c
        PS = self.PS
        with ExitStack() as es:
            self.es = es
            identb = self.sb("identb", [128, 128], BF16)
            self.DMA("pool", identb, self.cst[0])
            xT = self.sb("xT", [128, 16, NT], BF16)
            self.build_xT(self.x1_s[l], xT, identb)
            wE = [self.sb("wE%d" % i, [128, 16, 2, 128], BF16) for i in range(3)]
            wsrc = self.w_up[l].re("(k p) n -> p k n", p=128)
            fw = self.sb("fw", [128, 86, 3]); fb = self.sb("fb", [128, 86])
            self.DMA("sp", fw, self.fcw[l]); self.DMA("sp", fb, self.fcb[l])
            UP = [self.sb("UP%d" % i, [128, 2 + TP]) for i in range(2)]
            US = [self.sb("US%d" % i, [128, NS, 10]) for i in range(2)]
            for i in range(2):
                self.MEMSET("dve", UP[i][:, 0:2], 0.0)
            fcin = self.sb("fcin", [128, 86, NS, 2])
            self.DMA("sp", fcin, self.sfc[l])
            fco_p = self.sb("fco_p", [128, 86, 2]); fco_s = self.sb("fco_s", [128, 86, NS, 2])
            cv = [self.sb("cv%d" % i, [128, NT]) for i in range(2)]
            sg = self.sb("sg", [128, NT])
            hT = [self.sb("hT%d" % i, [128, NT], BF16) for i in range(2)]

            def loadw(ci):
                w = wE[ci % 3]
                self.DMA("pool", w[:, :, 0, :], wsrc[:, :, ci * 128:(ci + 1) * 128])
                self.DMA("pool", w[:, :, 1, :], wsrc[:, :, DFF + ci * 128:DFF + (ci + 1) * 128])

            loadw(0); loadw(1)
            nps = 0
            for ci in range(NCH):
                if ci + 2 < NCH:
                    loadw(ci + 2)
                w = wE[ci % 3]
                for gv in range(2):
                    ch = gv * NCH + ci
                    up, us = UP[gv], US[gv]
                    self.COPY("pool", us[:, :, 0:2], fcin[:, ch, :, :])
                    for (t0, tn) in TGS:
                        ps = PS[nps % 8]; nps += 1
                        for k in range(16):
                            self.MM(ps[:, 0:tn], w[:, k, gv, :], xT[:, k, t0:t0 + tn], start=(k == 0), stop=(k == 15), inc=(k == 15))
                        if t0 < TP:
                            self.evac(up[:, 2 + t0:2 + t0 + tn], ps[:, 0:tn])
                        else:
                            self.evac(us[:, :, 2:10], ps[:, 0:128].re("p (i t) -> p i t", i=NS))
                    self.COPY("pool", fco_p[:, ch, :], up[:, TP:TP + 2])
                    self.COPY("pool", fco_s[:, ch, :, :], us[:, :, 8:10])
                    cvp = cv[gv][:, 0:TP]
                    cvs = cv[gv][:, TP:NT].re("p (i t) -> p i t", i=NS)
                    e = "dve"
                    self.TS(e, cvp, up[:, 0:TP], fw[:, ch, 0:1], ALU.mult, fb[:, ch:ch + 1], ALU.add)
                    self.TS(e, cvs, us[:, :, 0:8], fw[:, ch, 0:1], ALU.mult, fb[:, ch:ch + 1], ALU.add)
                    for t in range(1, 3):
                        self.STT(e, cvp, up[:, t:t + TP], fw[:, ch, t:t + 1], cvp, ALU.mult, ALU.add)
                        self.STT(e, cvs, us[:, :, t:t + 8], fw[:, ch, t:t + 1], cvs, ALU.mult, ALU.add)
                self.ACT(sg, cv[0], AF.Silu)
                h = hT[ci % 2]
                self.TT("dve", h, sg, cv[1], ALU.mult)
                self.DMA("sp", self.hT_s[l][:, :, ci, :].re("b p t -> p b t"), h.re("p (b t) -> p b t", b=NTB))
            self.DMA("sp", self.o_pfc[l], fco_p)
            self.DMA("sp", self.o_sfc[l], fco_s)
            c.barrier()
        self.es = None

    def phaseF(self, l, dst):
        c = self.c
        PS = self.PS
        with ExitStack() as es:
            self.es = es
            KR = [(0, 11), (11, 22), (22, 33), (33, NCH)]
            wd = [[self.sb("wd%d_%d" % (i, r), [128, b - a, 512], BF16) for r, (a, b) in enumerate(KR)] for i in range(2)]
            wsrc = self.w_down[l].re("(k p) n -> p k n", p=128)
            hb = [self.sb("hb%d" % i, [128, NCH, 128], BF16) for i in range(3)]
            xq = [self.sb("xq%d" % i, [128, 512]) for i in range(3)]
            po = [self.sb("po%d" % i, [128, 512]) for i in range(3)]
            gam = self.sb("gam", [128, D]); bet = self.sb("bet", [128, D])
            self.DMA("sp", gam, V(self.ln2g.ap[l:l + 1, :].partition_broadcast(128), self.ln2g.buf))
            self.DMA("sp", bet, V(self.ln2b.ap[l:l + 1, :].partition_broadcast(128), self.ln2b.buf))
            pre = [self.sb("pre%d" % i, [128, D]) for i in range(3)]
            x2 = [self.sb("x2_%d" % i, [128, D]) for i in range(2)]
            tmp = self.sb("tmp", [128, D])
            bn = self.sb("bn", [128, 24]); mv = self.sb("mv", [128, 2]); rstd = self.sb("rstd", [128, 1]); nmr = self.sb("nmr", [128, 1])

            def loadw(q):
                for r, (a, b) in enumerate(KR):
                    self.DMA("pool", wd[q % 2][r], wsrc[:, a:b, q * 512:(q + 1) * 512])

            def load(n):
                q, tb = divmod(n, NTB)
                rows = slice(tb * 128, (tb + 1) * 128)
                self.DMA("sp", hb[n % 3], self.hT_s[l][tb])
                self.DMA("sp", xq[n % 3], self.x1_s[l][rows, q * 512:(q + 1) * 512])
                if q == 3:
                    self.DMA("sp", pre[n % 3][:, 0:1536], self.pre_s[l][rows, 0:1536])

            loadw(0)
            load(0); load(1)
            for q in range(4):
                if q + 1 < 4:
                    loadw(q + 1)
                for tb in range(NTB):
                    n = q * NTB + tb
                    if n + 2 < 4 * NTB:
                        if (n + 2) // NTB == 3 and (n + 2) % NTB == 0:
                            for tok in c.dtok:
                                c._wait("sp", tok)
                        load(n + 2)
                    ps = PS[n % 4]
                    for k in range(NCH):
                        r = min(k // 11, 3)
                        self.MM(ps, hb[n % 3][:, k, :], wd[q % 2][r][:, k - KR[r][0], :], start=(k == 0), stop=(k == NCH - 1), inc=(k == NCH - 1))
                    rows = slice(tb * 128, (tb + 1) * 128)
                    if q < 3:
                        self.STT("dve", po[n % 3], xq[n % 3], ALPHA, ps, ALU.mult, ALU.add)
                        self.DMA("sp", self.pre_s[l][rows, q * 512:(q + 1) * 512], po[n % 3])
                    else:
                        i = tb % 2
                        self.STT("dve", pre[n % 3][:, 1536:2048], xq[n % 3], ALPHA, ps, ALU.mult, ALU.add)
                        self.layernorm(pre[n % 3], x2[i], gam, bet, tmp, bn, mv, rstd, nmr)
                        self.DMA("sp", dst[rows, :], x2[i])
            c.barrier()
        self.es = None

    def build(self):
        self.declare()
        order = "ABCDEF"
        for l in range(2):
            src = self.x_in if l == 0 else self.x2_s
            dst = self.x2_s if l == 0 else self.y
            for ph in order:
                tag = "%d%s" % (l, ph)
                if tag > self.upto:
                    break
                if ph == "A":
                    self.phaseA(l, src)
                elif ph == "B":
                    self.phaseB(l)
                elif ph == "C":
                    self.phaseC(l)
                elif ph == "D":
                    self.phaseD(l, src)
                elif ph == "E":
                    self.phaseE(l)
                elif ph == "F":
                    self.phaseF(l, dst)
        self.c.finish()
        return self.nc


def make_consts():
    t = np.arange(128)
    cst = np.zeros((12, 128, 128), np.float32)
    cst[0] = np.eye(128)
    cst[1] = 1.0
    for kind, seq in ((0, np.zeros(128, np.int64)), (1, t // LS)):
        same = seq[:, None] == seq[None, :]
        le = t[:, None] <= t[None, :]
        allowed_st = same & le
        last = np.array([np.max(np.where(seq == seq[m])[0]) for m in range(128)])
        o = 2 + kind * 5
        cst[o + 0] = allowed_st.astype(np.float32)
        cst[o + 1] = np.where(allowed_st.T, 0.0, -1e30)
        cst[o + 2] = np.where(allowed_st, 0.0, 3e4)
        cst[o + 3] = allowed_st.astype(np.float32)
        cst[o + 4] = (t[:, None] == last[None, :]).astype(np.float32)
    smt = (np.arange(NS)[:, None] == (t // LS)[None, :]).astype(np.float32)
    seqm = np.zeros((2, 128, NS), np.float32)
    seqm[0] = smt.T
    seqm[1] = smt.T * ((t % LS) == LS - 1)[:, None]
    return cst, smt, seqm


_PROG_CACHE = {}


def _get_prog(debug=False, upto="Z"):
    key = (debug, upto)
    if key not in _PROG_CACHE:
        _PROG_CACHE[key] = Prog(debug=debug, upto=upto).build()
    return _PROG_CACHE[key]


def make_in_maps(inp):
    f = lambda a: np.ascontiguousarray(a, dtype=np.float32)
    cst, smt, seqm = make_consts()
    shared = {
        "w_in": f(inp["w_in"]), "w_out": f(inp["w_out"]), "w_up": f(inp["ffn_w_up"]), "w_down": f(inp["ffn_w_down"]),
        "b_i": f(inp["mlstm_b_i"]), "b_f": f(inp["mlstm_b_f"]), "mnw": f(inp["mlstm_norm_w"]), "snw": f(inp["ssm_norm_w"]),
        "dtb": f(inp["ssm_dt_bias"]), "alog": f(inp["ssm_A_log"]), "dsk": f(inp["ssm_D"]),
        "ln1g": f(inp["ln1_g"]), "ln1b": f(inp["ln1_b"]), "ln2g": f(inp["ln2_g"]), "ln2b": f(inp["ln2_b"]),
        "scw": f(inp["ssm_conv_w"].reshape(2, 4, 12, 128).transpose(0, 3, 2, 1)),
        "scb": f(inp["ssm_conv_b"].reshape(2, 12, 128).transpose(0, 2, 1)),
        "scb_row": f(inp["ssm_conv_b"]),
        "fcw": f(inp["ffn_conv_w"].reshape(2, 3, 86, 128).transpose(0, 3, 2, 1)),
        "fcb": f(inp["ffn_conv_b"].reshape(2, 86, 128).transpose(0, 2, 1)),
        "cst": cst, "smt": smt, "seqm": seqm,
    }
    maps = []
    for ci in range(NCORES):
        sl = slice(ci * NS, (ci + 1) * NS)
        xs = inp["x_sample"][sl].reshape(NS * LS, D)
        m = dict(shared)
        m["x_in"] = f(np.concatenate([inp["x_prompt"][ci % 4], xs], axis=0))
        m["sC"] = f(inp["state_mlstm_C"][:, sl])
        m["snT"] = f(inp["state_mlstm_n"][:, sl].reshape(2, NS, 4, 2, 128).transpose(0, 2, 4, 1, 3))
        m["smrep"] = f(np.repeat(inp["state_mlstm_m"][:, sl], LS, axis=1))
        m["sS"] = f(inp["state_ssm"][:, sl].reshape(2, NS, 2, 8, 64, 128).transpose(0, 2, 5, 1, 3, 4).reshape(2, 2, 128, NS, 512))
        m["ssc"] = f(inp["state_ssm_conv"][:, sl].reshape(2, NS, 3, 12, 128).transpose(0, 4, 3, 1, 2))
        m["sfc"] = f(inp["state_ffn_conv"][:, sl].reshape(2, NS, 2, 86, 128).transpose(0, 4, 3, 1, 2))
        maps.append(m)
    return maps


def assemble(res):
    L = 2
    y_prompt = np.stack([res[ci]["y"][:TP] for ci in range(4)], 0)
    y_sample = np.concatenate([res[ci]["y"][TP:].reshape(NS, LS, D) for ci in range(NCORES)], 0)
    p_C = np.stack([res[ci]["o_pC"] for ci in range(4)], 1)
    p_n = np.stack([res[ci]["o_pnT"].transpose(0, 2, 3, 1).reshape(L, 4, 256) for ci in range(4)], 1)
    p_m = np.stack([res[ci]["o_pm"] for ci in range(4)], 1)
    p_S = np.stack([res[ci]["o_pS"].reshape(L, 128, 16, 64).transpose(0, 2, 3, 1) for ci in range(4)], 1)
    p_sc = np.stack([res[ci]["o_psc"].transpose(0, 3, 2, 1).reshape(L, 3, 1536) for ci in range(4)], 1)
    p_fc = np.stack([res[ci]["o_pfc"].transpose(0, 3, 2, 1).reshape(L, 2, 11008) for ci in range(4)], 1)
    s_C = np.concatenate([res[ci]["o_sC"] for ci in range(NCORES)], 1)
    s_n = np.concatenate([res[ci]["o_snT"].transpose(0, 3, 1, 4, 2).reshape(L, NS, 4, 256) for ci in range(NCORES)], 1)
    s_m = np.concatenate([res[ci]["o_sm"] for ci in range(NCORES)], 1)
    s_S = np.concatenate([res[ci]["o_sS"].reshape(L, 2, 128, NS, 8, 64).transpose(0, 3, 1, 4, 5, 2).reshape(L, NS, 16, 64, 128)
                          for ci in range(NCORES)], 1)
    s_sc = np.concatenate([res[ci]["o_ssc"].transpose(0, 3, 4, 2, 1).reshape(L, NS, 3, 1536) for ci in range(NCORES)], 1)
    s_fc = np.concatenate([res[ci]["o_sfc"].transpose(0, 3, 4, 2, 1).reshape(L, NS, 2, 11008) for ci in range(NCORES)], 1)
    outs = (y_prompt, y_sample, p_C, p_n, p_m, p_S, p_sc, p_fc, s_C, s_n, s_m, s_S, s_sc, s_fc)
    return tuple(np.ascontiguousarray(o, dtype=np.float32) for o in outs)


def kernel(**inputs):
    inp = {k: np.asarray(v) for k, v in inputs.items()}
    nc = _get_prog()
    maps = make_in_maps(inp)
    res = run_bass_kernel_spmd(nc, maps, core_ids=list(range(NCORES)))
    return assemble(res.results)
```

```python
import numpy as np
from contextlib import ExitStack
import concourse.bass as bass
import concourse.mybir as mybir
from concourse.bass_utils import run_bass_kernel_spmd

F32 = mybir.dt.float32
BF16 = mybir.dt.bfloat16
AF = mybir.ActivationFunctionType
ALU = mybir.AluOpType
AX = mybir.AxisListType

NCORES = 8
D = 2048
TP = 2048
NS = 16
LS = 8
NT = TP + NS * LS
NTB = NT // 128
DFF = 5504
NCH = DFF // 128
IN_DIM = 6680
ALPHA = 4 ** 0.25
LN_EPS = 1e-5
GN_EPS = 1e-6
TGS = [(0, 512), (512, 512), (1024, 512), (1536, 512), (2048, 128)]


class Buf:
    __slots__ = ("w", "r", "name", "track")

    def __init__(self, name="", track=True):
        self.w = None
        self.r = {}
        self.name = name
        self.track = track


class V:
    __slots__ = ("ap", "buf")

    def __init__(self, ap, buf=None):
        self.ap = ap
        self.buf = buf if buf is not None else Buf()

    def __getitem__(self, k):
        return V(self.ap[k], self.buf)

    def re(self, pat, **kw):
        return V(self.ap.rearrange(pat, **kw), self.buf)

    def bc(self, axis, shape):
        return V(self.ap.unsqueeze(axis).to_broadcast(list(shape)), self.buf)

    def bitcast(self, dt):
        return V(self.ap.bitcast(dt), self.buf)

    def sub(self, ap):
        return V(ap, self.buf)

    def newbuf(self):
        return V(self.ap, Buf())


class Ctx:
    ENG = ("pe", "act", "dve", "pool", "sp")

    def __init__(self, nc, n_dma_sems=48):
        self.nc = nc
        self.eng = {"pe": nc.tensor, "act": nc.scalar, "dve": nc.vector, "pool": nc.gpsimd, "sp": nc.sync}
        self.sem = {e: nc.alloc_semaphore("s_" + e) for e in self.ENG}
        self.cnt = {e: 0 for e in self.ENG}
        self.seen = {e: {} for e in self.ENG}
        self.dsem = [nc.alloc_semaphore("d%d" % i) for i in range(n_dma_sems)]
        self.dval = [0] * n_dma_sems
        self.dtok = [None] * n_dma_sems
        self.dnext = 0
        self.dnext_sw = 0
        self.n_inst = 0
        self.n_wait = 0

    def _wait(self, e, tok):
        if tok is None:
            return
        key, sem, val, snap = tok
        if key == e and e == "pe":
            return
        seen = self.seen[e]
        if seen.get(key, 0) >= val:
            return
        self.eng[e].wait_ge(sem, val)
        self.n_wait += 1
        seen[key] = val
        for k, v in snap.items():
            if seen.get(k, 0) < v:
                seen[k] = v

    def _deps(self, e, reads, writes):
        for b in reads:
            if b.track:
                self._wait(e, b.w)
        for b in writes:
            if b.track:
                self._wait(e, b.w)
                for tok in b.r.values():
                    self._wait(e, tok)

    def op(self, e, fn, reads=(), writes=(), inc=True):
        self._deps(e, reads, writes)
        ins = fn(self.eng[e])
        self.n_inst += 1
        val = self.cnt[e] + 1
        if inc:
            ins.then_inc(self.sem[e], 1)
            self.cnt[e] = val
        tok = (e, self.sem[e], val, dict(self.seen[e]))
        for b in reads:
            if b.track:
                b.r[e] = tok
        for b in writes:
            if b.track:
                b.w = tok
                b.r = {}
        return tok

    def dma(self, q, out, in_, **kw):
        reads, writes = [in_.buf], [out.buf]
        self._deps(q, reads, writes)
        nsw = 12
        if q == "pool":
            j = self.dnext_sw
            self.dnext_sw = (j + 1) % nsw
        else:
            j = nsw + self.dnext
            self.dnext = (self.dnext + 1) % (len(self.dsem) - nsw)
        self._wait(q, self.dtok[j])
        ins = self.eng[q].dma_start(out=out.ap, in_=in_.ap, **kw)
        self.n_inst += 1
        self.dval[j] += 16
        ins.then_inc(self.dsem[j], 16)
        tok = ("d%d" % j, self.dsem[j], self.dval[j], dict(self.seen[q]))
        self.dtok[j] = tok
        for b in reads:
            if b.track:
                b.r["dma%d" % j] = tok
        for b in writes:
            if b.track:
                b.w = tok
                b.r = {}
        return tok

    def barrier(self):
        for e in self.ENG:
            for e2 in self.ENG:
                if e2 != e and self.cnt[e2] > 0:
                    self._wait(e, (e2, self.sem[e2], self.cnt[e2], {}))
            for tok in self.dtok:
                self._wait(e, tok)

    def finish(self):
        for tok in self.dtok:
            self._wait("sp", tok)
        for e2 in self.ENG:
            if e2 != "sp" and self.cnt[e2] > 0:
                self._wait("sp", (e2, self.sem[e2], self.cnt[e2], {}))


def run_gens(gens):
    gens = list(gens)
    while gens:
        for g in list(gens):
            try:
                next(g)
            except StopIteration:
                gens.remove(g)


def rr_gen(gens):
    gens = list(gens)
    while gens:
        for g in list(gens):
            try:
                next(g)
            except StopIteration:
                gens.remove(g)
        yield


class Prog:
    def __init__(self, debug=False, upto="Z"):
        self.debug = debug
        self.upto = upto
        nc = self.nc = bass.Bass("TRN2", target_bir_lowering=False)
        self.c = Ctx(nc)
        self.es = None
        self.PS = [V(nc.alloc_psum_tensor("ps%d" % i, [128, 512], F32).ap()) for i in range(8)]
        self.rr = 0
        self.nsb = 0

    def din(self, name, shape, dt=F32):
        return V(self.nc.dram_tensor(name, list(shape), dt, kind="ExternalInput").ap(), Buf(name, track=False))

    def dout(self, name, shape, dt=F32):
        return V(self.nc.dram_tensor(name, list(shape), dt, kind="ExternalOutput").ap(), Buf(name, track=False))

    def dscr(self, name, shape, dt=F32):
        kind = "ExternalOutput" if self.debug else "Internal"
        return V(self.nc.dram_tensor(name, list(shape), dt, kind=kind).ap(), Buf(name, track=False))

    def sb(self, name, shape, dt=F32):
        self.nsb += 1
        t = self.es.enter_context(self.nc.sbuf_tensor("%s_%d" % (name, self.nsb), list(shape), dt))
        return V(t.ap())

    def TT(self, e, out, a, b, op):
        self.c.op(e, lambda g: g.tensor_tensor(out=out.ap, in0=a.ap, in1=b.ap, op=op), [a.buf, b.buf], [out.buf])

    def TS(self, e, out, a, s1, op0, s2=None, op1=None):
        rd = [a.buf]
        s1a, s2a = s1, s2
        if isinstance(s1, V):
            rd.append(s1.buf); s1a = s1.ap
        if isinstance(s2, V):
            rd.append(s2.buf); s2a = s2.ap
        if op1 is None:
            self.c.op(e, lambda g: g.tensor_scalar(out=out.ap, in0=a.ap, scalar1=s1a, scalar2=None, op0=op0), rd, [out.buf])
        else:
            self.c.op(e, lambda g: g.tensor_scalar(out=out.ap, in0=a.ap, scalar1=s1a, scalar2=s2a, op0=op0, op1=op1), rd, [out.buf])

    def STT(self, e, out, a, s, b, op0, op1):
        rd = [a.buf, b.buf]
        sa = s
        if isinstance(s, V):
            rd.append(s.buf); sa = s.ap
        self.c.op(e, lambda g: g.scalar_tensor_tensor(out=out.ap, in0=a.ap, scalar=sa, in1=b.ap, op0=op0, op1=op1), rd, [out.buf])

    def ACT(self, out, a, func, bias=None, scale=None, accum=None):
        rd = [a.buf]
        wr = [out.buf]
        kw = {}
        if bias is not None:
            if isinstance(bias, V):
                rd.append(bias.buf); kw["bias"] = bias.ap
            else:
                kw["bias"] = float(bias)
        if scale is not None:
            if isinstance(scale, V):
                rd.append(scale.buf); kw["scale"] = scale.ap
            else:
                kw["scale"] = float(scale)
        if accum is not None:
            wr.append(accum.buf); kw["accum_out"] = accum.ap
        self.c.op("act", lambda g: g.activation(out=out.ap, in_=a.ap, func=func, **kw), rd, wr)

    def COPY(self, e, out, a):
        if e == "act":
            self.c.op("act", lambda g: g.copy(out=out.ap, in_=a.ap), [a.buf], [out.buf])
        else:
            self.c.op(e, lambda g: g.tensor_copy(out=out.ap, in_=a.ap), [a.buf], [out.buf])

    def MEMSET(self, e, out, val):
        self.c.op(e, lambda g: g.memset(out.ap, val), [], [out.buf])

    def MM(self, out, lhsT, rhs, start=True, stop=True, inc=True):
        self.c.op("pe", lambda g: g.matmul(out.ap, lhsT.ap, rhs.ap, start=start, stop=stop), [lhsT.buf, rhs.buf], [out.buf], inc=inc)

    def TR(self, out, a, ident, inc=True):
        self.c.op("pe", lambda g: g.transpose(out.ap, a.ap, ident.ap), [a.buf, ident.buf], [out.buf], inc=inc)

    def DMA(self, q, out, a, **kw):
        self.c.dma(q, out, a, **kw)

    def RECIP(self, out, a):
        self.c.op("dve", lambda g: g.reciprocal(out=out.ap, in_=a.ap), [a.buf], [out.buf])

    def REDMAX(self, out, a):
        self.c.op("dve", lambda g: g.tensor_reduce(out=out.ap, in_=a.ap, axis=AX.X, op=ALU.max), [a.buf], [out.buf])

    def evac(self, out, a, scale=None):
        self.rr += 1
        if self.rr % 2 == 0:
            if scale is None:
                self.COPY("act", out, a)
            else:
                self.c.op("act", lambda g: g.mul(out=out.ap, in_=a.ap, mul=float(scale)), [a.buf], [out.buf])
        else:
            if scale is None:
                self.COPY("dve", out, a)
            else:
                self.TS("dve", out, a, float(scale), ALU.mult)

    def declare(self):
        L = 2
        self.x_in = self.din("x_in", [NT, D])
        self.w_in = self.din("w_in", [L, D, IN_DIM])
        self.w_out = self.din("w_out", [L, D, D])
        self.w_up = self.din("w_up", [L, D, 2 * DFF])
        self.w_down = self.din("w_down", [L, DFF, D])
        self.b_i = self.din("b_i", [L, 4]); self.b_f = self.din("b_f", [L, 4])
        self.mnw = self.din("mnw", [L, 1024]); self.snw = self.din("snw", [L, 1024])
        self.dtb = self.din("dtb", [L, 16]); self.alog = self.din("alog", [L, 16]); self.dsk = self.din("dsk", [L, 16])
        self.ln1g = self.din("ln1g", [L, D]); self.ln1b = self.din("ln1b", [L, D])
        self.ln2g = self.din("ln2g", [L, D]); self.ln2b = self.din("ln2b", [L, D])
        self.scw = self.din("scw", [L, 128, 12, 4]); self.scb = self.din("scb", [L, 128, 12])
        self.scb_row = self.din("scb_row", [L, 1536])
        self.fcw = self.din("fcw", [L, 128, 86, 3]); self.fcb = self.din("fcb", [L, 128, 86])
        self.sC = self.din("sC", [L, NS, 4, 256, 256])
        self.snT = self.din("snT", [L, 4, 128, NS, 2])
        self.smrep = self.din("smrep", [L, 128, 4])
        self.sS = self.din("sS", [L, 2, 128, NS, 512])
        self.ssc = self.din("ssc", [L, 128, 12, NS, 3])
        self.sfc = self.din("sfc", [L, 128, 86, NS, 2])
        self.cst = self.din("cst", [12, 128, 128])
        self.smt = self.din("smt", [NS, 128])
        self.seqm = self.din("seqm", [2, 128, NS])
        self.y = self.dout("y", [NT, D])
        self.o_pC = self.dout("o_pC", [L, 4, 256, 256]); self.o_pnT = self.dout("o_pnT", [L, 128, 4, 2])
        self.o_pm = self.dout("o_pm", [L, 4]); self.o_pS = self.dout("o_pS", [L, 128, 1024])
        self.o_psc = self.dout("o_psc", [L, 128, 12, 3]); self.o_pfc = self.dout("o_pfc", [L, 128, 86, 2])
        self.o_sC = self.dout("o_sC", [L, NS, 4, 256, 256]); self.o_snT = self.dout("o_snT", [L, 4, 128, NS, 2])
        self.o_sm = self.dout("o_sm", [L, NS, 4]); self.o_sS = self.dout("o_sS", [L, 2, 128, NS, 512])
        self.o_ssc = self.dout("o_ssc", [L, 128, 12, NS, 3]); self.o_sfc = self.dout("o_sfc", [L, 128, 86, NS, 2])
        self.qT_s = [self.dscr("qT_s%d" % l, [1024, NT], BF16) for l in range(L)]
        self.kT_s = [self.dscr("kT_s%d" % l, [1024, NT], BF16) for l in range(L)]
        self.kv_s = [self.dscr("kv_s%d" % l, [NT, 2048], BF16) for l in range(L)]
        self.oz_s = [self.dscr("oz_s%d" % l, [NT, 2048]) for l in range(L)]
        self.g_s = [self.dscr("g_s%d" % l, [NT, 24]) for l in range(L)]
        self.xbcT_s = [self.dscr("xbcT_s%d" % l, [1536, NT]) for l in range(L)]
        self.mix_s = [self.dscr("mix_s%d" % l, [NT, 2048], BF16) for l in range(L)]
        self.x1_s = [self.dscr("x1_s%d" % l, [NT, D]) for l in range(L)]
        self.hT_s = [self.dscr("hT_s%d" % l, [NTB, 128, NCH, 128], BF16) for l in range(L)]
        self.pre_s = [self.dscr("pre_s%d" % l, [NT, D]) for l in range(L)]
        self.x2_s = self.dscr("x2_s", [NT, D])

    def load_consts(self, kinds=True):
        K = {}
        names = ["identf", "ones", "tri_p", "maskn_p", "masktp_p", "mask01t_p", "elast_p",
                 "tri_s", "maskn_s", "masktp_s", "mask01t_s", "elast_s"]
        for i, n in enumerate(names):
            t = self.sb("k_" + n, [128, 128])
            self.DMA("sp", t, self.cst[i])
            K[n] = t
        idb = self.sb("k_identb", [128, 128], BF16)
        self.DMA("pool", idb, self.cst[0])
        K["identb"] = idb
        return K

    def build_xT(self, src, xT, identb):
        NXB = 4
        xb = [self.sb("xb%d" % i, [128, D], BF16) for i in range(NXB)]
        for tb in range(min(NXB - 1, NTB)):
            self.DMA("pool", xb[tb % NXB], src[tb * 128:(tb + 1) * 128, :])
        for tb in range(NTB):
            b = xb[tb % NXB]
            nx = tb + NXB - 1
            if nx < NTB:
                self.DMA("pool", xb[nx % NXB], src[nx * 128:(nx + 1) * 128, :])
            for half in range(2):
                ps = self.PS[4 + (tb * 2 + half) % 4]
                psb = ps.bitcast(BF16)
                for k in range(8):
                    kk = half * 8 + k
                    self.TR(psb[:, k * 128:(k + 1) * 128], b[:, kk * 128:(kk + 1) * 128], identb, inc=(k == 7))
                self.evac(xT[:, half * 8:(half + 1) * 8, tb * 128:(tb + 1) * 128], psb.re("p (k t) -> p k t", k=8))

    def phaseA(self, l, src):
        c = self.c
        with ExitStack() as es:
            self.es = es
            identb = self.sb("identb", [128, 128], BF16)
            self.DMA("pool", identb, self.cst[0])
            xT = self.sb("xT", [128, 16, NT], BF16)
            wb = [self.sb("wA%d" % i, [128, 16, 512], BF16) for i in range(2)]
            st32 = [self.sb("st32_%d" % i, [128, 512]) for i in range(4)]
            st16 = [self.sb("st16_%d" % i, [128, 512], BF16) for i in range(4)]
            fm32 = [self.sb("fm32_%d" % i, [128, NT]) for i in range(2)]
            fm16 = [self.sb("fm16_%d" % i, [128, NT], BF16) for i in range(2)]
            wsrc = self.w_in[l].re("(k p) n -> p k n", p=128)
            jobs = [(0, 512, "q"), (512, 512, "q"), (1024, 512, "k"), (1536, 512, "k"),
                    (2048, 512, "v"), (2560, 512, "v"), (3072, 512, "o"), (3584, 512, "o"),
                    (4104, 512, "z"), (4616, 512, "z"), (5128, 512, "x"), (5640, 512, "x"), (6152, 512, "x"),
                    (-1, 24, "g")]

            def loadw(j):
                c0, ncol, mode = jobs[j]
                w = wb[j % 2]
                if mode == "g":
                    self.DMA("pool", w[:, :, 0:8], wsrc[:, :, 4096:4104])
                    self.DMA("pool", w[:, :, 8:24], wsrc[:, :, 6664:6680])
                else:
                    self.DMA("pool", w, wsrc[:, :, c0:c0 + ncol])

            cnt = {"q": 0, "k": 0, "v": 0, "o": 0, "z": 0, "x": 0}
            nps = 0
            nst = 0
            nfm = 0
            loadw(0)
            self.build_xT(src, xT, identb)
            for j in range(len(jobs)):
                if j + 1 < len(jobs):
                    loadw(j + 1)
                c0, ncol, mode = jobs[j]
                w = wb[j % 2]
                if mode in ("q", "x"):
                    for cc in range(4):
                        if mode == "x":
                            stg = fm32[nfm % 2]
                        else:
                            stg = fm16[nfm % 2]
                        nfm += 1
                        for (t0, tn) in TGS:
                            ps = self.PS[nps % 4]; nps += 1
                            for k in range(16):
                                self.MM(ps[:, 0:tn], w[:, k, cc * 128:(cc + 1) * 128], xT[:, k, t0:t0 + tn],
                                        start=(k == 0), stop=(k == 15), inc=(k == 15))
                            self.evac(stg[:, t0:t0 + tn], ps[:, 0:tn])
                        ch = cnt[mode] * 4 + cc
                        dst = {"q": self.qT_s, "x": self.xbcT_s}[mode][l]
                        self.DMA("sp", dst[ch * 128:(ch + 1) * 128, :], stg)
                if mode in ("k", "v", "o", "z", "g"):
                    for tb in range(NTB):
                        ps = self.PS[nps % 4]; nps += 1
                        for k in range(16):
                            self.MM(ps[:, 0:ncol], xT[:, k, tb * 128:(tb + 1) * 128], w[:, k, 0:ncol],
                                    start=(k == 0), stop=(k == 15), inc=(k == 15))
                        rows = slice(tb * 128, (tb + 1) * 128)
                        if mode in ("k", "v"):
                            stg = st16[nst % 4]; nst += 1
                            self.evac(stg, ps, scale=(0.0625 if mode == "k" else None))
                            cb = (0 if mode == "k" else 1024) + cnt[mode] * 512
                            self.DMA("sp", self.kv_s[l][rows, cb:cb + 512], stg)
                        elif mode in ("o", "z"):
                            stg = st32[nst % 4]; nst += 1
                            self.evac(stg, ps)
                            cb = (0 if mode == "o" else 1024) + cnt[mode] * 512
                            self.DMA("sp", self.oz_s[l][rows, cb:cb + 512], stg)
                        else:
                            stg = st32[nst % 4]; nst += 1
                            self.evac(stg[:, 0:24], ps[:, 0:24])
                            self.DMA("sp", self.g_s[l][rows, :], stg[:, 0:24])
                if mode in cnt:
                    cnt[mode] += 1
            c.barrier()
        self.es = None

    def phaseB(self, l):
        c = self.c
        PS = self.PS
        with ExitStack() as es:
            self.es = es
            K = self.load_consts()
            identf, ones, identb = K["identf"], K["ones"], K["identb"]
            bi = self.sb("bi", [128, 4]); bf = self.sb("bf", [128, 4])
            self.DMA("sp", bi, V(self.b_i.ap[l:l + 1, :].partition_broadcast(128), self.b_i.buf))
            self.DMA("sp", bf, V(self.b_f.ap[l:l + 1, :].partition_broadcast(128), self.b_f.buf))
            mnw = self.sb("mnw", [128, 1024])
            self.DMA("sp", mnw, V(self.mnw.ap[l:l + 1, :].partition_broadcast(128), self.mnw.buf))
            smt = self.sb("smt", [128, NS, 128], BF16)
            self.DMA("pool", smt, V(self.smt.ap.unsqueeze(0).to_broadcast([128, NS, 128]), self.smt.buf))
            seqm = self.sb("seqm", [128, NS]); seql = self.sb("seql", [128, NS])
            self.DMA("sp", seqm, self.seqm[0]); self.DMA("sp", seql, self.seqm[1])
            C32 = self.sb("C32", [128, 4, 2, 257]); Cbf = self.sb("Cbf", [128, 4, 2, 257], BF16)
            self.MEMSET("dve", C32, 0.0); self.MEMSET("pool", Cbf, 0.0)
            mprev = self.sb("mprev", [128, 4]); self.MEMSET("dve", mprev, 0.0)
            vaug = self.sb("vaug", [128, 4, 257], BF16); self.MEMSET("pool", vaug, 1.0)
            qT = [self.sb("qT%d" % i, [128, 8, 128], BF16) for i in range(2)]
            kT = [self.sb("kT%d" % i, [128, 8, 128], BF16) for i in range(2)]
            kv = [self.sb("kv%d" % i, [128, 2048], BF16) for i in range(2)]
            oo = [self.sb("oo%d" % i, [128, 1024]) for i in range(2)]
            gt = [self.sb("gt%d" % i, [128, 8]) for i in range(2)]
            gnames = ["ig", "fz", "e1", "sp", "b", "a", "cm", "g", "t1", "wint", "enm", "mend", "t2", "wend", "dec", "t3"]
            g4 = [{n: self.sb("g4_%s%d" % (n, i), [128, 4]) for n in gnames} for i in range(2)]
            gb8 = [self.sb("gb8_%d" % i, [128, 8]) for i in range(2)]
            glb = [self.sb("glb_%d" % i, [128, 8]) for i in range(2)]
            DT = [self.sb("DT%d" % i, [128, 4, 128]) for i in range(2)]
            R = self.sb("R", [128, 4, 128]); tmpA = self.sb("tmpA", [128, 4, 128])
            PT = self.sb("PT", [128, 4, 128], BF16)
            wv = self.sb("wv", [128, 4, 257], BF16)
            tmpI = [self.sb("tmpI%d" % i, [128, 257]) for i in range(2)]
            comb = [self.sb("comb%d" % i, [128, 257]) for i in range(2)]
            dd = [self.sb("dd%d" % i, [128, 1]) for i in range(2)]
            rr_ = [self.sb("rr%d" % i, [128, 1]) for i in range(2)]
            bn = [self.sb("bn%d" % i, [128, 6]) for i in range(2)]
            mv = [self.sb("mv%d" % i, [128, 2]) for i in range(2)]
            rstd = [self.sb("rstd%d" % i, [128, 1]) for i in range(2)]
            hh = [self.sb("hh%d" % i, [128, 256]) for i in range(2)]
            hn = self.sb("hn", [128, 1024]); sig = self.sb("sig", [128, 1024])
            mixm = self.sb("mixm", [128, 1024], BF16)
            pn_t = self.sb("pn_t", [128, 4, 2])
            Cs32s = [self.sb("Cs32_%d" % i, [128, NS, 2, 257]) for i in range(2)]
            Csbf = self.sb("Csbf", [128, NS, 2, 257], BF16)
            ns_ts = [self.sb("ns_t%d" % i, [128, NS, 2]) for i in range(2)]
            ns_o = self.sb("ns_o", [128, NS, 2])

            def load_cs(h):
                for cc in range(2):
                    self.DMA("sp", Cs32s[h % 2][:, :, cc, 0:256], self.sC[l, :, h, cc * 128:(cc + 1) * 128, :].re("i p e -> p i e"))
                self.DMA("sp", ns_ts[h % 2], self.snT[l, h])
            qTm = self.sb("qTm", [128, 2, NS, 128], BF16); wvm = self.sb("wvm", [128, NS, 257], BF16)
            R3 = self.sb("R3", [128, 4, NS]); decrep = self.sb("decrep", [128, 4, NS])

            def load(tb):
                i = tb % 2
                cols = slice(tb * 128, (tb + 1) * 128)
                self.DMA("sp", gt[i], self.g_s[l][cols, 0:8])
                self.DMA("sp", qT[i], self.qT_s[l][:, cols].re("(j p) t -> p j t", p=128))
                self.DMA("sp", kv[i], self.kv_s[l][cols, :])
                self.DMA("sp", oo[i], self.oz_s[l][cols, 0:1024])

            def gates(tb):
                i = tb % 2
                G = g4[i]
                smp = (tb == NTB - 1)
                sfx = "_s" if smp else "_p"
                TRI, MASKN, MASKTP, ELAST = K["tri" + sfx], K["maskn" + sfx], K["masktp" + sfx], K["elast" + sfx]
                g_ = gt[i]
                if smp:
                    self.DMA("sp", mprev, self.smrep[l])
                self.TT("dve", G["ig"], g_[:, 0:4], bi, ALU.add)
                self.TT("dve", G["fz"], g_[:, 4:8], bf, ALU.add)
                yield
                self.ACT(G["e1"], G["fz"], AF.Exp, scale=-1.0)
                yield
                self.ACT(G["sp"], G["e1"], AF.Ln, bias=1.0)
                yield
                self.MM(PS[0][:, 0:4], TRI, G["sp"])
                yield
                self.TS("dve", G["b"], PS[0][:, 0:4], -1.0, ALU.mult)
                yield
                self.TT("dve", G["a"], G["ig"], G["b"], ALU.subtract)
                yield
                self.TT("dve", R, identf.bc(1, [128, 4, 128]), G["a"].bc(2, [128, 4, 128]), ALU.mult)
                yield
                self.MM(PS[1], ones, R.re("p h s -> p (h s)"))
                yield
                self.TT("dve", tmpA, PS[1].re("p (h s) -> p h s", h=4), MASKN.bc(1, [128, 4, 128]), ALU.add)
                yield
                self.REDMAX(G["cm"], tmpA)
                yield
                self.TT("dve", G["g"], G["cm"], mprev, ALU.max)
                yield
                self.TT("dve", R, identf.bc(1, [128, 4, 128]), G["g"].bc(2, [128, 4, 128]), ALU.mult)
                self.TT("dve", G["t1"], mprev, G["g"], ALU.subtract)
                self.TT("dve", G["t2"], G["b"], G["g"], ALU.add)
                self.COPY("dve", gb8[i][:, 0:4], G["g"]); self.COPY("dve", gb8[i][:, 4:8], G["b"])
                yield
                self.MM(PS[1], ones, R.re("p h s -> p (h s)"))
                self.MM(PS[0][:, 8:16], ELAST, gb8[i])
                self.ACT(G["wint"], G["t1"], AF.Exp)
                self.ACT(G["enm"], G["t2"], AF.Exp, scale=-1.0)
                yield
                self.TT("dve", tmpA, PS[1].re("p (h s) -> p h s", h=4), MASKTP.bc(1, [128, 4, 128]), ALU.add)
                self.COPY("dve", glb[i], PS[0][:, 8:16])
                yield
                for h in range(4):
                    self.ACT(DT[i][:, h, :], tmpA[:, h, :], AF.Exp, bias=G["a"][:, h:h + 1], scale=-1.0)
                self.TT("dve", G["mend"], glb[i][:, 0:4], glb[i][:, 4:8], ALU.add)
                self.TT("dve", G["t3"], G["a"], glb[i][:, 0:4], ALU.subtract)
                yield
                self.ACT(G["wend"], G["t3"], AF.Exp)
                yield
                self.TT("dve", G["t3"], mprev, glb[i][:, 0:4], ALU.subtract)
                yield
                self.ACT(G["dec"], G["t3"], AF.Exp)
                yield
                if smp:
                    self.TT("dve", R3, seql.bc(1, [128, 4, NS]), G["dec"].bc(2, [128, 4, NS]), ALU.mult)
                    self.MM(PS[0][:, 64:128], ones, R3.re("p h i -> p (h i)"))
                    self.COPY("dve", decrep, PS[0][:, 64:128].re("p (h i) -> p h i", h=4))
                    self.DMA("sp", self.o_sm[l], G["mend"][7:128:8, :])
                else:
                    self.COPY("dve", mprev, G["mend"])
                    if tb == NTB - 2:
                        self.DMA("sp", self.o_pm[l:l + 1, :], G["mend"][0:1, :])
                yield

            def head(tb, h):
                i = tb % 2
                G = g4[i]
                ti = h % 2
                q_T, kv_ = qT[i], kv[i]
                pi, pn, pu = PS[3 + ti], PS[5 + ti], PS[7]
                self.MM(pi[:, 0:257], PT[:, h, :], vaug[:, h, :])
                for cc in range(2):
                    self.MM(pn[:, 0:257], q_T[:, h * 2 + cc, :], Cbf[:, h, cc, :], start=(cc == 0), stop=(cc == 1), inc=(cc == 1))
                yield
                self.ACT(tmpI[ti], pn[:, 0:257], AF.Copy, scale=G["wint"][:, h:h + 1])
                yield
                self.TT("dve", comb[ti], tmpI[ti], pi[:, 0:257], ALU.add)
                yield
                self.ACT(dd[ti], comb[ti][:, 256:257], AF.Abs)
                for cc in range(2):
                    pu2 = pu if cc == 0 else PS[2]
                    self.MM(pu2[:, 0:257], kv_[:, h * 256 + cc * 128: h * 256 + (cc + 1) * 128], wv[:, h, :])
                    self.STT("dve", C32[:, h, cc, :], C32[:, h, cc, :], G["dec"][:, h:h + 1], pu2[:, 0:257], ALU.mult, ALU.add)
                    self.COPY("act", Cbf[:, h, cc, :], C32[:, h, cc, :])
                yield
                self.TS("dve", dd[ti], dd[ti], G["enm"][:, h:h + 1], ALU.max)
                yield
                self.RECIP(rr_[ti], dd[ti])
                yield
                self.TS("dve", hh[ti], comb[ti][:, 0:256], rr_[ti], ALU.mult)
                yield
                self.c.op("dve", lambda g: g.bn_stats(out=bn[ti].ap, in_=hh[ti].ap), [hh[ti].buf], [bn[ti].buf])
                yield
                self.c.op("dve", lambda g: g.bn_aggr(out=mv[ti].ap, in_=bn[ti].ap), [bn[ti].buf], [mv[ti].buf])
                yield
                self.ACT(rstd[ti], mv[ti][:, 1:2], AF.Ln, bias=GN_EPS)
                yield
                self.ACT(rstd[ti], rstd[ti], AF.Exp, scale=-0.5)
                yield
                self.TS("dve", hn[:, h * 256:(h + 1) * 256], hh[ti], mv[ti][:, 0:1], ALU.subtract, rstd[ti], ALU.mult)
                yield

            def head_smp(tb, h):
                i = tb % 2
                G = g4[i]
                ti = 0
                q_T, kv_ = qT[i], kv[i]
                pi, pn = PS[3], PS[5]
                self.MM(pi[:, 0:257], PT[:, h, :], vaug[:, h, :])
                Cs32 = Cs32s[h % 2]
                ns_t = ns_ts[h % 2]
                if h == 0:
                    load_cs(0)
                if h + 1 < 4:
                    load_cs(h + 1)
                self.COPY("dve", Cs32[:, :, :, 256], ns_t)
                self.COPY("act", Csbf, Cs32)
                for cc in range(2):
                    self.TT("dve", qTm[:, cc], q_T[:, h * 2 + cc, :].bc(1, [128, NS, 128]), smt, ALU.mult)
                for si in range(NS):
                    for cc in range(2):
                        self.MM(pn[:, 0:257], qTm[:, cc, si, :], Csbf[:, si, cc, :],
                                start=(si == 0 and cc == 0), stop=(si == NS - 1 and cc == 1), inc=(si == NS - 1 and cc == 1))
                yield
                self.ACT(tmpI[ti], pn[:, 0:257], AF.Copy, scale=G["wint"][:, h:h + 1])
                self.TT("dve", comb[ti], tmpI[ti], pi[:, 0:257], ALU.add)
                self.ACT(dd[ti], comb[ti][:, 256:257], AF.Abs)
                self.TS("dve", dd[ti], dd[ti], G["enm"][:, h:h + 1], ALU.max)
                self.RECIP(rr_[ti], dd[ti])
                self.TS("dve", hh[ti], comb[ti][:, 0:256], rr_[ti], ALU.mult)
                self.c.op("dve", lambda g: g.bn_stats(out=bn[ti].ap, in_=hh[ti].ap), [hh[ti].buf], [bn[ti].buf])
                self.c.op("dve", lambda g: g.bn_aggr(out=mv[ti].ap, in_=bn[ti].ap), [bn[ti].buf], [mv[ti].buf])
                self.ACT(rstd[ti], mv[ti][:, 1:2], AF.Ln, bias=GN_EPS)
                self.ACT(rstd[ti], rstd[ti], AF.Exp, scale=-0.5)
                self.TS("dve", hn[:, h * 256:(h + 1) * 256], hh[ti], mv[ti][:, 0:1], ALU.subtract, rstd[ti], ALU.mult)
                yield
                self.TT("dve", wvm, wv[:, h, :].bc(1, [128, NS, 257]), seqm.bc(2, [128, NS, 257]), ALU.mult)
                n = 0
                for si in range(NS):
                    for cc in range(2):
                        pu2 = (PS[7], PS[2], PS[4], PS[6])[n % 4]
                        n += 1
                        self.MM(pu2[:, 0:257], kv_[:, h * 256 + cc * 128: h * 256 + (cc + 1) * 128], wvm[:, si, :])
                        self.STT("dve", Cs32[:, si, cc, :], Cs32[:, si, cc, :], decrep[:, h, si:si + 1], pu2[:, 0:257], ALU.mult, ALU.add)
                for cc in range(2):
                    self.DMA("sp", self.o_sC[l, :, h, cc * 128:(cc + 1) * 128, :].re("i p e -> p i e"), Cs32[:, :, cc, 0:256])
                self.COPY("dve", ns_o, Cs32[:, :, :, 256])
                self.DMA("sp", self.o_snT[l, h], ns_o)
                yield

            def heavy(tb):
                i = tb % 2
                G = g4[i]
                smp = (tb == NTB - 1)
                q_T, k_T, kv_, o_ = qT[i], kT[i], kv[i], oo[i]
                self.COPY("act", vaug[:, :, 0:256], kv_[:, 1024:2048].re("p (h e) -> p h e", h=4))
                psb = PS[2].bitcast(BF16)
                for j in range(8):
                    self.TR(psb[:, j * 128:(j + 1) * 128], kv_[:, j * 128:(j + 1) * 128], identb, inc=(j == 7))
                self.COPY("act", k_T, psb.re("p (j t) -> p j t", j=8))
                yield
                for h in range(4):
                    for cc in range(2):
                        self.MM(PS[2][:, h * 128:(h + 1) * 128], k_T[:, h * 2 + cc, :], q_T[:, h * 2 + cc, :],
                                start=(cc == 0), stop=(cc == 1), inc=(cc == 1))
                self.ACT(sig, o_, AF.Exp, scale=-1.0)
                self.ACT(sig, sig, AF.Ln, bias=1.0)
                self.ACT(sig, sig, AF.Exp, scale=-1.0)
                yield
                self.TT("dve", wv, vaug, G["wend"].bc(2, [128, 4, 257]), ALU.mult)
                self.TT("dve", PT, PS[2].re("p (h t) -> p h t", h=4), DT[i], ALU.mult)
                yield
                if not smp:
                    for hp in range(2):
                        yield from rr_gen([head(tb, 2 * hp), head(tb, 2 * hp + 1)])
                else:
                    for h in range(4):
                        yield from head_smp(tb, h)
                self.TT("dve", hn, hn, mnw, ALU.mult)
                yield
                self.TT("dve", mixm, hn, sig, ALU.mult)
                self.DMA("sp", self.mix_s[l][tb * 128:(tb + 1) * 128, 0:1024], mixm)
                if tb == NTB - 2:
                    for cc in range(2):
                        self.DMA("sp", self.o_pC[l, :, cc * 128:(cc + 1) * 128, :].re("h p e -> p h e"), C32[:, :, cc, 0:256])
                    self.COPY("dve", pn_t, C32[:, :, :, 256])
                    self.DMA("sp", self.o_pnT[l], pn_t)
                yield

            load(0)
            run_gens([gates(0)])
            for tb in range(NTB):
                if tb + 1 < NTB:
                    load(tb + 1)
                    run_gens([heavy(tb), gates(tb + 1)])
                else:
                    run_gens([heavy(tb)])
            c.barrier()
        self.es = None

    def phaseC(self, l):
        c = self.c
        PS = self.PS
        with ExitStack() as es:
            self.es = es
            K = self.load_consts()
            identf, ones = K["identf"], K["ones"]
            dtb = self.sb("dtb", [128, 16]); aneg = self.sb("aneg", [128, 16]); dsk = self.sb("dsk", [128, 16])
            self.DMA("sp", dtb, V(self.dtb.ap[l:l + 1, :].partition_broadcast(128), self.dtb.buf))
            self.DMA("sp", aneg, V(self.alog.ap[l:l + 1, :].partition_broadcast(128), self.alog.buf))
            self.DMA("sp", dsk, V(self.dsk.ap[l:l + 1, :].partition_broadcast(128), self.dsk.buf))
            self.ACT(aneg, aneg, AF.Exp)
            self.TS("dve", aneg, aneg, -1.0, ALU.mult)
            snw = self.sb("snw", [128, 1024])
            self.DMA("sp", snw, V(self.snw.ap[l:l + 1, :].partition_broadcast(128), self.snw.buf))
            cw = self.sb("cw", [128, 12, 4]); cb = self.sb("cb", [128, 12])
            self.DMA("sp", cw, self.scw[l]); self.DMA("sp", cb, self.scb[l])
            smtb = self.sb("smtb", [128, NS, 128], BF16)
            self.DMA("pool", smtb, V(self.smt.ap.unsqueeze(0).to_broadcast([128, NS, 128]), self.smt.buf))
            seqm = self.sb("seqm", [128, NS]); seql = self.sb("seql", [128, NS])
            self.DMA("sp", seqm, self.seqm[0]); self.DMA("sp", seql, self.seqm[1])
            nident = self.sb("nident", [128, 128])
            self.TS("dve", nident, identf, -1.0, ALU.mult)
            ca = [self.sb("ca%d" % i, [128, 12, 128]) for i in range(4)]
            ST32 = self.sb("ST32", [128, 1024]); STb = self.sb("STb", [128, 1024], BF16)
            self.MEMSET("dve", ST32, 0.0); self.MEMSET("pool", STb, 0.0)
            XP = [self.sb("XP%d" % i, [128, 12, 131]) for i in range(3)]
            self.MEMSET("dve", XP[0], 0.0)
            zz = [self.sb("zz%d" % i, [128, 1024]) for i in range(2)]
            gt = [self.sb("gtc%d" % i, [128, 16]) for i in range(3)]
            gn = ["fz", "e1", "dt", "a", "b", "eb", "bl", "t1", "wend", "ebl", "lnd", "nb"]
            g16 = [{n: self.sb("g16_%s%d" % (n, i), [128, 16]) for n in gn} for i in range(2)]
            xbca = self.sb("xbca", [128, 12, 128]); u_t = self.sb("u_t", [128, 12, 128])
            xs32 = [self.sb("xs32_%d" % i, [128, 1024]) for i in range(3)]
            xsbs = [self.sb("xsb%d" % i, [128, 1024], BF16) for i in range(2)]
            Btok = [self.sb("Btok%d" % i, [128, 2, 128], BF16) for i in range(3)]
            CTb = [self.sb("CTb%d" % i, [128, 2, 128], BF16) for i in range(3)]
            BTbs = [self.sb("BTb%d" % i, [128, 2, 128], BF16) for i in range(2)]
            cbm = self.sb("cbm", [128, 2, 128])
            R = [self.sb("Rc%d" % i, [128, 4, 128]) for i in range(2)]
            decT = self.sb("decT", [128, 16, 128])
            mwT = self.sb("mwT", [128, 16, 128], BF16)
            yin = [self.sb("yin%d" % i, [128, 1024]) for i in range(2)]
            wxs = [self.sb("wxs%d" % i, [128, 1024], BF16) for i in range(2)]
            y1 = self.sb("y1", [128, 1024]); y2 = self.sb("y2", [128, 1024])
            sq = self.sb("sq", [128, 512]); ss = self.sb("ss", [128, 2]); rinv = self.sb("rinv", [128, 2])
            mixs = self.sb("mixs", [128, 1024], BF16)
            sc_t = self.sb("sc_t", [128, 12, 3])
            XS = self.sb("XS", [128, 12, NS, 11])
            sc_in = self.sb("sc_in", [128, 12, NS, 3])
            Ss32 = self.sb("Ss32", [128, 8, 512]); Ssb = self.sb("Ssb", [128, 8, 512], BF16)
            CTm = self.sb("CTm", [128, NS, 128], BF16); wxsm = self.sb("wxsm", [128, 4, 512], BF16)
            R3 = self.sb("R3c", [128, NS, 16]); decS = self.sb("decS", [128, NS, 16])

            def load(tb):
                i = tb % 2
                cols = slice(tb * 128, (tb + 1) * 128)
                if tb < NTB - 1:
                    self.DMA("sp", XP[tb % 3][:, :, 3:131], self.xbcT_s[l][:, cols].re("(j p) t -> p j t", p=128))
                else:
                    for j in range(12):
                        self.DMA("sp", XS[:, j, :, 3:11], self.xbcT_s[l][j * 128:(j + 1) * 128, cols].re("p (i t) -> p i t", i=NS))
                    self.DMA("sp", sc_in, self.ssc[l])
                self.DMA("sp", gt[tb % 3], self.g_s[l][cols, 8:24])

            def load_z(tb):
                self.DMA("sp", zz[tb % 2], self.oz_s[l][tb * 128:(tb + 1) * 128, 1024:2048])

            done1b = {}

            def stage1b(tb):
                i = tb % 2
                smp = (tb == NTB - 1)
                sfx = "_s" if smp else "_p"
                TRI, MASKTP, ELAST = K["tri" + sfx], K["masktp" + sfx], K["elast" + sfx]
                G = g16[i]
                self.TT("dve", G["fz"], gt[tb % 3], dtb, ALU.add)
                yield
                self.ACT(G["e1"], G["fz"], AF.Exp)
                self.ACT(G["dt"], G["e1"], AF.Ln, bias=1.0)
                self.ACT(G["lnd"], G["dt"], AF.Ln)
                yield
                self.TT("dve", G["a"], G["dt"], aneg, ALU.mult)
                yield
                self.MM(PS[3][:, 0:16], TRI, G["a"])
                yield
                self.COPY("dve", G["b"], PS[3][:, 0:16])
                yield
                self.MM(PS[3][:, 16:32], ELAST, G["b"])
                self.ACT(G["eb"], G["b"], AF.Exp)
                self.TT("dve", G["nb"], G["lnd"], G["b"], ALU.subtract)
                yield
                self.COPY("dve", G["bl"], PS[3][:, 16:32])
                yield
                self.ACT(G["ebl"], G["bl"], AF.Exp)
                self.TT("dve", G["t1"], G["bl"], G["b"], ALU.subtract)
                yield
                self.ACT(G["wend"], G["t1"], AF.Exp)
                yield
                self.TT("dve", G["wend"], G["wend"], G["dt"], ALU.mult)
                for qd in range(4):
                    Rq = R[qd % 2]
                    ps = PS[4]
                    self.TT("pool", Rq, identf.bc(1, [128, 4, 128]), G["b"][:, qd * 4:(qd + 1) * 4].bc(2, [128, 4, 128]), ALU.mult)
                    yield
                    self.MM(ps, ones, Rq.re("p h t -> p (h t)"), start=True, stop=False, inc=False)
                    self.MM(ps.re("p (h t) -> p h t", h=4), nident, MASKTP.bc(1, [128, 4, 128]), start=False, stop=True)
                    yield
                    for hh_ in range(4):
                        h = qd * 4 + hh_
                        self.ACT(decT[:, h, :], ps[:, hh_ * 128:(hh_ + 1) * 128], AF.Exp, bias=G["nb"][:, h:h + 1])
                    yield
                done1b[tb] = True

            def stage0(tb):
                i = tb % 2
                k3 = tb % 3
                smp = (tb == NTB - 1)
                xp = XP[k3]
                xsb = xsbs[i]
                BTb = BTbs[i]
                if smp:
                    self.COPY("dve", XS[:, :, :, 0:3], sc_in)
                    yield
                def tapv(t):
                    if not smp:
                        return xp[:, :, t:t + 128], V(cw.ap[:, :, t:t + 1].to_broadcast([128, 12, 128]), cw.buf), (lambda a: a)
                    return (XS[:, :, :, t:t + 8], V(cw.ap[:, :, t:t + 1].unsqueeze(3).to_broadcast([128, 12, NS, 8]), cw.buf),
                            (lambda a: a.re("p j (i t) -> p j i t", i=NS)))
                for t in range(4):
                    xin, wbc, view = tapv(t)
                    self.TT("dve" if t % 2 == 0 else "pool", view(ca[t]), xin, wbc, ALU.mult)
                yield
                self.TT("dve", ca[0], ca[0], ca[2], ALU.add)
                self.TT("pool", ca[1], ca[1], ca[3], ALU.add)
                yield
                self.TT("dve", ca[0], ca[0], ca[1], ALU.add)
                yield
                self.TT("dve", u_t, ca[0], cb.bc(2, [128, 12, 128]), ALU.add)
                yield
                if not smp:
                    if tb + 1 < NTB - 1:
                        self.COPY("pool", XP[(tb + 1) % 3][:, :, 0:3], xp[:, :, 128:131])
                    if tb == NTB - 2:
                        self.COPY("dve", sc_t, xp[:, :, 128:131])
                        self.DMA("sp", self.o_psc[l], sc_t)
                else:
                    self.COPY("dve", sc_in, XS[:, :, :, 8:11])
                    self.DMA("sp", self.o_ssc[l], sc_in)
                self.ACT(xbca, u_t, AF.Exp, scale=-1.0)
                yield
                self.ACT(xbca, xbca, AF.Ln, bias=1.0)
                yield
                self.ACT(xbca, xbca, AF.Exp, scale=-1.0)
                yield
                self.TT("dve", xbca, xbca, u_t, ALU.mult)
                yield
                for half in range(2):
                    for j in range(4):
                        self.TR(PS[0][:, j * 128:(j + 1) * 128], xbca[:, half * 4 + j, :], identf, inc=(j == 3))
                    self.COPY("act", xs32[k3][:, half * 512:(half + 1) * 512], PS[0])
                    yield
                for g in range(2):
                    self.TR(PS[2][:, g * 128:(g + 1) * 128], xbca[:, 8 + g, :], identf, inc=(g == 1))
                self.COPY("pool", BTb, xbca[:, 8:10, :])
                self.COPY("pool", CTb[k3], xbca[:, 10:12, :])
                self.COPY("act", xsb, xs32[k3])
                yield
                self.COPY("dve", Btok[k3], PS[2][:, 0:256].re("p (g n) -> p g n", g=2))
                yield

            def stage1j(tb):
                i = tb % 2
                k3 = tb % 3
                smp = (tb == NTB - 1)
                sfx = "_s" if smp else "_p"
                MASK01T = K["mask01t" + sfx]
                G = g16[i]
                xsb = xsbs[i]
                BTb = BTbs[i]
                for g in range(2):
                    self.MM(PS[2][:, 256 + g * 128:256 + (g + 1) * 128], BTb[:, g, :], CTb[k3][:, g, :])
                yield
                self.TT("dve", cbm, PS[2][:, 256:512].re("p (g t) -> p g t", g=2), MASK01T.bc(1, [128, 2, 128]), ALU.mult)
                yield
                while not done1b.get(tb):
                    yield
                for g in range(2):
                    self.TT("pool", mwT[:, g * 8:(g + 1) * 8, :], decT[:, g * 8:(g + 1) * 8, :], cbm[:, g, :].bc(1, [128, 8, 128]), ALU.mult)
                    yield
                self.TT("pool", wxs[i].re("p (h q) -> p h q", h=16), xs32[k3].re("p (h q) -> p h q", h=16),
                        G["wend"].bc(2, [128, 16, 64]), ALU.mult)
                for h in range(16):
                    ps = PS[6 + h // 8]
                    hh_ = h % 8
                    self.MM(ps[:, hh_ * 64:(hh_ + 1) * 64], mwT[:, h, :], xsb[:, h * 64:(h + 1) * 64], inc=(hh_ == 7))
                for g in range(2):
                    self.COPY("act", yin[i][:, g * 512:(g + 1) * 512], PS[6 + g])
                yield

            def stage2(tb):
                i = tb % 2
                k3 = tb % 3
                smp = (tb == NTB - 1)
                G = g16[i]
                z_ = zz[i]
                cols = slice(tb * 128, (tb + 1) * 128)
                pbank = (PS[1], PS[5])
                self.TT("pool", y2.re("p (h q) -> p h q", h=16), xs32[k3].re("p (h q) -> p h q", h=16), dsk.bc(2, [128, 16, 64]), ALU.mult)
                if smp:
                    self.TT("dve", R3, seql.bc(2, [128, NS, 16]), G["ebl"].bc(1, [128, NS, 16]), ALU.mult)
                    self.MM(PS[3][:, 256:512], ones, R3.re("p i h -> p (i h)"))
                    self.COPY("dve", decS, PS[3][:, 256:512].re("p (i h) -> p i h", i=NS))
                for g in range(2):
                    ps = pbank[g]
                    gs = slice(g * 512, (g + 1) * 512)
                    if not smp:
                        self.MM(ps, CTb[k3][:, g, :], STb[:, gs])
                    else:
                        self.TT("dve", CTm, CTb[k3][:, g, :].bc(1, [128, NS, 128]), smtb, ALU.mult)
                        for hf in range(2):
                            self.DMA("sp", Ss32, self.sS[l, g][:, hf * 8:(hf + 1) * 8, :])
                            self.COPY("act", Ssb, Ss32)
                            for s8 in range(8):
                                si = hf * 8 + s8
                                self.MM(ps, CTm[:, si, :], Ssb[:, s8, :], start=(si == 0), stop=(si == NS - 1), inc=(s8 == 7))
                            for qq in range(2):
                                self.TT("dve", wxsm, wxs[i][:, gs].bc(1, [128, 4, 512]),
                                        seqm[:, hf * 8 + qq * 4:hf * 8 + (qq + 1) * 4].bc(2, [128, 4, 512]), ALU.mult)
                                for s4 in range(4):
                                    s8 = qq * 4 + s4
                                    si = hf * 8 + s8
                                    pu = PS[0] if si % 2 == 0 else PS[6]
                                    self.MM(pu, Btok[k3][:, g, :], wxsm[:, s4, :])
                                    sv = Ss32[:, s8, :].re("p (h q) -> p h q", h=8)
                                    self.TT("dve", sv, sv, decS[:, si, g * 8:(g + 1) * 8].bc(2, [128, 8, 64]), ALU.mult)
                                    self.TT("dve", Ss32[:, s8, :], Ss32[:, s8, :], pu, ALU.add)
                            self.DMA("sp", self.o_sS[l, g][:, hf * 8:(hf + 1) * 8, :], Ss32)
                    yield
                    self.TT("dve", y1[:, gs].re("p (h q) -> p h q", h=8), ps.re("p (h q) -> p h q", h=8),
                            G["eb"][:, g * 8:(g + 1) * 8].bc(2, [128, 8, 64]), ALU.mult)
                    yield
                    self.TT("dve", y1[:, gs], y1[:, gs], yin[i][:, gs], ALU.add)
                    yield
                self.TT("dve", y1, y1, y2, ALU.add)
                yield
                self.ACT(y2, z_, AF.Exp, scale=-1.0)
                yield
                self.ACT(y2, y2, AF.Ln, bias=1.0)
                self.TT("dve", y1, y1, z_, ALU.mult)
                yield
                self.ACT(y2, y2, AF.Exp, scale=-1.0)
                yield
                self.TT("dve", y1, y1, y2, ALU.mult)
                yield
                for g in range(2):
                    self.ACT(sq, y1[:, g * 512:(g + 1) * 512], AF.Square, accum=ss[:, g:g + 1])
                yield
                self.ACT(rinv, ss, AF.Ln, scale=1.0 / 512.0, bias=GN_EPS)
                yield
                self.ACT(rinv, rinv, AF.Exp, scale=-0.5)
                yield
                for g in range(2):
                    gs = slice(g * 512, (g + 1) * 512)
                    self.STT("dve", mixs[:, gs], y1[:, gs], rinv[:, g:g + 1], snw[:, gs], ALU.mult, ALU.mult)
                self.DMA("sp", self.mix_s[l][cols, 1024:2048], mixs)
                yield
                if not smp:
                    for g in range(2):
                        gs = slice(g * 512, (g + 1) * 512)
                        ps = pbank[g]
                        self.MM(ps, Btok[k3][:, g, :], wxs[i][:, gs])
                        sv = ST32[:, gs].re("p (h q) -> p h q", h=8)
                        self.TT("dve", sv, sv, G["ebl"][:, g * 8:(g + 1) * 8].bc(2, [128, 8, 64]), ALU.mult)
                        yield
                        self.TT("dve", ST32[:, gs], ST32[:, gs], ps, ALU.add)
                        self.COPY("act", STb[:, gs], ST32[:, gs])
                        yield
                    if tb == NTB - 2:
                        self.DMA("sp", self.o_pS[l], ST32)
                yield

            load(0); load_z(0)
            load(1); load_z(1)
            load(2)
            run_gens([stage1b(0), stage0(0)])
            run_gens([stage1j(0), stage0(1)])
            for tb in range(NTB):
                if tb + 3 < NTB:
                    load(tb + 3)
                gens = [stage2(tb)]
                if tb + 1 < NTB:
                    gens += [stage1b(tb + 1), stage1j(tb + 1)]
                if tb + 2 < NTB:
                    gens += [stage0(tb + 2)]
                run_gens(gens)
                if tb + 2 < NTB:
                    load_z(tb + 2)
            c.barrier()
        self.es = None

    def layernorm(self, pre, out, gam, bet, tmp, bn, mv, rstd, nmr):
        for q in range(4):
            self.c.op("dve", lambda g: g.bn_stats(out=bn.ap[:, q * 6:(q + 1) * 6], in_=pre.ap[:, q * 512:(q + 1) * 512]), [pre.buf], [bn.buf])
        self.c.op("dve", lambda g: g.bn_aggr(out=mv.ap, in_=bn.ap), [bn.buf], [mv.buf])
        self.ACT(rstd, mv[:, 1:2], AF.Sqrt, bias=LN_EPS)
        self.RECIP(rstd, rstd)
        self.STT("dve", nmr, mv[:, 0:1], -1.0, rstd, ALU.mult, ALU.mult)
        self.ACT(tmp, pre, AF.Identity, bias=nmr, scale=rstd)
        self.TT("pool", tmp, tmp, gam, ALU.mult)
        self.TT("dve", out, tmp, bet, ALU.add)

    def phaseD(self, l, src):
        c = self.c
        PS = self.PS
        with ExitStack() as es:
            self.es = es
            identb = self.sb("identb", [128, 128], BF16)
            self.DMA("pool", identb, self.cst[0])
            wo = [self.sb("wo%d" % q, [128, 16, 512], BF16) for q in range(4)]
            wsrc = self.w_out[l].re("(k p) n -> p k n", p=128)
            for q in range(4):
                self.DMA("pool", wo[q], wsrc[:, :, q * 512:(q + 1) * 512])
            gam = self.sb("gam", [128, D]); bet = self.sb("bet", [128, D])
            self.DMA("sp", gam, V(self.ln1g.ap[l:l + 1, :].partition_broadcast(128), self.ln1g.buf))
            self.DMA("sp", bet, V(self.ln1b.ap[l:l + 1, :].partition_broadcast(128), self.ln1b.buf))
            mx = [self.sb("mx%d" % i, [128, 2048], BF16) for i in range(2)]
            xr = [self.sb("xr%d" % i, [128, D]) for i in range(2)]
            mT = [self.sb("mT%d" % i, [128, 16, 128], BF16) for i in range(2)]
            pre = [self.sb("pre%d" % i, [128, D]) for i in range(2)]
            x1 = [self.sb("x1_%d" % i, [128, D]) for i in range(2)]
            tmp = [self.sb("tmp%d" % i, [128, D]) for i in range(2)]
            bn = [self.sb("bn%d" % i, [128, 24]) for i in range(2)]; mv = [self.sb("mv%d" % i, [128, 2]) for i in range(2)]
            rstd = [self.sb("rstd%d" % i, [128, 1]) for i in range(2)]; nmr = [self.sb("nmr%d" % i, [128, 1]) for i in range(2)]

            def load(tb):
                i = tb % 2
                rows = slice(tb * 128, (tb + 1) * 128)
                self.DMA("sp", mx[i], self.mix_s[l][rows, :])
                self.DMA("sp", xr[i], src[rows, :])

            def tre(tb):
                i = tb % 2
                for half in range(2):
                    psb = PS[4 + half].bitcast(BF16)
                    for k in range(8):
                        kk = half * 8 + k
                        self.TR(psb[:, k * 128:(k + 1) * 128], mx[i][:, kk * 128:(kk + 1) * 128], identb, inc=(k == 7))
                    self.evac(mT[i][:, half * 8:(half + 1) * 8, :], psb.re("p (k t) -> p k t", k=8))

            load(0)
            tre(0)
            for tb in range(NTB):
                if tb + 1 < NTB:
                    load(tb + 1)
                i = tb % 2
                rows = slice(tb * 128, (tb + 1) * 128)
                for q in range(4):
                    ps = PS[q]
                    for k in range(16):
                        self.MM(ps, mT[i][:, k, :], wo[q][:, k, :], start=(k == 0), stop=(k == 15), inc=(k == 15))
                    qs = slice(q * 512, (q + 1) * 512)
                    self.STT("dve", pre[i][:, qs], xr[i][:, qs], ALPHA, ps, ALU.mult, ALU.add)
                if tb + 1 < NTB:
                    tre(tb + 1)
                self.layernorm(pre[i], x1[i], gam, bet, tmp[i], bn[i], mv[i], rstd[i], nmr[i])
                self.DMA("sp", self.x1_s[l][rows, :], x1[i])
            c.barrier()
        self.es = None

    def phaseE(self, l):
        c = self.c
        PS = self.PS
        with ExitStack() as es:
            self.es = es
            identb = self.sb("identb", [128, 128], BF16)
            self.DMA("pool", identb, self.cst[0])
            xT = self.sb("xT", [128, 16, NT], BF16)
            wE = [self.sb("wE%d" % i, [128, 16, 2, 128], BF16) for i in range(3)]
            wsrc = self.w_up[l].re("(k p) n -> p k n", p=128)
            fw = self.sb("fw", [128, 86, 3]); fb = self.sb("fb", [128, 86])
            self.DMA("sp", fw, self.fcw[l]); self.DMA("sp", fb, self.fcb[l])
            UP = [self.sb("UP%d" % i, [128, 2 + TP]) for i in range(2)]
            US = [self.sb("US%d" % i, [128, NS, 10]) for i in range(2)]
            for i in range(2):
                self.MEMSET("dve", UP[i][:, 0:2], 0.0)
            fcin = self.sb("fcin", [128, 86, NS, 2])
            self.DMA("sp", fcin, self.sfc[l])
            fco_p = self.sb("fco_p", [128, 86, 2]); fco_s = self.sb("fco_s", [128, 86, NS, 2])
            cv = [self.sb("cv%d" % i, [128, NT]) for i in range(2)]
            sg = self.sb("sg", [128, NT])
            hT = [self.sb("hT%d" % i, [128, NT], BF16) for i in range(2)]

            def loadw(ci):
                w = wE[ci % 3]
                self.DMA("pool", w[:, :, 0, :], wsrc[:, :, ci * 128:(ci + 1) * 128])
                self.DMA("pool", w[:, :, 1, :], wsrc[:, :, DFF + ci * 128:DFF + (ci + 1) * 128])

            loadw(0); loadw(1)
            self.build_xT(self.x1_s[l], xT, identb)
            nps = 0
            for ci in range(NCH):
                if ci + 2 < NCH:
                    loadw(ci + 2)
                w = wE[ci % 3]
                for gv in range(2):
                    ch = gv * NCH + ci
                    up, us = UP[gv], US[gv]
                    self.COPY("pool", us[:, :, 0:2], fcin[:, ch, :, :])
                    for (t0, tn) in TGS:
                        ps = PS[nps % 8]; nps += 1
                        for k in range(16):
                            self.MM(ps[:, 0:tn], w[:, k, gv, :], xT[:, k, t0:t0 + tn], start=(k == 0), stop=(k == 15), inc=(k == 15))
                        if t0 < TP:
                            self.evac(up[:, 2 + t0:2 + t0 + tn], ps[:, 0:tn])
                        else:
                            self.evac(us[:, :, 2:10], ps[:, 0:128].re("p (i t) -> p i t", i=NS))
                    self.COPY("pool", fco_p[:, ch, :], up[:, TP:TP + 2])
                    self.COPY("pool", fco_s[:, ch, :, :], us[:, :, 8:10])
                    cvp = cv[gv][:, 0:TP]
                    cvs = cv[gv][:, TP:NT].re("p (i t) -> p i t", i=NS)
                    e = "dve"
                    self.TS(e, cvp, up[:, 0:TP], fw[:, ch, 0:1], ALU.mult, fb[:, ch:ch + 1], ALU.add)
                    self.TS(e, cvs, us[:, :, 0:8], fw[:, ch, 0:1], ALU.mult, fb[:, ch:ch + 1], ALU.add)
                    for t in range(1, 3):
                        self.STT(e, cvp, up[:, t:t + TP], fw[:, ch, t:t + 1], cvp, ALU.mult, ALU.add)
                        self.STT(e, cvs, us[:, :, t:t + 8], fw[:, ch, t:t + 1], cvs, ALU.mult, ALU.add)
                self.ACT(sg, cv[0], AF.Silu)
                h = hT[ci % 2]
                self.TT("dve", h, sg, cv[1], ALU.mult)
                self.DMA("sp", self.hT_s[l][:, :, ci, :].re("b p t -> p b t"), h.re("p (b t) -> p b t", b=NTB))
            self.DMA("sp", self.o_pfc[l], fco_p)
            self.DMA("sp", self.o_sfc[l], fco_s)
            c.barrier()
        self.es = None

    def phaseF(self, l, dst):
        c = self.c
        PS = self.PS
        with ExitStack() as es:
            self.es = es
            KR = [(0, 11), (11, 22), (22, 33), (33, NCH)]
            wd = [[self.sb("wd%d_%d" % (i, r), [128, b - a, 512], BF16) for r, (a, b) in enumerate(KR)] for i in range(2)]
            wsrc = self.w_down[l].re("(k p) n -> p k n", p=128)
            hb = [self.sb("hb%d" % i, [128, NCH, 128], BF16) for i in range(3)]
            xq = [self.sb("xq%d" % i, [128, 512]) for i in range(3)]
            po = [self.sb("po%d" % i, [128, 512]) for i in range(3)]
            gam = self.sb("gam", [128, D]); bet = self.sb("bet", [128, D])
            self.DMA("sp", gam, V(self.ln2g.ap[l:l + 1, :].partition_broadcast(128), self.ln2g.buf))
            self.DMA("sp", bet, V(self.ln2b.ap[l:l + 1, :].partition_broadcast(128), self.ln2b.buf))
            pre = [self.sb("pre%d" % i, [128, D]) for i in range(3)]
            x2 = [self.sb("x2_%d" % i, [128, D]) for i in range(2)]
            tmp = self.sb("tmp", [128, D])
            bn = self.sb("bn", [128, 24]); mv = self.sb("mv", [128, 2]); rstd = self.sb("rstd", [128, 1]); nmr = self.sb("nmr", [128, 1])

            def loadw(q):
                for r, (a, b) in enumerate(KR):
                    self.DMA("pool", wd[q % 2][r], wsrc[:, a:b, q * 512:(q + 1) * 512])

            def load(n):
                q, tb = divmod(n, NTB)
                rows = slice(tb * 128, (tb + 1) * 128)
                self.DMA("sp", hb[n % 3], self.hT_s[l][tb])
                self.DMA("sp", xq[n % 3], self.x1_s[l][rows, q * 512:(q + 1) * 512])
                if q == 3:
                    self.DMA("sp", pre[n % 3][:, 0:1536], self.pre_s[l][rows, 0:1536])

            loadw(0)
            load(0); load(1)
            for q in range(4):
                if q + 1 < 4:
                    loadw(q + 1)
                for tb in range(NTB):
                    n = q * NTB + tb
                    if n + 2 < 4 * NTB:
                        if (n + 2) // NTB == 3 and (n + 2) % NTB == 0:
                            for tok in c.dtok:
                                c._wait("sp", tok)
                        load(n + 2)
                    ps = PS[n % 4]
                    for k in range(NCH):
                        r = min(k // 11, 3)
                        self.MM(ps, hb[n % 3][:, k, :], wd[q % 2][r][:, k - KR[r][0], :], start=(k == 0), stop=(k == NCH - 1), inc=(k == NCH - 1))
                    rows = slice(tb * 128, (tb + 1) * 128)
                    if q < 3:
                        self.STT("dve", po[n % 3], xq[n % 3], ALPHA, ps, ALU.mult, ALU.add)
                        self.DMA("sp", self.pre_s[l][rows, q * 512:(q + 1) * 512], po[n % 3])
                    else:
                        i = tb % 2
                        self.STT("dve", pre[n % 3][:, 1536:2048], xq[n % 3], ALPHA, ps, ALU.mult, ALU.add)
                        self.layernorm(pre[n % 3], x2[i], gam, bet, tmp, bn, mv, rstd, nmr)
                        self.DMA("sp", dst[rows, :], x2[i])
            c.barrier()
        self.es = None

    def build(self):
        self.declare()
        order = "ABCDEF"
        for l in range(2):
            src = self.x_in if l == 0 else self.x2_s
            dst = self.x2_s if l == 0 else self.y
            for ph in order:
                tag = "%d%s" % (l, ph)
                if tag > self.upto:
                    break
                if ph == "A":
                    self.phaseA(l, src)
                elif ph == "B":
                    self.phaseB(l)
                elif ph == "C":
                    self.phaseC(l)
                elif ph == "D":
                    self.phaseD(l, src)
                elif ph == "E":
                    self.phaseE(l)
                elif ph == "F":
                    self.phaseF(l, dst)
        self.c.finish()
        return self.nc


def make_consts():
    t = np.arange(128)
    cst = np.zeros((12, 128, 128), np.float32)
    cst[0] = np.eye(128)
    cst[1] = 1.0
    for kind, seq in ((0, np.zeros(128, np.int64)), (1, t // LS)):
        same = seq[:, None] == seq[None, :]
        le = t[:, None] <= t[None, :]
        allowed_st = same & le
        last = np.array([np.max(np.where(seq == seq[m])[0]) for m in range(128)])
        o = 2 + kind * 5
        cst[o + 0] = allowed_st.astype(np.float32)
        cst[o + 1] = np.where(allowed_st.T, 0.0, -1e30)
        cst[o + 2] = np.where(allowed_st, 0.0, 3e4)
        cst[o + 3] = allowed_st.astype(np.float32)
        cst[o + 4] = (t[:, None] == last[None, :]).astype(np.float32)
    smt = (np.arange(NS)[:, None] == (t // LS)[None, :]).astype(np.float32)
    seqm = np.zeros((2, 128, NS), np.float32)
    seqm[0] = smt.T
    seqm[1] = smt.T * ((t % LS) == LS - 1)[:, None]
    return cst, smt, seqm


_PROG_CACHE = {}


def _get_prog(debug=False, upto="Z"):
    key = (debug, upto)
    if key not in _PROG_CACHE:
        _PROG_CACHE[key] = Prog(debug=debug, upto=upto).build()
    return _PROG_CACHE[key]


def make_in_maps(inp):
    f = lambda a: np.ascontiguousarray(a, dtype=np.float32)
    cst, smt, seqm = make_consts()
    shared = {
        "w_in": f(inp["w_in"]), "w_out": f(inp["w_out"]), "w_up": f(inp["ffn_w_up"]), "w_down": f(inp["ffn_w_down"]),
        "b_i": f(inp["mlstm_b_i"]), "b_f": f(inp["mlstm_b_f"]), "mnw": f(inp["mlstm_norm_w"]), "snw": f(inp["ssm_norm_w"]),
        "dtb": f(inp["ssm_dt_bias"]), "alog": f(inp["ssm_A_log"]), "dsk": f(inp["ssm_D"]),
        "ln1g": f(inp["ln1_g"]), "ln1b": f(inp["ln1_b"]), "ln2g": f(inp["ln2_g"]), "ln2b": f(inp["ln2_b"]),
        "scw": f(inp["ssm_conv_w"].reshape(2, 4, 12, 128).transpose(0, 3, 2, 1)),
        "scb": f(inp["ssm_conv_b"].reshape(2, 12, 128).transpose(0, 2, 1)),
        "scb_row": f(inp["ssm_conv_b"]),
        "fcw": f(inp["ffn_conv_w"].reshape(2, 3, 86, 128).transpose(0, 3, 2, 1)),
        "fcb": f(inp["ffn_conv_b"].reshape(2, 86, 128).transpose(0, 2, 1)),
        "cst": cst, "smt": smt, "seqm": seqm,
    }
    maps = []
    for ci in range(NCORES):
        sl = slice(ci * NS, (ci + 1) * NS)
        xs = inp["x_sample"][sl].reshape(NS * LS, D)
        m = dict(shared)
        m["x_in"] = f(np.concatenate([inp["x_prompt"][ci % 4], xs], axis=0))
        m["sC"] = f(inp["state_mlstm_C"][:, sl])
        m["snT"] = f(inp["state_mlstm_n"][:, sl].reshape(2, NS, 4, 2, 128).transpose(0, 2, 4, 1, 3))
        m["smrep"] = f(np.repeat(inp["state_mlstm_m"][:, sl], LS, axis=1))
        m["sS"] = f(inp["state_ssm"][:, sl].reshape(2, NS, 2, 8, 64, 128).transpose(0, 2, 5, 1, 3, 4).reshape(2, 2, 128, NS, 512))
        m["ssc"] = f(inp["state_ssm_conv"][:, sl].reshape(2, NS, 3, 12, 128).transpose(0, 4, 3, 1, 2))
        m["sfc"] = f(inp["state_ffn_conv"][:, sl].reshape(2, NS, 2, 86, 128).transpose(0, 4, 3, 1, 2))
        maps.append(m)
    return maps


def assemble(res):
    L = 2
    y_prompt = np.stack([res[ci]["y"][:TP] for ci in range(4)], 0)
    y_sample = np.concatenate([res[ci]["y"][TP:].reshape(NS, LS, D) for ci in range(NCORES)], 0)
    p_C = np.stack([res[ci]["o_pC"] for ci in range(4)], 1)
    p_n = np.stack([res[ci]["o_pnT"].transpose(0, 2, 3, 1).reshape(L, 4, 256) for ci in range(4)], 1)
    p_m = np.stack([res[ci]["o_pm"] for ci in range(4)], 1)
    p_S = np.stack([res[ci]["o_pS"].reshape(L, 128, 16, 64).transpose(0, 2, 3, 1) for ci in range(4)], 1)
    p_sc = np.stack([res[ci]["o_psc"].transpose(0, 3, 2, 1).reshape(L, 3, 1536) for ci in range(4)], 1)
    p_fc = np.stack([res[ci]["o_pfc"].transpose(0, 3, 2, 1).reshape(L, 2, 11008) for ci in range(4)], 1)
    s_C = np.concatenate([res[ci]["o_sC"] for ci in range(NCORES)], 1)
    s_n = np.concatenate([res[ci]["o_snT"].transpose(0, 3, 1, 4, 2).reshape(L, NS, 4, 256) for ci in range(NCORES)], 1)
    s_m = np.concatenate([res[ci]["o_sm"] for ci in range(NCORES)], 1)
    s_S = np.concatenate([res[ci]["o_sS"].reshape(L, 2, 128, NS, 8, 64).transpose(0, 3, 1, 4, 5, 2).reshape(L, NS, 16, 64, 128)
                          for ci in range(NCORES)], 1)
    s_sc = np.concatenate([res[ci]["o_ssc"].transpose(0, 3, 4, 2, 1).reshape(L, NS, 3, 1536) for ci in range(NCORES)], 1)
    s_fc = np.concatenate([res[ci]["o_sfc"].transpose(0, 3, 4, 2, 1).reshape(L, NS, 2, 11008) for ci in range(NCORES)], 1)
    outs = (y_prompt, y_sample, p_C, p_n, p_m, p_S, p_sc, p_fc, s_C, s_n, s_m, s_S, s_sc, s_fc)
    return tuple(np.ascontiguousarray(o, dtype=np.float32) for o in outs)


def kernel(**inputs):
    inp = {k: np.asarray(v) for k, v in inputs.items()}
    nc = _get_prog()
    maps = make_in_maps(inp)
    res = run_bass_kernel_spmd(nc, maps, core_ids=list(range(NCORES)))
    return assemble(res.results)
```

```python
import numpy as np
from contextlib import ExitStack
import concourse.bass as bass
import concourse.mybir as mybir
from concourse.bass_utils import run_bass_kernel_spmd

F32 = mybir.dt.float32
BF16 = mybir.dt.bfloat16
AF = mybir.ActivationFunctionType
ALU = mybir.AluOpType
AX = mybir.AxisListType

NCORES = 8
D = 2048
TP = 2048
NS = 16
LS = 8
NT = TP + NS * LS
NTB = NT // 128
DFF = 5504
NCH = DFF // 128
IN_DIM = 6680
ALPHA = 4 ** 0.25
LN_EPS = 1e-5
GN_EPS = 1e-6
TGS = [(0, 512), (512, 512), (1024, 512), (1536, 512), (2048, 128)]


class Buf:
    __slots__ = ("w", "r", "name", "track")

    def __init__(self, name="", track=True):
        self.w = None
        self.r = {}
        self.name = name
        self.track = track


class V:
    __slots__ = ("ap", "buf")

    def __init__(self, ap, buf=None):
        self.ap = ap
        self.buf = buf if buf is not None else Buf()

    def __getitem__(self, k):
        return V(self.ap[k], self.buf)

    def re(self, pat, **kw):
        return V(self.ap.rearrange(pat, **kw), self.buf)

    def bc(self, axis, shape):
        return V(self.ap.unsqueeze(axis).to_broadcast(list(shape)), self.buf)

    def bitcast(self, dt):
        return V(self.ap.bitcast(dt), self.buf)

    def sub(self, ap):
        return V(ap, self.buf)

    def newbuf(self):
        return V(self.ap, Buf())


class Ctx:
    ENG = ("pe", "act", "dve", "pool", "sp")

    def __init__(self, nc, n_dma_sems=48):
        self.nc = nc
        self.eng = {"pe": nc.tensor, "act": nc.scalar, "dve": nc.vector, "pool": nc.gpsimd, "sp": nc.sync}
        self.sem = {e: nc.alloc_semaphore("s_" + e) for e in self.ENG}
        self.cnt = {e: 0 for e in self.ENG}
        self.seen = {e: {} for e in self.ENG}
        self.dsem = [nc.alloc_semaphore("d%d" % i) for i in range(n_dma_sems)]
        self.dval = [0] * n_dma_sems
        self.dtok = [None] * n_dma_sems
        self.dnext = 0
        self.dnext_sw = 0
        self.n_inst = 0
        self.n_wait = 0

    def _wait(self, e, tok):
        if tok is None:
            return
        key, sem, val, snap = tok
        if key == e and e == "pe":
            return
        seen = self.seen[e]
        if seen.get(key, 0) >= val:
            return
        self.eng[e].wait_ge(sem, val)
        self.n_wait += 1
        seen[key] = val
        for k, v in snap.items():
            if seen.get(k, 0) < v:
                seen[k] = v

    def _deps(self, e, reads, writes):
        for b in reads:
            if b.track:
                self._wait(e, b.w)
        for b in writes:
            if b.track:
                self._wait(e, b.w)
                for tok in b.r.values():
                    self._wait(e, tok)

    def op(self, e, fn, reads=(), writes=(), inc=True):
        self._deps(e, reads, writes)
        ins = fn(self.eng[e])
        self.n_inst += 1
        val = self.cnt[e] + 1
        if inc:
            ins.then_inc(self.sem[e], 1)
            self.cnt[e] = val
        tok = (e, self.sem[e], val, dict(self.seen[e]))
        for b in reads:
            if b.track:
                b.r[e] = tok
        for b in writes:
            if b.track:
                b.w = tok
                b.r = {}
        return tok

    def dma(self, q, out, in_, **kw):
        reads, writes = [in_.buf], [out.buf]
        self._deps(q, reads, writes)
        nsw = 12
        if q == "pool":
            j = self.dnext_sw
            self.dnext_sw = (j + 1) % nsw
        else:
            j = nsw + self.dnext
            self.dnext = (self.dnext + 1) % (len(self.dsem) - nsw)
        self._wait(q, self.dtok[j])
        ins = self.eng[q].dma_start(out=out.ap, in_=in_.ap, **kw)
        self.n_inst += 1
        self.dval[j] += 16
        ins.then_inc(self.dsem[j], 16)
        tok = ("d%d" % j, self.dsem[j], self.dval[j], dict(self.seen[q]))
        self.dtok[j] = tok
        for b in reads:
            if b.track:
                b.r["dma%d" % j] = tok
        for b in writes:
            if b.track:
                b.w = tok
                b.r = {}
        return tok

    def barrier(self):
        for e in self.ENG:
            for e2 in self.ENG:
                if e2 != e and self.cnt[e2] > 0:
                    self._wait(e, (e2, self.sem[e2], self.cnt[e2], {}))
            for tok in self.dtok:
                self._wait(e, tok)

    def finish(self):
        for tok in self.dtok:
            self._wait("sp", tok)
        for e2 in self.ENG:
            if e2 != "sp" and self.cnt[e2] > 0:
                self._wait("sp", (e2, self.sem[e2], self.cnt[e2], {}))


def run_gens(gens):
    gens = list(gens)
    while gens:
        for g in list(gens):
            try:
                next(g)
            except StopIteration:
                gens.remove(g)


def rr_gen(gens):
    gens = list(gens)
    while gens:
        for g in list(gens):
            try:
                next(g)
            except StopIteration:
                gens.remove(g)
        yield


class Prog:
    def __init__(self, debug=False, upto="Z"):
        self.debug = debug
        self.upto = upto
        nc = self.nc = bass.Bass("TRN2", target_bir_lowering=False)
        self.c = Ctx(nc)
        self.es = None
        self.PS = [V(nc.alloc_psum_tensor("ps%d" % i, [128, 512], F32).ap()) for i in range(8)]
        self.rr = 0
        self.nsb = 0

    def din(self, name, shape, dt=F32):
        return V(self.nc.dram_tensor(name, list(shape), dt, kind="ExternalInput").ap(), Buf(name, track=False))

    def dout(self, name, shape, dt=F32):
        return V(self.nc.dram_tensor(name, list(shape), dt, kind="ExternalOutput").ap(), Buf(name, track=False))

    def dscr(self, name, shape, dt=F32):
        kind = "ExternalOutput" if self.debug else "Internal"
        return V(self.nc.dram_tensor(name, list(shape), dt, kind=kind).ap(), Buf(name, track=False))

    def sb(self, name, shape, dt=F32):
        self.nsb += 1
        t = self.es.enter_context(self.nc.sbuf_tensor("%s_%d" % (name, self.nsb), list(shape), dt))
        return V(t.ap())

    def TT(self, e, out, a, b, op):
        self.c.op(e, lambda g: g.tensor_tensor(out=out.ap, in0=a.ap, in1=b.ap, op=op), [a.buf, b.buf], [out.buf])

    def TS(self, e, out, a, s1, op0, s2=None, op1=None):
        rd = [a.buf]
        s1a, s2a = s1, s2
        if isinstance(s1, V):
            rd.append(s1.buf); s1a = s1.ap
        if isinstance(s2, V):
            rd.append(s2.buf); s2a = s2.ap
        if op1 is None:
            self.c.op(e, lambda g: g.tensor_scalar(out=out.ap, in0=a.ap, scalar1=s1a, scalar2=None, op0=op0), rd, [out.buf])
        else:
            self.c.op(e, lambda g: g.tensor_scalar(out=out.ap, in0=a.ap, scalar1=s1a, scalar2=s2a, op0=op0, op1=op1), rd, [out.buf])

    def STT(self, e, out, a, s, b, op0, op1):
        rd = [a.buf, b.buf]
        sa = s
        if isinstance(s, V):
            rd.append(s.buf); sa = s.ap
        self.c.op(e, lambda g: g.scalar_tensor_tensor(out=out.ap, in0=a.ap, scalar=sa, in1=b.ap, op0=op0, op1=op1), rd, [out.buf])

    def ACT(self, out, a, func, bias=None, scale=None, accum=None):
        rd = [a.buf]
        wr = [out.buf]
        kw = {}
        if bias is not None:
            if isinstance(bias, V):
                rd.append(bias.buf); kw["bias"] = bias.ap
            else:
                kw["bias"] = float(bias)
        if scale is not None:
            if isinstance(scale, V):
                rd.append(scale.buf); kw["scale"] = scale.ap
            else:
                kw["scale"] = float(scale)
        if accum is not None:
            wr.append(accum.buf); kw["accum_out"] = accum.ap
        self.c.op("act", lambda g: g.activation(out=out.ap, in_=a.ap, func=func, **kw), rd, wr)

    def COPY(self, e, out, a):
        if e == "act":
            self.c.op("act", lambda g: g.copy(out=out.ap, in_=a.ap), [a.buf], [out.buf])
        else:
            self.c.op(e, lambda g: g.tensor_copy(out=out.ap, in_=a.ap), [a.buf], [out.buf])

    def MEMSET(self, e, out, val):
        self.c.op(e, lambda g: g.memset(out.ap, val), [], [out.buf])

    def MM(self, out, lhsT, rhs, start=True, stop=True, inc=True):
        self.c.op("pe", lambda g: g.matmul(out.ap, lhsT.ap, rhs.ap, start=start, stop=stop), [lhsT.buf, rhs.buf], [out.buf], inc=inc)

    def TR(self, out, a, ident, inc=True):
        self.c.op("pe", lambda g: g.transpose(out.ap, a.ap, ident.ap), [a.buf, ident.buf], [out.buf], inc=inc)

    def DMA(self, q, out, a, **kw):
        self.c.dma(q, out, a, **kw)

    def RECIP(self, out, a):
        self.c.op("dve", lambda g: g.reciprocal(out=out.ap, in_=a.ap), [a.buf], [out.buf])

    def REDMAX(self, out, a):
        self.c.op("dve", lambda g: g.tensor_reduce(out=out.ap, in_=a.ap, axis=AX.X, op=ALU.max), [a.buf], [out.buf])

    def evac(self, out, a, scale=None):
        self.rr += 1
        if self.rr % 2 == 0:
            if scale is None:
                self.COPY("act", out, a)
            else:
                self.c.op("act", lambda g: g.mul(out=out.ap, in_=a.ap, mul=float(scale)), [a.buf], [out.buf])
        else:
            if scale is None:
                self.COPY("dve", out, a)
            else:
                self.TS("dve", out, a, float(scale), ALU.mult)

    def declare(self):
        L = 2
        self.x_in = self.din("x_in", [NT, D])
        self.w_in = self.din("w_in", [L, D, IN_DIM])
        self.w_out = self.din("w_out", [L, D, D])
        self.w_up = self.din("w_up", [L, D, 2 * DFF])
        self.w_down = self.din("w_down", [L, DFF, D])
        self.b_i = self.din("b_i", [L, 4]); self.b_f = self.din("b_f", [L, 4])
        self.mnw = self.din("mnw", [L, 1024]); self.snw = self.din("snw", [L, 1024])
        self.dtb = self.din("dtb", [L, 16]); self.alog = self.din("alog", [L, 16]); self.dsk = self.din("dsk", [L, 16])
        self.ln1g = self.din("ln1g", [L, D]); self.ln1b = self.din("ln1b", [L, D])
        self.ln2g = self.din("ln2g", [L, D]); self.ln2b = self.din("ln2b", [L, D])
        self.scw = self.din("scw", [L, 128, 12, 4]); self.scb = self.din("scb", [L, 128, 12])
        self.scb_row = self.din("scb_row", [L, 1536])
        self.fcw = self.din("fcw", [L, 128, 86, 3]); self.fcb = self.din("fcb", [L, 128, 86])
        self.sC = self.din("sC", [L, NS, 4, 256, 256])
        self.snT = self.din("snT", [L, 4, 128, NS, 2])
        self.smrep = self.din("smrep", [L, 128, 4])
        self.sS = self.din("sS", [L, 2, 128, NS, 512])
        self.ssc = self.din("ssc", [L, 128, 12, NS, 3])
        self.sfc = self.din("sfc", [L, 128, 86, NS, 2])
        self.cst = self.din("cst", [12, 128, 128])
        self.smt = self.din("smt", [NS, 128])
        self.seqm = self.din("seqm", [2, 128, NS])
        self.y = self.dout("y", [NT, D])
        self.o_pC = self.dout("o_pC", [L, 4, 256, 256]); self.o_pnT = self.dout("o_pnT", [L, 128, 4, 2])
        self.o_pm = self.dout("o_pm", [L, 4]); self.o_pS = self.dout("o_pS", [L, 128, 1024])
        self.o_psc = self.dout("o_psc", [L, 128, 12, 3]); self.o_pfc = self.dout("o_pfc", [L, 128, 86, 2])
        self.o_sC = self.dout("o_sC", [L, NS, 4, 256, 256]); self.o_snT = self.dout("o_snT", [L, 4, 128, NS, 2])
        self.o_sm = self.dout("o_sm", [L, NS, 4]); self.o_sS = self.dout("o_sS", [L, 2, 128, NS, 512])
        self.o_ssc = self.dout("o_ssc", [L, 128, 12, NS, 3]); self.o_sfc = self.dout("o_sfc", [L, 128, 86, NS, 2])
        self.qT_s = [self.dscr("qT_s%d" % l, [1024, NT], BF16) for l in range(L)]
        self.kT_s = [self.dscr("kT_s%d" % l, [1024, NT], BF16) for l in range(L)]
        self.kv_s = [self.dscr("kv_s%d" % l, [NT, 2048], BF16) for l in range(L)]
        self.oz_s = [self.dscr("oz_s%d" % l, [NT, 2048]) for l in range(L)]
        self.g_s = [self.dscr("g_s%d" % l, [NT, 24]) for l in range(L)]
        self.xbcT_s = [self.dscr("xbcT_s%d" % l, [1536, NT]) for l in range(L)]
        self.mix_s = [self.dscr("mix_s%d" % l, [NT, 2048], BF16) for l in range(L)]
        self.x1_s = [self.dscr("x1_s%d" % l, [NT, D]) for l in range(L)]
        self.hT_s = [self.dscr("hT_s%d" % l, [NTB, 128, NCH, 128], BF16) for l in range(L)]
        self.pre_s = [self.dscr("pre_s%d" % l, [NT, D]) for l in range(L)]
        self.x2_s = self.dscr("x2_s", [NT, D])

    def load_consts(self, kinds=True):
        K = {}
        names = ["identf", "ones", "tri_p", "maskn_p", "masktp_p", "mask01t_p", "elast_p",
                 "tri_s", "maskn_s", "masktp_s", "mask01t_s", "elast_s"]
        for i, n in enumerate(names):
            t = self.sb("k_" + n, [128, 128])
            self.DMA("sp", t, self.cst[i])
            K[n] = t
        idb = self.sb("k_identb", [128, 128], BF16)
        self.DMA("pool", idb, self.cst[0])
        K["identb"] = idb
        return K

    def build_xT(self, src, xT, identb):
        NXB = 4
        xb = [self.sb("xb%d" % i, [128, D], BF16) for i in range(NXB)]
        for tb in range(min(NXB - 1, NTB)):
            self.DMA("pool", xb[tb % NXB], src[tb * 128:(tb + 1) * 128, :])
        for tb in range(NTB):
            b = xb[tb % NXB]
            nx = tb + NXB - 1
            if nx < NTB:
                self.DMA("pool", xb[nx % NXB], src[nx * 128:(nx + 1) * 128, :])
            for half in range(2):
                ps = self.PS[4 + (tb * 2 + half) % 4]
                psb = ps.bitcast(BF16)
                for k in range(8):
                    kk = half * 8 + k
                    self.TR(psb[:, k * 128:(k + 1) * 128], b[:, kk * 128:(kk + 1) * 128], identb, inc=(k == 7))
                self.evac(xT[:, half * 8:(half + 1) * 8, tb * 128:(tb + 1) * 128], psb.re("p (k t) -> p k t", k=8))

    def phaseA(self, l, src):
        c = self.c
        with ExitStack() as es:
            self.es = es
            identb = self.sb("identb", [128, 128], BF16)
            self.DMA("pool", identb, self.cst[0])
            xT = self.sb("xT", [128, 16, NT], BF16)
            wb = [self.sb("wA%d" % i, [128, 16, 512], BF16) for i in range(2)]
            st32 = [self.sb("st32_%d" % i, [128, 512]) for i in range(4)]
            st16 = [self.sb("st16_%d" % i, [128, 512], BF16) for i in range(4)]
            fm32 = [self.sb("fm32_%d" % i, [128, NT]) for i in range(2)]
            fm16 = [self.sb("fm16_%d" % i, [128, NT], BF16) for i in range(2)]
            wsrc = self.w_in[l].re("(k p) n -> p k n", p=128)
            jobs = [(0, 512, "q"), (512, 512, "q"), (1024, 512, "k"), (1536, 512, "k"),
                    (2048, 512, "v"), (2560, 512, "v"), (3072, 512, "o"), (3584, 512, "o"),
                    (4104, 512, "z"), (4616, 512, "z"), (5128, 512, "x"), (5640, 512, "x"), (6152, 512, "x"),
                    (-1, 24, "g")]

            def loadw(j):
                c0, ncol, mode = jobs[j]
                w = wb[j % 2]
                if mode == "g":
                    self.DMA("pool", w[:, :, 0:8], wsrc[:, :, 4096:4104])
                    self.DMA("pool", w[:, :, 8:24], wsrc[:, :, 6664:6680])
                else:
                    self.DMA("pool", w, wsrc[:, :, c0:c0 + ncol])

            cnt = {"q": 0, "k": 0, "v": 0, "o": 0, "z": 0, "x": 0}
            nps = 0
            nst = 0
            nfm = 0
            loadw(0)
            self.build_xT(src, xT, identb)
            for j in range(len(jobs)):
                if j + 1 < len(jobs):
                    loadw(j + 1)
                c0, ncol, mode = jobs[j]
                w = wb[j % 2]
                if mode in ("q", "x"):
                    for cc in range(4):
                        if mode == "x":
                            stg = fm32[nfm % 2]
                        else:
                            stg = fm16[nfm % 2]
                        nfm += 1
                        for (t0, tn) in TGS:
                            ps = self.PS[nps % 4]; nps += 1
                            for k in range(16):
                                self.MM(ps[:, 0:tn], w[:, k, cc * 128:(cc + 1) * 128], xT[:, k, t0:t0 + tn],
                                        start=(k == 0), stop=(k == 15), inc=(k == 15))
                            self.evac(stg[:, t0:t0 + tn], ps[:, 0:tn])
                        ch = cnt[mode] * 4 + cc
                        dst = {"q": self.qT_s, "x": self.xbcT_s}[mode][l]
                        self.DMA("sp", dst[ch * 128:(ch + 1) * 128, :], stg)
                if mode in ("k", "v", "o", "z", "g"):
                    for tb in range(NTB):
                        ps = self.PS[nps % 4]; nps += 1
                        for k in range(16):
                            self.MM(ps[:, 0:ncol], xT[:, k, tb * 128:(tb + 1) * 128], w[:, k, 0:ncol],
                                    start=(k == 0), stop=(k == 15), inc=(k == 15))
                        rows = slice(tb * 128, (tb + 1) * 128)
                        if mode in ("k", "v"):
                            stg = st16[nst % 4]; nst += 1
                            self.evac(stg, ps, scale=(0.0625 if mode == "k" else None))
                            cb = (0 if mode == "k" else 1024) + cnt[mode] * 512
                            self.DMA("sp", self.kv_s[l][rows, cb:cb + 512], stg)
                        elif mode in ("o", "z"):
                            stg = st32[nst % 4]; nst += 1
                            self.evac(stg, ps)
                            cb = (0 if mode == "o" else 1024) + cnt[mode] * 512
                            self.DMA("sp", self.oz_s[l][rows, cb:cb + 512], stg)
                        else:
                            stg = st32[nst % 4]; nst += 1
                            self.evac(stg[:, 0:24], ps[:, 0:24])
                            self.DMA("sp", self.g_s[l][rows, :], stg[:, 0:24])
                if mode in cnt:
                    cnt[mode] += 1
            c.barrier()
        self.es = None

    def phaseB(self, l):
        c = self.c
        PS = self.PS
        with ExitStack() as es:
            self.es = es
            K = self.load_consts()
            identf, ones, identb = K["identf"], K["ones"], K["identb"]
            bi = self.sb("bi", [128, 4]); bf = self.sb("bf", [128, 4])
            self.DMA("sp", bi, V(self.b_i.ap[l:l + 1, :].partition_broadcast(128), self.b_i.buf))
            self.DMA("sp", bf, V(self.b_f.ap[l:l + 1, :].partition_broadcast(128), self.b_f.buf))
            mnw = self.sb("mnw", [128, 1024])
            self.DMA("sp", mnw, V(self.mnw.ap[l:l + 1, :].partition_broadcast(128), self.mnw.buf))
            smt = self.sb("smt", [128, NS, 128], BF16)
            self.DMA("pool", smt, V(self.smt.ap.unsqueeze(0).to_broadcast([128, NS, 128]), self.smt.buf))
            seqm = self.sb("seqm", [128, NS]); seql = self.sb("seql", [128, NS])
            self.DMA("sp", seqm, self.seqm[0]); self.DMA("sp", seql, self.seqm[1])
            C32 = self.sb("C32", [128, 4, 2, 257]); Cbf = self.sb("Cbf", [128, 4, 2, 257], BF16)
            self.MEMSET("dve", C32, 0.0); self.MEMSET("pool", Cbf, 0.0)
            mprev = self.sb("mprev", [128, 4]); self.MEMSET("dve", mprev, 0.0)
            vaug = self.sb("vaug", [128, 4, 257], BF16); self.MEMSET("pool", vaug, 1.0)
            qT = [self.sb("qT%d" % i, [128, 8, 128], BF16) for i in range(2)]
            kT = [self.sb("kT%d" % i, [128, 8, 128], BF16) for i in range(2)]
            kv = [self.sb("kv%d" % i, [128, 2048], BF16) for i in range(2)]
            oo = [self.sb("oo%d" % i, [128, 1024]) for i in range(2)]
            gt = [self.sb("gt%d" % i, [128, 8]) for i in range(3)]
            gnames = ["ig", "fz", "e1", "sp", "b", "a", "cm", "g", "t1", "wint", "enm", "mend", "t2", "wend", "dec", "t3"]
            g4 = [{n: self.sb("g4_%s%d" % (n, i), [128, 4]) for n in gnames} for i in range(2)]
            gb8 = [self.sb("gb8_%d" % i, [128, 8]) for i in range(2)]
            glb = [self.sb("glb_%d" % i, [128, 8]) for i in range(2)]
            DT = [self.sb("DT%d" % i, [128, 4, 128]) for i in range(2)]
            R = self.sb("R", [128, 4, 128]); tmpA = self.sb("tmpA", [128, 4, 128])
            PT = self.sb("PT", [128, 4, 128], BF16)
            wv = self.sb("wv", [128, 4, 257], BF16)
            tmpI = [self.sb("tmpI%d" % i, [128, 257]) for i in range(2)]
            comb = [self.sb("comb%d" % i, [128, 257]) for i in range(2)]
            dd = [self.sb("dd%d" % i, [128, 1]) for i in range(2)]
            rr_ = [self.sb("rr%d" % i, [128, 1]) for i in range(2)]
            bn = [self.sb("bn%d" % i, [128, 6]) for i in range(2)]
            mv = [self.sb("mv%d" % i, [128, 2]) for i in range(2)]
            rstd = [self.sb("rstd%d" % i, [128, 1]) for i in range(2)]
            hh = [self.sb("hh%d" % i, [128, 256]) for i in range(2)]
            hn = self.sb("hn", [128, 1024]); sig = self.sb("sig", [128, 1024])
            mixm = self.sb("mixm", [128, 1024], BF16)
            pn_t = self.sb("pn_t", [128, 4, 2])
            Cs32s = [self.sb("Cs32_%d" % i, [128, NS, 2, 257]) for i in range(2)]
            Csbf = self.sb("Csbf", [128, NS, 2, 257], BF16)
            ns_ts = [self.sb("ns_t%d" % i, [128, NS, 2]) for i in range(2)]
            ns_o = self.sb("ns_o", [128, NS, 2])

            def load_cs(h):
                for cc in range(2):
                    self.DMA("sp", Cs32s[h % 2][:, :, cc, 0:256], self.sC[l, :, h, cc * 128:(cc + 1) * 128, :].re("i p e -> p i e"))
                self.DMA("sp", ns_ts[h % 2], self.snT[l, h])
            qTm = self.sb("qTm", [128, 2, NS, 128], BF16); wvm = self.sb("wvm", [128, NS, 257], BF16)
            R3 = self.sb("R3", [128, 4, NS]); decrep = self.sb("decrep", [128, 4, NS])

            def load(tb):
                i = tb % 2
                cols = slice(tb * 128, (tb + 1) * 128)
                self.DMA("sp", qT[i], self.qT_s[l][:, cols].re("(j p) t -> p j t", p=128))
                self.DMA("sp", kv[i], self.kv_s[l][cols, :])
                self.DMA("sp", oo[i], self.oz_s[l][cols, 0:1024])

            def load_g(tb):
                self.DMA("sp", gt[tb % 3], self.g_s[l][tb * 128:(tb + 1) * 128, 0:8])

            def gates(tb):
                i = tb % 2
                G = g4[i]
                smp = (tb == NTB - 1)
                sfx = "_s" if smp else "_p"
                TRI, MASKN, MASKTP, ELAST = K["tri" + sfx], K["maskn" + sfx], K["masktp" + sfx], K["elast" + sfx]
                g_ = gt[tb % 3]
                if smp:
                    self.DMA("sp", mprev, self.smrep[l])
                self.TT("dve", G["ig"], g_[:, 0:4], bi, ALU.add)
                self.TT("dve", G["fz"], g_[:, 4:8], bf, ALU.add)
                yield
                self.ACT(G["e1"], G["fz"], AF.Exp, scale=-1.0)
                yield
                self.ACT(G["sp"], G["e1"], AF.Ln, bias=1.0)
                yield
                self.MM(PS[0][:, 0:4], TRI, G["sp"])
                yield
                self.TS("dve", G["b"], PS[0][:, 0:4], -1.0, ALU.mult)
                yield
                self.TT("dve", G["a"], G["ig"], G["b"], ALU.subtract)
                yield
                self.TT("dve", R, identf.bc(1, [128, 4, 128]), G["a"].bc(2, [128, 4, 128]), ALU.mult)
                yield
                self.MM(PS[1], ones, R.re("p h s -> p (h s)"))
                yield
                self.TT("dve", tmpA, PS[1].re("p (h s) -> p h s", h=4), MASKN.bc(1, [128, 4, 128]), ALU.add)
                yield
                self.REDMAX(G["cm"], tmpA)
                yield
                self.TT("dve", G["g"], G["cm"], mprev, ALU.max)
                yield
                self.TT("dve", R, identf.bc(1, [128, 4, 128]), G["g"].bc(2, [128, 4, 128]), ALU.mult)
                self.TT("dve", G["t1"], mprev, G["g"], ALU.subtract)
                self.TT("dve", G["t2"], G["b"], G["g"], ALU.add)
                self.COPY("dve", gb8[i][:, 0:4], G["g"]); self.COPY("dve", gb8[i][:, 4:8], G["b"])
                yield
                self.MM(PS[1], ones, R.re("p h s -> p (h s)"))
                self.MM(PS[0][:, 8:16], ELAST, gb8[i])
                self.ACT(G["wint"], G["t1"], AF.Exp)
                self.ACT(G["enm"], G["t2"], AF.Exp, scale=-1.0)
                yield
                self.TT("dve", tmpA, PS[1].re("p (h s) -> p h s", h=4), MASKTP.bc(1, [128, 4, 128]), ALU.add)
                self.COPY("dve", glb[i], PS[0][:, 8:16])
                yield
                for h in range(4):
                    self.ACT(DT[i][:, h, :], tmpA[:, h, :], AF.Exp, bias=G["a"][:, h:h + 1], scale=-1.0)
                self.TT("dve", G["mend"], glb[i][:, 0:4], glb[i][:, 4:8], ALU.add)
                self.TT("dve", G["t3"], G["a"], glb[i][:, 0:4], ALU.subtract)
                yield
                self.ACT(G["wend"], G["t3"], AF.Exp)
                yield
                self.TT("dve", G["t3"], mprev, glb[i][:, 0:4], ALU.subtract)
                yield
                self.ACT(G["dec"], G["t3"], AF.Exp)
                yield
                if smp:
                    self.TT("dve", R3, seql.bc(1, [128, 4, NS]), G["dec"].bc(2, [128, 4, NS]), ALU.mult)
                    self.MM(PS[0][:, 64:128], ones, R3.re("p h i -> p (h i)"))
                    self.COPY("dve", decrep, PS[0][:, 64:128].re("p (h i) -> p h i", h=4))
                    self.DMA("sp", self.o_sm[l], G["mend"][7:128:8, :])
                else:
                    self.COPY("dve", mprev, G["mend"])
                    if tb == NTB - 2:
                        self.DMA("sp", self.o_pm[l:l + 1, :], G["mend"][0:1, :])
                yield

            def head(tb, h):
                i = tb % 2
                G = g4[i]
                ti = h % 2
                q_T, kv_ = qT[i], kv[i]
                pi, pn, pu = PS[3 + ti], PS[5 + ti], PS[7]
                self.MM(pi[:, 0:257], PT[:, h, :], vaug[:, h, :])
                for cc in range(2):
                    self.MM(pn[:, 0:257], q_T[:, h * 2 + cc, :], Cbf[:, h, cc, :], start=(cc == 0), stop=(cc == 1), inc=(cc == 1))
                yield
                self.ACT(tmpI[ti], pn[:, 0:257], AF.Copy, scale=G["wint"][:, h:h + 1])
                yield
                self.TT("dve", comb[ti], tmpI[ti], pi[:, 0:257], ALU.add)
                yield
                self.ACT(dd[ti], comb[ti][:, 256:257], AF.Abs)
                for cc in range(2):
                    pu2 = pu if cc == 0 else PS[2]
                    self.MM(pu2[:, 0:257], kv_[:, h * 256 + cc * 128: h * 256 + (cc + 1) * 128], wv[:, h, :])
                    self.STT("dve", C32[:, h, cc, :], C32[:, h, cc, :], G["dec"][:, h:h + 1], pu2[:, 0:257], ALU.mult, ALU.add)
                    self.COPY("act", Cbf[:, h, cc, :], C32[:, h, cc, :])
                yield
                self.TS("dve", dd[ti], dd[ti], G["enm"][:, h:h + 1], ALU.max)
                yield
                self.RECIP(rr_[ti], dd[ti])
                yield
                self.TS("dve", hh[ti], comb[ti][:, 0:256], rr_[ti], ALU.mult)
                yield
                self.c.op("dve", lambda g: g.bn_stats(out=bn[ti].ap, in_=hh[ti].ap), [hh[ti].buf], [bn[ti].buf])
                yield
                self.c.op("dve", lambda g: g.bn_aggr(out=mv[ti].ap, in_=bn[ti].ap), [bn[ti].buf], [mv[ti].buf])
                yield
                self.ACT(rstd[ti], mv[ti][:, 1:2], AF.Ln, bias=GN_EPS)
                yield
                self.ACT(rstd[ti], rstd[ti], AF.Exp, scale=-0.5)
                yield
                self.TS("dve", hn[:, h * 256:(h + 1) * 256], hh[ti], mv[ti][:, 0:1], ALU.subtract, rstd[ti], ALU.mult)
                yield

            def head_smp(tb, h):
                i = tb % 2
                G = g4[i]
                ti = 0
                q_T, kv_ = qT[i], kv[i]
                pi, pn = PS[3], PS[5]
                self.MM(pi[:, 0:257], PT[:, h, :], vaug[:, h, :])
                Cs32 = Cs32s[h % 2]
                ns_t = ns_ts[h % 2]
                if h == 0:
                    load_cs(0)
                if h + 1 < 4:
                    load_cs(h + 1)
                self.COPY("dve", Cs32[:, :, :, 256], ns_t)
                self.COPY("act", Csbf, Cs32)
                for cc in range(2):
                    self.TT("dve", qTm[:, cc], q_T[:, h * 2 + cc, :].bc(1, [128, NS, 128]), smt, ALU.mult)
                for si in range(NS):
                    for cc in range(2):
                        self.MM(pn[:, 0:257], qTm[:, cc, si, :], Csbf[:, si, cc, :],
                                start=(si == 0 and cc == 0), stop=(si == NS - 1 and cc == 1), inc=(si == NS - 1 and cc == 1))
                yield
                self.ACT(tmpI[ti], pn[:, 0:257], AF.Copy, scale=G["wint"][:, h:h + 1])
                self.TT("dve", comb[ti], tmpI[ti], pi[:, 0:257], ALU.add)
                self.ACT(dd[ti], comb[ti][:, 256:257], AF.Abs)
                self.TS("dve", dd[ti], dd[ti], G["enm"][:, h:h + 1], ALU.max)
                self.RECIP(rr_[ti], dd[ti])
                self.TS("dve", hh[ti], comb[ti][:, 0:256], rr_[ti], ALU.mult)
                self.c.op("dve", lambda g: g.bn_stats(out=bn[ti].ap, in_=hh[ti].ap), [hh[ti].buf], [bn[ti].buf])
                self.c.op("dve", lambda g: g.bn_aggr(out=mv[ti].ap, in_=bn[ti].ap), [bn[ti].buf], [mv[ti].buf])
                self.ACT(rstd[ti], mv[ti][:, 1:2], AF.Ln, bias=GN_EPS)
                self.ACT(rstd[ti], rstd[ti], AF.Exp, scale=-0.5)
                self.TS("dve", hn[:, h * 256:(h + 1) * 256], hh[ti], mv[ti][:, 0:1], ALU.subtract, rstd[ti], ALU.mult)
                yield
                self.TT("dve", wvm, wv[:, h, :].bc(1, [128, NS, 257]), seqm.bc(2, [128, NS, 257]), ALU.mult)
                n = 0
                for si in range(NS):
                    for cc in range(2):
                        pu2 = (PS[7], PS[2], PS[4], PS[6])[n % 4]
                        n += 1
                        self.MM(pu2[:, 0:257], kv_[:, h * 256 + cc * 128: h * 256 + (cc + 1) * 128], wvm[:, si, :])
                        self.STT("dve", Cs32[:, si, cc, :], Cs32[:, si, cc, :], decrep[:, h, si:si + 1], pu2[:, 0:257], ALU.mult, ALU.add)
                for cc in range(2):
                    self.DMA("sp", self.o_sC[l, :, h, cc * 128:(cc + 1) * 128, :].re("i p e -> p i e"), Cs32[:, :, cc, 0:256])
                self.COPY("dve", ns_o, Cs32[:, :, :, 256])
                self.DMA("sp", self.o_snT[l, h], ns_o)
                yield

            def heavy(tb):
                i = tb % 2
                G = g4[i]
                smp = (tb == NTB - 1)
                q_T, k_T, kv_, o_ = qT[i], kT[i], kv[i], oo[i]
                self.COPY("act", vaug[:, :, 0:256], kv_[:, 1024:2048].re("p (h e) -> p h e", h=4))
                psb = PS[2].bitcast(BF16)
                for j in range(8):
                    self.TR(psb[:, j * 128:(j + 1) * 128], kv_[:, j * 128:(j + 1) * 128], identb, inc=(j == 7))
                self.COPY("act", k_T, psb.re("p (j t) -> p j t", j=8))
                yield
                for h in range(4):
                    for cc in range(2):
                        self.MM(PS[2][:, h * 128:(h + 1) * 128], k_T[:, h * 2 + cc, :], q_T[:, h * 2 + cc, :],
                                start=(cc == 0), stop=(cc == 1), inc=(cc == 1))
                self.ACT(sig, o_, AF.Exp, scale=-1.0)
                self.ACT(sig, sig, AF.Ln, bias=1.0)
                self.ACT(sig, sig, AF.Exp, scale=-1.0)
                yield
                self.TT("dve", wv, vaug, G["wend"].bc(2, [128, 4, 257]), ALU.mult)
                self.TT("dve", PT, PS[2].re("p (h t) -> p h t", h=4), DT[i], ALU.mult)
                yield
                if not smp:
                    for hp in range(2):
                        yield from rr_gen([head(tb, 2 * hp), head(tb, 2 * hp + 1)])
                else:
                    for h in range(4):
                        yield from head_smp(tb, h)
                self.TT("dve", hn, hn, mnw, ALU.mult)
                yield
                self.TT("dve", mixm, hn, sig, ALU.mult)
                self.DMA("sp", self.mix_s[l][tb * 128:(tb + 1) * 128, 0:1024], mixm)
                if tb == NTB - 2:
                    for cc in range(2):
                        self.DMA("sp", self.o_pC[l, :, cc * 128:(cc + 1) * 128, :].re("h p e -> p h e"), C32[:, :, cc, 0:256])
                    self.COPY("dve", pn_t, C32[:, :, :, 256])
                    self.DMA("sp", self.o_pnT[l], pn_t)
                yield

            load_g(0); load_g(1)
            load(0)
            run_gens([gates(0)])
            for tb in range(NTB):
                if tb + 2 < NTB:
                    load_g(tb + 2)
                if tb + 1 < NTB:
                    load(tb + 1)
                    run_gens([heavy(tb), gates(tb + 1)])
                else:
                    run_gens([heavy(tb)])
            c.barrier()
        self.es = None

    def phaseC(self, l):
        c = self.c
        PS = self.PS
        with ExitStack() as es:
            self.es = es
            K = self.load_consts()
            identf, ones = K["identf"], K["ones"]
            dtb = self.sb("dtb", [128, 16]); aneg = self.sb("aneg", [128, 16]); dsk = self.sb("dsk", [128, 16])
            self.DMA("sp", dtb, V(self.dtb.ap[l:l + 1, :].partition_broadcast(128), self.dtb.buf))
            self.DMA("sp", aneg, V(self.alog.ap[l:l + 1, :].partition_broadcast(128), self.alog.buf))
            self.DMA("sp", dsk, V(self.dsk.ap[l:l + 1, :].partition_broadcast(128), self.dsk.buf))
            self.ACT(aneg, aneg, AF.Exp)
            self.TS("dve", aneg, aneg, -1.0, ALU.mult)
            snw = self.sb("snw", [128, 1024])
            self.DMA("sp", snw, V(self.snw.ap[l:l + 1, :].partition_broadcast(128), self.snw.buf))
            cw = self.sb("cw", [128, 12, 4]); cb = self.sb("cb", [128, 12])
            self.DMA("sp", cw, self.scw[l]); self.DMA("sp", cb, self.scb[l])
            smtb = self.sb("smtb", [128, NS, 128], BF16)
            self.DMA("pool", smtb, V(self.smt.ap.unsqueeze(0).to_broadcast([128, NS, 128]), self.smt.buf))
            seqm = self.sb("seqm", [128, NS]); seql = self.sb("seql", [128, NS])
            self.DMA("sp", seqm, self.seqm[0]); self.DMA("sp", seql, self.seqm[1])
            nident = self.sb("nident", [128, 128])
            self.TS("dve", nident, identf, -1.0, ALU.mult)
            ca = [self.sb("ca%d" % i, [128, 12, 128]) for i in range(4)]
            ST32 = self.sb("ST32", [128, 1024]); STb = self.sb("STb", [128, 1024], BF16)
            self.MEMSET("dve", ST32, 0.0); self.MEMSET("pool", STb, 0.0)
            XP = [self.sb("XP%d" % i, [128, 12, 131]) for i in range(3)]
            self.MEMSET("dve", XP[0], 0.0)
            zz = [self.sb("zz%d" % i, [128, 1024]) for i in range(2)]
            gt = [self.sb("gtc%d" % i, [128, 16]) for i in range(3)]
            gn = ["fz", "e1", "dt", "a", "b", "eb", "bl", "t1", "wend", "ebl", "lnd", "nb"]
            g16 = [{n: self.sb("g16_%s%d" % (n, i), [128, 16]) for n in gn} for i in range(2)]
            xbca = self.sb("xbca", [128, 12, 128]); u_t = self.sb("u_t", [128, 12, 128])
            xs32 = [self.sb("xs32_%d" % i, [128, 1024]) for i in range(3)]
            xsbs = [self.sb("xsb%d" % i, [128, 1024], BF16) for i in range(2)]
            Btok = [self.sb("Btok%d" % i, [128, 2, 128], BF16) for i in range(3)]
            CTb = [self.sb("CTb%d" % i, [128, 2, 128], BF16) for i in range(3)]
            BTbs = [self.sb("BTb%d" % i, [128, 2, 128], BF16) for i in range(2)]
            cbm = self.sb("cbm", [128, 2, 128])
            R = [self.sb("Rc%d" % i, [128, 4, 128]) for i in range(2)]
            decT = self.sb("decT", [128, 16, 128])
            mwT = self.sb("mwT", [128, 16, 128], BF16)
            yin = [self.sb("yin%d" % i, [128, 1024]) for i in range(2)]
            wxs = [self.sb("wxs%d" % i, [128, 1024], BF16) for i in range(2)]
            y1 = self.sb("y1", [128, 1024]); y2 = self.sb("y2", [128, 1024])
            sq = self.sb("sq", [128, 512]); ss = self.sb("ss", [128, 2]); rinv = self.sb("rinv", [128, 2])
            mixs = self.sb("mixs", [128, 1024], BF16)
            sc_t = self.sb("sc_t", [128, 12, 3])
            XS = self.sb("XS", [128, 12, NS, 11])
            sc_in = self.sb("sc_in", [128, 12, NS, 3])
            Ss32 = self.sb("Ss32", [128, 8, 512]); Ssb = self.sb("Ssb", [128, 8, 512], BF16)
            CTm = self.sb("CTm", [128, NS, 128], BF16); wxsm = self.sb("wxsm", [128, 4, 512], BF16)
            R3 = self.sb("R3c", [128, NS, 16]); decS = self.sb("decS", [128, NS, 16])

            def load(tb):
                i = tb % 2
                cols = slice(tb * 128, (tb + 1) * 128)
                if tb < NTB - 1:
                    self.DMA("sp", XP[tb % 3][:, :, 3:131], self.xbcT_s[l][:, cols].re("(j p) t -> p j t", p=128))
                else:
                    for j in range(12):
                        self.DMA("sp", XS[:, j, :, 3:11], self.xbcT_s[l][j * 128:(j + 1) * 128, cols].re("p (i t) -> p i t", i=NS))
                    self.DMA("sp", sc_in, self.ssc[l])
                self.DMA("sp", gt[tb % 3], self.g_s[l][cols, 8:24])

            def load_z(tb):
                self.DMA("sp", zz[tb % 2], self.oz_s[l][tb * 128:(tb + 1) * 128, 1024:2048])

            done1b = {}

            def stage1b(tb):
                i = tb % 2
                smp = (tb == NTB - 1)
                sfx = "_s" if smp else "_p"
                TRI, MASKTP, ELAST = K["tri" + sfx], K["masktp" + sfx], K["elast" + sfx]
                G = g16[i]
                self.TT("dve", G["fz"], gt[tb % 3], dtb, ALU.add)
                yield
                self.ACT(G["e1"], G["fz"], AF.Exp)
                self.ACT(G["dt"], G["e1"], AF.Ln, bias=1.0)
                self.ACT(G["lnd"], G["dt"], AF.Ln)
                yield
                self.TT("dve", G["a"], G["dt"], aneg, ALU.mult)
                yield
                self.MM(PS[3][:, 0:16], TRI, G["a"])
                yield
                self.COPY("dve", G["b"], PS[3][:, 0:16])
                yield
                self.MM(PS[3][:, 16:32], ELAST, G["b"])
                self.ACT(G["eb"], G["b"], AF.Exp)
                self.TT("dve", G["nb"], G["lnd"], G["b"], ALU.subtract)
                yield
                self.COPY("dve", G["bl"], PS[3][:, 16:32])
                yield
                self.ACT(G["ebl"], G["bl"], AF.Exp)
                self.TT("dve", G["t1"], G["bl"], G["b"], ALU.subtract)
                yield
                self.ACT(G["wend"], G["t1"], AF.Exp)
                yield
                self.TT("dve", G["wend"], G["wend"], G["dt"], ALU.mult)
                for qd in range(4):
                    Rq = R[qd % 2]
                    ps = PS[4]
                    self.TT("pool", Rq, identf.bc(1, [128, 4, 128]), G["b"][:, qd * 4:(qd + 1) * 4].bc(2, [128, 4, 128]), ALU.mult)
                    yield
                    self.MM(ps, ones, Rq.re("p h t -> p (h t)"), start=True, stop=False, inc=False)
                    self.MM(ps.re("p (h t) -> p h t", h=4), nident, MASKTP.bc(1, [128, 4, 128]), start=False, stop=True)
                    yield
                    for hh_ in range(4):
                        h = qd * 4 + hh_
                        self.ACT(decT[:, h, :], ps[:, hh_ * 128:(hh_ + 1) * 128], AF.Exp, bias=G["nb"][:, h:h + 1])
                    yield
                done1b[tb] = True

            def stage0(tb):
                i = tb % 2
                k3 = tb % 3
                smp = (tb == NTB - 1)
                xp = XP[k3]
                xsb = xsbs[i]
                BTb = BTbs[i]
                if smp:
                    self.COPY("dve", XS[:, :, :, 0:3], sc_in)
                    yield
                def tapv(t):
                    if not smp:
                        return xp[:, :, t:t + 128], V(cw.ap[:, :, t:t + 1].to_broadcast([128, 12, 128]), cw.buf), (lambda a: a)
                    return (XS[:, :, :, t:t + 8], V(cw.ap[:, :, t:t + 1].unsqueeze(3).to_broadcast([128, 12, NS, 8]), cw.buf),
                            (lambda a: a.re("p j (i t) -> p j i t", i=NS)))
                for t in range(4):
                    xin, wbc, view = tapv(t)
                    self.TT("dve" if t % 2 == 0 else "pool", view(ca[t]), xin, wbc, ALU.mult)
                yield
                self.TT("dve", ca[0], ca[0], ca[2], ALU.add)
                self.TT("pool", ca[1], ca[1], ca[3], ALU.add)
                yield
                self.TT("dve", ca[0], ca[0], ca[1], ALU.add)
                yield
                self.TT("dve", u_t, ca[0], cb.bc(2, [128, 12, 128]), ALU.add)
                yield
                if not smp:
                    if tb + 1 < NTB - 1:
                        self.COPY("pool", XP[(tb + 1) % 3][:, :, 0:3], xp[:, :, 128:131])
                    if tb == NTB - 2:
                        self.COPY("dve", sc_t, xp[:, :, 128:131])
                        self.DMA("sp", self.o_psc[l], sc_t)
                else:
                    self.COPY("dve", sc_in, XS[:, :, :, 8:11])
                    self.DMA("sp", self.o_ssc[l], sc_in)
                self.ACT(xbca, u_t, AF.Exp, scale=-1.0)
                yield
                self.ACT(xbca, xbca, AF.Ln, bias=1.0)
                yield
                self.ACT(xbca, xbca, AF.Exp, scale=-1.0)
                yield
                self.TT("dve", xbca, xbca, u_t, ALU.mult)
                yield
                for half in range(2):
                    for j in range(4):
                        self.TR(PS[0][:, j * 128:(j + 1) * 128], xbca[:, half * 4 + j, :], identf, inc=(j == 3))
                    self.COPY("act", xs32[k3][:, half * 512:(half + 1) * 512], PS[0])
                    yield
                for g in range(2):
                    self.TR(PS[2][:, g * 128:(g + 1) * 128], xbca[:, 8 + g, :], identf, inc=(g == 1))
                self.COPY("pool", BTb, xbca[:, 8:10, :])
                self.COPY("pool", CTb[k3], xbca[:, 10:12, :])
                self.COPY("act", xsb, xs32[k3])
                yield
                self.COPY("dve", Btok[k3], PS[2][:, 0:256].re("p (g n) -> p g n", g=2))
                yield

            def stage1j(tb):
                i = tb % 2
                k3 = tb % 3
                smp = (tb == NTB - 1)
                sfx = "_s" if smp else "_p"
                MASK01T = K["mask01t" + sfx]
                G = g16[i]
                xsb = xsbs[i]
                BTb = BTbs[i]
                for g in range(2):
                    self.MM(PS[2][:, 256 + g * 128:256 + (g + 1) * 128], BTb[:, g, :], CTb[k3][:, g, :])
                yield
                self.TT("dve", cbm, PS[2][:, 256:512].re("p (g t) -> p g t", g=2), MASK01T.bc(1, [128, 2, 128]), ALU.mult)
                yield
                while not done1b.get(tb):
                    yield
                for g in range(2):
                    self.TT("pool", mwT[:, g * 8:(g + 1) * 8, :], decT[:, g * 8:(g + 1) * 8, :], cbm[:, g, :].bc(1, [128, 8, 128]), ALU.mult)
                    yield
                self.TT("pool", wxs[i].re("p (h q) -> p h q", h=16), xs32[k3].re("p (h q) -> p h q", h=16),
                        G["wend"].bc(2, [128, 16, 64]), ALU.mult)
                for h in range(16):
                    ps = PS[6 + h // 8]
                    hh_ = h % 8
                    self.MM(ps[:, hh_ * 64:(hh_ + 1) * 64], mwT[:, h, :], xsb[:, h * 64:(h + 1) * 64], inc=(hh_ == 7))
                for g in range(2):
                    self.COPY("act", yin[i][:, g * 512:(g + 1) * 512], PS[6 + g])
                yield

            def stage2(tb):
                i = tb % 2
                k3 = tb % 3
                smp = (tb == NTB - 1)
                G = g16[i]
                z_ = zz[i]
                cols = slice(tb * 128, (tb + 1) * 128)
                pbank = (PS[1], PS[5])
                self.TT("pool", y2.re("p (h q) -> p h q", h=16), xs32[k3].re("p (h q) -> p h q", h=16), dsk.bc(2, [128, 16, 64]), ALU.mult)
                if smp:
                    self.TT("dve", R3, seql.bc(2, [128, NS, 16]), G["ebl"].bc(1, [128, NS, 16]), ALU.mult)
                    self.MM(PS[3][:, 256:512], ones, R3.re("p i h -> p (i h)"))
                    self.COPY("dve", decS, PS[3][:, 256:512].re("p (i h) -> p i h", i=NS))
                for g in range(2):
                    ps = pbank[g]
                    gs = slice(g * 512, (g + 1) * 512)
                    if not smp:
                        self.MM(ps, CTb[k3][:, g, :], STb[:, gs])
                    else:
                        self.TT("dve", CTm, CTb[k3][:, g, :].bc(1, [128, NS, 128]), smtb, ALU.mult)
                        for hf in range(2):
                            self.DMA("sp", Ss32, self.sS[l, g][:, hf * 8:(hf + 1) * 8, :])
                            self.COPY("act", Ssb, Ss32)
                            for s8 in range(8):
                                si = hf * 8 + s8
                                self.MM(ps, CTm[:, si, :], Ssb[:, s8, :], start=(si == 0), stop=(si == NS - 1), inc=(s8 == 7))
                            for qq in range(2):
                                self.TT("dve", wxsm, wxs[i][:, gs].bc(1, [128, 4, 512]),
                                        seqm[:, hf * 8 + qq * 4:hf * 8 + (qq + 1) * 4].bc(2, [128, 4, 512]), ALU.mult)
                                for s4 in range(4):
                                    s8 = qq * 4 + s4
                                    si = hf * 8 + s8
                                    pu = PS[0] if si % 2 == 0 else PS[6]
                                    self.MM(pu, Btok[k3][:, g, :], wxsm[:, s4, :])
                                    sv = Ss32[:, s8, :].re("p (h q) -> p h q", h=8)
                                    self.TT("dve", sv, sv, decS[:, si, g * 8:(g + 1) * 8].bc(2, [128, 8, 64]), ALU.mult)
                                    self.TT("dve", Ss32[:, s8, :], Ss32[:, s8, :], pu, ALU.add)
                            self.DMA("sp", self.o_sS[l, g][:, hf * 8:(hf + 1) * 8, :], Ss32)
                    yield
                    self.TT("dve", y1[:, gs].re("p (h q) -> p h q", h=8), ps.re("p (h q) -> p h q", h=8),
                            G["eb"][:, g * 8:(g + 1) * 8].bc(2, [128, 8, 64]), ALU.mult)
                    yield
                    self.TT("dve", y1[:, gs], y1[:, gs], yin[i][:, gs], ALU.add)
                    yield
                self.TT("dve", y1, y1, y2, ALU.add)
                yield
                self.ACT(y2, z_, AF.Exp, scale=-1.0)
                yield
                self.ACT(y2, y2, AF.Ln, bias=1.0)
                self.TT("dve", y1, y1, z_, ALU.mult)
                yield
                self.ACT(y2, y2, AF.Exp, scale=-1.0)
                yield
                self.TT("dve", y1, y1, y2, ALU.mult)
                yield
                for g in range(2):
                    self.ACT(sq, y1[:, g * 512:(g + 1) * 512], AF.Square, accum=ss[:, g:g + 1])
                yield
                self.ACT(rinv, ss, AF.Ln, scale=1.0 / 512.0, bias=GN_EPS)
                yield
                self.ACT(rinv, rinv, AF.Exp, scale=-0.5)
                yield
                for g in range(2):
                    gs = slice(g * 512, (g + 1) * 512)
                    self.STT("dve", mixs[:, gs], y1[:, gs], rinv[:, g:g + 1], snw[:, gs], ALU.mult, ALU.mult)
                self.DMA("sp", self.mix_s[l][cols, 1024:2048], mixs)
                yield
                if not smp:
                    for g in range(2):
                        gs = slice(g * 512, (g + 1) * 512)
                        ps = pbank[g]
                        self.MM(ps, Btok[k3][:, g, :], wxs[i][:, gs])
                        sv = ST32[:, gs].re("p (h q) -> p h q", h=8)
                        self.TT("dve", sv, sv, G["ebl"][:, g * 8:(g + 1) * 8].bc(2, [128, 8, 64]), ALU.mult)
                        yield
                        self.TT("dve", ST32[:, gs], ST32[:, gs], ps, ALU.add)
                        self.COPY("act", STb[:, gs], ST32[:, gs])
                        yield
                    if tb == NTB - 2:
                        self.DMA("sp", self.o_pS[l], ST32)
                yield

            load(0); load_z(0)
            load(1); load_z(1)
            load(2)
            run_gens([stage1b(0), stage0(0)])
            run_gens([stage1j(0), stage0(1)])
            for tb in range(NTB):
                if tb + 3 < NTB:
                    load(tb + 3)
                gens = [stage2(tb)]
                if tb + 1 < NTB:
                    gens += [stage1b(tb + 1), stage1j(tb + 1)]
                if tb + 2 < NTB:
                    gens += [stage0(tb + 2)]
                run_gens(gens)
                if tb + 2 < NTB:
                    load_z(tb + 2)
            c.barrier()
        self.es = None

    def layernorm(self, pre, out, gam, bet, tmp, bn, mv, rstd, nmr):
        for q in range(4):
            self.c.op("dve", lambda g: g.bn_stats(out=bn.ap[:, q * 6:(q + 1) * 6], in_=pre.ap[:, q * 512:(q + 1) * 512]), [pre.buf], [bn.buf])
        self.c.op("dve", lambda g: g.bn_aggr(out=mv.ap, in_=bn.ap), [bn.buf], [mv.buf])
        self.ACT(rstd, mv[:, 1:2], AF.Sqrt, bias=LN_EPS)
        self.RECIP(rstd, rstd)
        self.STT("dve", nmr, mv[:, 0:1], -1.0, rstd, ALU.mult, ALU.mult)
        self.ACT(tmp, pre, AF.Identity, bias=nmr, scale=rstd)
        self.TT("pool", tmp, tmp, gam, ALU.mult)
        self.TT("dve", out, tmp, bet, ALU.add)

    def phaseD(self, l, src):
        c = self.c
        PS = self.PS
        with ExitStack() as es:
            self.es = es
            identb = self.sb("identb", [128, 128], BF16)
            self.DMA("pool", identb, self.cst[0])
            wo = [self.sb("wo%d" % q, [128, 16, 512], BF16) for q in range(4)]
            wsrc = self.w_out[l].re("(k p) n -> p k n", p=128)
            for q in range(4):
                self.DMA("pool", wo[q], wsrc[:, :, q * 512:(q + 1) * 512])
            gam = self.sb("gam", [128, D]); bet = self.sb("bet", [128, D])
            self.DMA("sp", gam, V(self.ln1g.ap[l:l + 1, :].partition_broadcast(128), self.ln1g.buf))
            self.DMA("sp", bet, V(self.ln1b.ap[l:l + 1, :].partition_broadcast(128), self.ln1b.buf))
            mx = [self.sb("mx%d" % i, [128, 2048], BF16) for i in range(3)]
            xr = [self.sb("xr%d" % i, [128, D]) for i in range(3)]
            mT = [self.sb("mT%d" % i, [128, 16, 128], BF16) for i in range(2)]
            pre = [self.sb("pre%d" % i, [128, D]) for i in range(2)]
            x1 = [self.sb("x1_%d" % i, [128, D]) for i in range(2)]
            tmp = [self.sb("tmp%d" % i, [128, D]) for i in range(2)]
            bn = [self.sb("bn%d" % i, [128, 24]) for i in range(2)]; mv = [self.sb("mv%d" % i, [128, 2]) for i in range(2)]
            rstd = [self.sb("rstd%d" % i, [128, 1]) for i in range(2)]; nmr = [self.sb("nmr%d" % i, [128, 1]) for i in range(2)]

            def load(tb):
                i = tb % 3
                rows = slice(tb * 128, (tb + 1) * 128)
                self.DMA("sp", mx[i], self.mix_s[l][rows, :])
                self.DMA("sp", xr[i], src[rows, :])

            def tre(tb):
                i = tb % 2
                for half in range(2):
                    psb = PS[4 + half].bitcast(BF16)
                    for k in range(8):
                        kk = half * 8 + k
                        self.TR(psb[:, k * 128:(k + 1) * 128], mx[tb % 3][:, kk * 128:(kk + 1) * 128], identb, inc=(k == 7))
                    self.evac(mT[i][:, half * 8:(half + 1) * 8, :], psb.re("p (k t) -> p k t", k=8))

            load(0)
            load(1)
            tre(0)
            for tb in range(NTB):
                if tb + 2 < NTB:
                    load(tb + 2)
                i = tb % 2
                rows = slice(tb * 128, (tb + 1) * 128)
                for q in range(4):
                    ps = PS[q]
                    for k in range(16):
                        self.MM(ps, mT[i][:, k, :], wo[q][:, k, :], start=(k == 0), stop=(k == 15), inc=(k == 15))
                    qs = slice(q * 512, (q + 1) * 512)
                    self.STT("dve", pre[i][:, qs], xr[tb % 3][:, qs], ALPHA, ps, ALU.mult, ALU.add)
                if tb + 1 < NTB:
                    tre(tb + 1)
                self.layernorm(pre[i], x1[i], gam, bet, tmp[i], bn[i], mv[i], rstd[i], nmr[i])
                self.DMA("sp", self.x1_s[l][rows, :], x1[i])
            c.barrier()
        self.es = None

    def phaseE(self, l):
        c = self.c
        PS = self.PS
        with ExitStack() as es:
            self.es = es
            identb = self.sb("identb", [128, 128], BF16)
            self.DMA("pool", identb, self.cst[0])
            xT = self.sb("xT", [128, 16, NT], BF16)
            wE = [self.sb("wE%d" % i, [128, 16, 2, 128], BF16) for i in range(3)]
            wsrc = self.w_up[l].re("(k p) n -> p k n", p=128)
            fw = self.sb("fw", [128, 86, 3]); fb = self.sb("fb", [128, 86])
            self.DMA("sp", fw, self.fcw[l]); self.DMA("sp", fb, self.fcb[l])
            UP = [self.sb("UP%d" % i, [128, 2 + TP]) for i in range(2)]
            US = [self.sb("US%d" % i, [128, NS, 10]) for i in range(2)]
            for i in range(2):
                self.MEMSET("dve", UP[i][:, 0:2], 0.0)
            fcin = self.sb("fcin", [128, 86, NS, 2])
            self.DMA("sp", fcin, self.sfc[l])
            fco_p = self.sb("fco_p", [128, 86, 2]); fco_s = self.sb("fco_s", [128, 86, NS, 2])
            cv = [self.sb("cv%d" % i, [128, NT]) for i in range(2)]
            sg = self.sb("sg", [128, NT])
            hT = [self.sb("hT%d" % i, [128, NT], BF16) for i in range(2)]

            def loadw(ci):
                w = wE[ci % 3]
                self.DMA("pool", w[:, :, 0, :], wsrc[:, :, ci * 128:(ci + 1) * 128])
                self.DMA("pool", w[:, :, 1, :], wsrc[:, :, DFF + ci * 128:DFF + (ci + 1) * 128])

            loadw(0); loadw(1)
            self.build_xT(self.x1_s[l], xT, identb)
            nps = 0
            for ci in range(NCH):
                if ci + 2 < NCH:
                    loadw(ci + 2)
                w = wE[ci % 3]
                for gv in range(2):
                    ch = gv * NCH + ci
                    up, us = UP[gv], US[gv]
                    self.COPY("pool", us[:, :, 0:2], fcin[:, ch, :, :])
                    for (t0, tn) in TGS:
                        ps = PS[nps % 8]; nps += 1
                        for k in range(16):
                            self.MM(ps[:, 0:tn], w[:, k, gv, :], xT[:, k, t0:t0 + tn], start=(k == 0), stop=(k == 15), inc=(k == 15))
                        if t0 < TP:
                            self.evac(up[:, 2 + t0:2 + t0 + tn], ps[:, 0:tn])
                        else:
                            self.evac(us[:, :, 2:10], ps[:, 0:128].re("p (i t) -> p i t", i=NS))
                    self.COPY("pool", fco_p[:, ch, :], up[:, TP:TP + 2])
                    self.COPY("pool", fco_s[:, ch, :, :], us[:, :, 8:10])
                    cvp = cv[gv][:, 0:TP]
                    cvs = cv[gv][:, TP:NT].re("p (i t) -> p i t", i=NS)
                    e = "dve"
                    self.TS(e, cvp, up[:, 0:TP], fw[:, ch, 0:1], ALU.mult, fb[:, ch:ch + 1], ALU.add)
                    self.TS(e, cvs, us[:, :, 0:8], fw[:, ch, 0:1], ALU.mult, fb[:, ch:ch + 1], ALU.add)
                    for t in range(1, 3):
                        self.STT(e, cvp, up[:, t:t + TP], fw[:, ch, t:t + 1], cvp, ALU.mult, ALU.add)
                        self.STT(e, cvs, us[:, :, t:t + 8], fw[:, ch, t:t + 1], cvs, ALU.mult, ALU.add)
                self.ACT(sg, cv[0], AF.Silu)
                h = hT[ci % 2]
                self.TT("dve", h, sg, cv[1], ALU.mult)
                self.DMA("sp", self.hT_s[l][:, :, ci, :].re("b p t -> p b t"), h.re("p (b t) -> p b t", b=NTB))
            self.DMA("sp", self.o_pfc[l], fco_p)
            self.DMA("sp", self.o_sfc[l], fco_s)
            c.barrier()
        self.es = None

    def phaseF(self, l, dst):
        c = self.c
        PS = self.PS
        with ExitStack() as es:
            self.es = es
            KR = [(0, 11), (11, 22), (22, 33), (33, NCH)]
            wd = [[self.sb("wd%d_%d" % (i, r), [128, b - a, 512], BF16) for r, (a, b) in enumerate(KR)] for i in range(2)]
            wsrc = self.w_down[l].re("(k p) n -> p k n", p=128)
            hb = [self.sb("hb%d" % i, [128, NCH, 128], BF16) for i in range(3)]
            xq = [self.sb("xq%d" % i, [128, 512]) for i in range(3)]
            po = [self.sb("po%d" % i, [128, 512]) for i in range(3)]
            gam = self.sb("gam", [128, D]); bet = self.sb("bet", [128, D])
            self.DMA("sp", gam, V(self.ln2g.ap[l:l + 1, :].partition_broadcast(128), self.ln2g.buf))
            self.DMA("sp", bet, V(self.ln2b.ap[l:l + 1, :].partition_broadcast(128), self.ln2b.buf))
            pre = [self.sb("pre%d" % i, [128, D]) for i in range(3)]
            x2 = [self.sb("x2_%d" % i, [128, D]) for i in range(2)]
            tmp = self.sb("tmp", [128, D])
            bn = self.sb("bn", [128, 24]); mv = self.sb("mv", [128, 2]); rstd = self.sb("rstd", [128, 1]); nmr = self.sb("nmr", [128, 1])

            def loadw(q):
                for r, (a, b) in enumerate(KR):
                    self.DMA("pool", wd[q % 2][r], wsrc[:, a:b, q * 512:(q + 1) * 512])

            def load(n):
                q, tb = divmod(n, NTB)
                rows = slice(tb * 128, (tb + 1) * 128)
                self.DMA("sp", hb[n % 3], self.hT_s[l][tb])
                self.DMA("sp", xq[n % 3], self.x1_s[l][rows, q * 512:(q + 1) * 512])
                if q == 3:
                    self.DMA("sp", pre[n % 3][:, 0:1536], self.pre_s[l][rows, 0:1536])

            loadw(0)
            load(0); load(1)
            for q in range(4):
                if q + 1 < 4:
                    loadw(q + 1)
                for tb in range(NTB):
                    n = q * NTB + tb
                    if n + 2 < 4 * NTB:
                        if (n + 2) // NTB == 3 and (n + 2) % NTB == 0:
                            for tok in c.dtok:
                                c._wait("sp", tok)
                        load(n + 2)
                    ps = PS[n % 4]
                    for k in range(NCH):
                        r = min(k // 11, 3)
                        self.MM(ps, hb[n % 3][:, k, :], wd[q % 2][r][:, k - KR[r][0], :], start=(k == 0), stop=(k == NCH - 1), inc=(k == NCH - 1))
                    rows = slice(tb * 128, (tb + 1) * 128)
                    if q < 3:
                        self.STT("dve", po[n % 3], xq[n % 3], ALPHA, ps, ALU.mult, ALU.add)
                        self.DMA("sp", self.pre_s[l][rows, q * 512:(q + 1) * 512], po[n % 3])
                    else:
                        i = tb % 2
                        self.STT("dve", pre[n % 3][:, 1536:2048], xq[n % 3], ALPHA, ps, ALU.mult, ALU.add)
                        self.layernorm(pre[n % 3], x2[i], gam, bet, tmp, bn, mv, rstd, nmr)
                        self.DMA("sp", dst[rows, :], x2[i])
            c.barrier()
        self.es = None

    def build(self):
        self.declare()
        order = "ABCDEF"
        for l in range(2):
            src = self.x_in if l == 0 else self.x2_s
            dst = self.x2_s if l == 0 else self.y
            for ph in order:
                tag = "%d%s" % (l, ph)
                if tag > self.upto:
                    break
                if ph == "A":
                    self.phaseA(l, src)
                elif ph == "B":
                    self.phaseB(l)
                elif ph == "C":
                    self.phaseC(l)
                elif ph == "D":
                    self.phaseD(l, src)
                elif ph == "E":
                    self.phaseE(l)
                elif ph == "F":
                    self.phaseF(l, dst)
        self.c.finish()
        return self.nc


def make_consts():
    t = np.arange(128)
    cst = np.zeros((12, 128, 128), np.float32)
    cst[0] = np.eye(128)
    cst[1] = 1.0
    for kind, seq in ((0, np.zeros(128, np.int64)), (1, t // LS)):
        same = seq[:, None] == seq[None, :]
        le = t[:, None] <= t[None, :]
        allowed_st = same & le
        last = np.array([np.max(np.where(seq == seq[m])[0]) for m in range(128)])
        o = 2 + kind * 5
        cst[o + 0] = allowed_st.astype(np.float32)
        cst[o + 1] = np.where(allowed_st.T, 0.0, -1e30)
        cst[o + 2] = np.where(allowed_st, 0.0, 3e4)
        cst[o + 3] = allowed_st.astype(np.float32)
        cst[o + 4] = (t[:, None] == last[None, :]).astype(np.float32)
    smt = (np.arange(NS)[:, None] == (t // LS)[None, :]).astype(np.float32)
    seqm = np.zeros((2, 128, NS), np.float32)
    seqm[0] = smt.T
    seqm[1] = smt.T * ((t % LS) == LS - 1)[:, None]
    return cst, smt, seqm


_PROG_CACHE = {}


def _get_prog(debug=False, upto="Z"):
    key = (debug, upto)
    if key not in _PROG_CACHE:
        _PROG_CACHE[key] = Prog(debug=debug, upto=upto).build()
    return _PROG_CACHE[key]


def make_in_maps(inp):
    f = lambda a: np.ascontiguousarray(a, dtype=np.float32)
    cst, smt, seqm = make_consts()
    shared = {
        "w_in": f(inp["w_in"]), "w_out": f(inp["w_out"]), "w_up": f(inp["ffn_w_up"]), "w_down": f(inp["ffn_w_down"]),
        "b_i": f(inp["mlstm_b_i"]), "b_f": f(inp["mlstm_b_f"]), "mnw": f(inp["mlstm_norm_w"]), "snw": f(inp["ssm_norm_w"]),
        "dtb": f(inp["ssm_dt_bias"]), "alog": f(inp["ssm_A_log"]), "dsk": f(inp["ssm_D"]),
        "ln1g": f(inp["ln1_g"]), "ln1b": f(inp["ln1_b"]), "ln2g": f(inp["ln2_g"]), "ln2b": f(inp["ln2_b"]),
        "scw": f(inp["ssm_conv_w"].reshape(2, 4, 12, 128).transpose(0, 3, 2, 1)),
        "scb": f(inp["ssm_conv_b"].reshape(2, 12, 128).transpose(0, 2, 1)),
        "scb_row": f(inp["ssm_conv_b"]),
        "fcw": f(inp["ffn_conv_w"].reshape(2, 3, 86, 128).transpose(0, 3, 2, 1)),
        "fcb": f(inp["ffn_conv_b"].reshape(2, 86, 128).transpose(0, 2, 1)),
        "cst": cst, "smt": smt, "seqm": seqm,
    }
    maps = []
    for ci in range(NCORES):
        sl = slice(ci * NS, (ci + 1) * NS)
        xs = inp["x_sample"][sl].reshape(NS * LS, D)
        m = dict(shared)
        m["x_in"] = f(np.concatenate([inp["x_prompt"][ci % 4], xs], axis=0))
        m["sC"] = f(inp["state_mlstm_C"][:, sl])
        m["snT"] = f(inp["state_mlstm_n"][:, sl].reshape(2, NS, 4, 2, 128).transpose(0, 2, 4, 1, 3))
        m["smrep"] = f(np.repeat(inp["state_mlstm_m"][:, sl], LS, axis=1))
        m["sS"] = f(inp["state_ssm"][:, sl].reshape(2, NS, 2, 8, 64, 128).transpose(0, 2, 5, 1, 3, 4).reshape(2, 2, 128, NS, 512))
        m["ssc"] = f(inp["state_ssm_conv"][:, sl].reshape(2, NS, 3, 12, 128).transpose(0, 4, 3, 1, 2))
        m["sfc"] = f(inp["state_ffn_conv"][:, sl].reshape(2, NS, 2, 86, 128).transpose(0, 4, 3, 1, 2))
        maps.append(m)
    return maps


def assemble(res):
    L = 2
    y_prompt = np.stack([res[ci]["y"][:TP] for ci in range(4)], 0)
    y_sample = np.concatenate([res[ci]["y"][TP:].reshape(NS, LS, D) for ci in range(NCORES)], 0)
    p_C = np.stack([res[ci]["o_pC"] for ci in range(4)], 1)
    p_n = np.stack([res[ci]["o_pnT"].transpose(0, 2, 3, 1).reshape(L, 4, 256) for ci in range(4)], 1)
    p_m = np.stack([res[ci]["o_pm"] for ci in range(4)], 1)
    p_S = np.stack([res[ci]["o_pS"].reshape(L, 128, 16, 64).transpose(0, 2, 3, 1) for ci in range(4)], 1)
    p_sc = np.stack([res[ci]["o_psc"].transpose(0, 3, 2, 1).reshape(L, 3, 1536) for ci in range(4)], 1)
    p_fc = np.stack([res[ci]["o_pfc"].transpose(0, 3, 2, 1).reshape(L, 2, 11008) for ci in range(4)], 1)
    s_C = np.concatenate([res[ci]["o_sC"] for ci in range(NCORES)], 1)
    s_n = np.concatenate([res[ci]["o_snT"].transpose(0, 3, 1, 4, 2).reshape(L, NS, 4, 256) for ci in range(NCORES)], 1)
    s_m = np.concatenate([res[ci]["o_sm"] for ci in range(NCORES)], 1)
    s_S = np.concatenate([res[ci]["o_sS"].reshape(L, 2, 128, NS, 8, 64).transpose(0, 3, 1, 4, 5, 2).reshape(L, NS, 16, 64, 128)
                          for ci in range(NCORES)], 1)
    s_sc = np.concatenate([res[ci]["o_ssc"].transpose(0, 3, 4, 2, 1).reshape(L, NS, 3, 1536) for ci in range(NCORES)], 1)
    s_fc = np.concatenate([res[ci]["o_sfc"].transpose(0, 3, 4, 2, 1).reshape(L, NS, 2, 11008) for ci in range(NCORES)], 1)
    outs = (y_prompt, y_sample, p_C, p_n, p_m, p_S, p_sc, p_fc, s_C, s_n, s_m, s_S, s_sc, s_fc)
    return tuple(np.ascontiguousarray(o, dtype=np.float32) for o in outs)


def kernel(**inputs):
    inp = {k: np.asarray(v) for k, v in inputs.items()}
    nc = _get_prog()
    maps = make_in_maps(inp)
    res = run_bass_kernel_spmd(nc, maps, core_ids=list(range(NCORES)))
    return assemble(res.results)
```
